# Optimizing a Trainium2 kernel written in Bass

```python
import math
import jax, jax.numpy as jnp
from jax import lax
import numpy as np

D_MODEL = 1024
BATCH = 8
SEQ = 2048
DEPTH = 1
DEC_BATCH = 32
DEC_SEQ = 1
PAST_LEN = 16384
PAGE_SIZE = 128

PLE_DIM = 256
HEAD_DIM = 64
NSA_HEADS = 8
NSA_KV_GROUPS = 2
NSA_REP = NSA_HEADS // NSA_KV_GROUPS
NSA_WIDTH = NSA_HEADS * HEAD_DIM
NSA_KV_WIDTH = NSA_KV_GROUPS * HEAD_DIM
CMP_BLOCK = 32
CMP_STRIDE = 16
CMP_SEGS = CMP_BLOCK // CMP_STRIDE
CMP_HIDDEN = 2 * HEAD_DIM
SLC_BLOCK = 64
SLC_RATIO = SLC_BLOCK // CMP_STRIDE
SLC_TOPN = 16
WINDOW = 512
Q_BLOCK = 64
RET_HEADS = 4
RET_HEAD_DIM = 128
RET_WIDTH = RET_HEADS * RET_HEAD_DIM
RET_CHUNK = 128
ROPE_BASE = 10000.0
NUM_BUCKETS = 32
MAX_DISTANCE = 128
D_FF = ((8 * D_MODEL + 3 * 256 - 1) // (3 * 256)) * 256
SPLIT_SIZES = (NSA_WIDTH,) + (NSA_KV_WIDTH,) * 6 + (3 * NSA_HEADS,) + (RET_WIDTH,) * 4
SPLIT_CUTS = tuple(int(c) for c in np.cumsum(SPLIT_SIZES)[:-1])
IN_COLS = sum(SPLIT_SIZES)
EPS = 1e-6
NEG = -1e30
FORCE_SCORE = 1e4

kernel_name = 'hybrid_nsa_retention_decode_step'


def rms_norm(x, w):
    xf = x.astype(jnp.float32)
    y = xf * lax.rsqrt(jnp.mean(xf * xf, axis=-1, keepdims=True) + EPS)
    return (y * w.astype(jnp.float32)).astype(x.dtype)


def rel_bucket(dist):
    n = jnp.maximum(dist, 0)
    exact = NUM_BUCKETS // 2
    nf = jnp.maximum(n, exact).astype(jnp.float32)
    large = exact + (jnp.log(nf / exact) / math.log(MAX_DISTANCE / exact) * (NUM_BUCKETS - exact)).astype(jnp.int32)
    return jnp.where(n < exact, n, jnp.minimum(large, NUM_BUCKETS - 1))


def rotary(x, pos):
    half = x.shape[-1] // 2
    inv = ROPE_BASE ** (-jnp.arange(half, dtype=jnp.float32) / half)
    ang = pos.astype(jnp.float32)[:, None] * inv[None, :]
    cos, sin = jnp.cos(ang)[:, None, :], jnp.sin(ang)[:, None, :]
    xf = x.astype(jnp.float32)
    x1, x2 = xf[..., :half], xf[..., half:]
    return jnp.concatenate([x1 * cos - x2 * sin, x1 * sin + x2 * cos], axis=-1)


def ret_log_gamma():
    return jnp.log1p(-jnp.exp2(-5.0 - jnp.arange(RET_HEADS, dtype=jnp.float32)))


def retention_chunk(state, qc, kc, vc):
    C = qc.shape[1]
    lg = ret_log_gamma()
    n = jnp.arange(C)
    diff = n[:, None] - n[None, :]
    decay = jnp.where(diff >= 0, jnp.exp(jnp.maximum(diff, 0)[None] * lg[:, None, None]), 0.0)
    att = jnp.einsum('bnhd,bmhd->bhnm', qc, kc) * decay
    o = jnp.einsum('bhnm,bmhe->bnhe', att, vc)
    o = o + jnp.einsum('bnhd,bhde->bnhe', qc, state) * jnp.exp((n[:, None] + 1) * lg[None, :])[None, :, :, None]
    k_dec = kc * jnp.exp((C - 1 - n)[:, None] * lg[None, :])[None, :, :, None]
    new_state = jnp.exp(C * lg)[None, :, None, None] * state + jnp.einsum('bmhd,bmhe->bhde', k_dec, vc)
    return o, new_state


def ret_output(o, g, gn_w):
    B, T = o.shape[:2]
    mu = jnp.mean(o, axis=-1, keepdims=True)
    var = jnp.mean(jnp.square(o - mu), axis=-1, keepdims=True)
    on = ((o - mu) * lax.rsqrt(var + EPS)).reshape(B, T, RET_WIDTH) * gn_w.astype(jnp.float32)
    return (jax.nn.silu(g.astype(jnp.float32)) * on).astype(g.dtype)


def compress(k, pe, w1, w2):
    B, T, G, D = k.shape
    nc = (T - CMP_BLOCK) // CMP_STRIDE + 1
    nseg = nc + CMP_SEGS - 1
    seg = k[:, :nseg * CMP_STRIDE].reshape(B, nseg, CMP_STRIDE, G, D)
    w1r = w1.reshape(CMP_SEGS, CMP_STRIDE, D, CMP_HIDDEN)
    hid = pe.reshape(-1) @ w1
    for j in range(CMP_SEGS):
        hid = hid + jnp.einsum('bnsgd,sdh->bngh', seg[:, j:j + nc], w1r[j])
    return jax.nn.silu(hid) @ w2


def to_blocks(k):
    B, T, G, D = k.shape
    nsb = -(-T // SLC_BLOCK)
    k = jnp.pad(k, ((0, 0), (0, nsb * SLC_BLOCK - T), (0, 0), (0, 0)))
    return k.reshape(B, nsb, SLC_BLOCK, G, D).transpose(0, 3, 1, 2, 4)


def nsa_attend(q, q_pos, kc, vc, kb, vb, kw, vw, kw_pos, gates, rel_bias):
    B, Q = q.shape[:2]
    G, R = NSA_KV_GROUPS, NSA_REP
    f32 = jnp.float32
    nc = kc.shape[1]
    c_end = jnp.arange(nc) * CMP_STRIDE + (CMP_BLOCK - 1)
    dist_c = q_pos[:, None] - c_end[None, :]
    valid_c = dist_c >= 0
    bias_c = rel_bias[rel_bucket(dist_c)].reshape(Q, nc, G, R).transpose(2, 3, 0, 1).astype(f32)
    s_c = jnp.einsum('bqgrd,bcgd->bgrqc', q, kc).astype(f32) + bias_c
    p_c = jax.nn.softmax(jnp.where(valid_c, s_c, NEG), axis=-1) * valid_c
    o_c = jnp.einsum('bgrqc,bcgd->bqgrd', p_c.astype(vc.dtype), vc)
    nsb = kb.shape[2]
    imp = jnp.pad(p_c.sum(axis=2), ((0, 0), (0, 0), (0, 0), (1, SLC_RATIO * (nsb + 1) - 1 - nc)))
    row_w = jnp.array([0.5] + [1.0] * (SLC_RATIO - 1), f32)
    imp_s = (imp[..., :SLC_RATIO * nsb].reshape(B, G, Q, nsb, SLC_RATIO) @ row_w
             + 0.5 * imp[..., SLC_RATIO::SLC_RATIO][..., :nsb])
    blk = jnp.arange(nsb)[None, :]
    q_blk = (q_pos // SLC_BLOCK)[:, None]
    valid_b = blk <= q_blk
    forced = (blk == 0) | (blk == q_blk) | (blk == q_blk - 1)
    score = jnp.where(valid_b, jnp.where(forced, FORCE_SCORE, imp_s), -1.0)
    n_top = min(SLC_TOPN, nsb)
    _, idx = lax.top_k(score, n_top)
    bi = jnp.arange(B)[:, None, None, None]
    gi = jnp.arange(G)[None, :, None, None]
    k_sel = kb[bi, gi, idx].reshape(B, G, Q, n_top * SLC_BLOCK, HEAD_DIM)
    v_sel = vb[bi, gi, idx].reshape(B, G, Q, n_top * SLC_BLOCK, HEAD_DIM)
    k_pos = (idx[..., None] * SLC_BLOCK + jnp.arange(SLC_BLOCK)).reshape(B, G, Q, n_top * SLC_BLOCK)
    dist_s = q_pos[:, None] - k_pos
    valid_s = dist_s >= 0
    tb = rel_bias.reshape(NUM_BUCKETS, G, R).transpose(1, 0, 2)
    bias_s = jnp.moveaxis(tb[gi, rel_bucket(dist_s)], -1, 2).astype(f32)
    s_s = jnp.einsum('bqgrd,bgqkd->bgrqk', q, k_sel).astype(f32) + bias_s
    p_s = jax.nn.softmax(jnp.where(valid_s[:, :, None], s_s, NEG), axis=-1)
    o_s = jnp.einsum('bgrqk,bgqkd->bqgrd', p_s.astype(v_sel.dtype), v_sel)
    L = kw.shape[1]
    dist_w = q_pos[:, None] - kw_pos[None, :]
    valid_w = (dist_w >= 0) & (dist_w < WINDOW) & (kw_pos >= 0)[None, :]
    bias_w = rel_bias[rel_bucket(dist_w)].reshape(Q, L, G, R).transpose(2, 3, 0, 1).astype(f32)
    s_w = jnp.einsum('bqgrd,blgd->bgrql', q, kw).astype(f32) + bias_w
    p_w = jax.nn.softmax(jnp.where(valid_w, s_w, NEG), axis=-1)
    o_w = jnp.einsum('bgrql,blgd->bqgrd', p_w.astype(vw.dtype), vw)
    g = jax.nn.sigmoid(gates)
    o = g[..., 0:1] * o_c + g[..., 1:2] * o_s + g[..., 2:3] * o_w
    return o.reshape(B, Q, NSA_WIDTH)


def mixer_inputs(x, n1, w_in, qn, kn, pos):
    B, T, _ = x.shape
    proj = rms_norm(x, n1) @ w_in
    q, kc, vc, ks, vs, kw, vw, gt, rq, rk, rv, rg = jnp.split(proj, SPLIT_CUTS, axis=-1)
    kvs = (B, T, NSA_KV_GROUPS, HEAD_DIM)
    q = rms_norm(q.reshape(B, T, NSA_KV_GROUPS, NSA_REP, HEAD_DIM), qn) * HEAD_DIM ** -0.5
    ks = rms_norm(ks.reshape(kvs), kn)
    kw = rms_norm(kw.reshape(kvs), kn)
    gt = jnp.moveaxis(gt.reshape(B, T, 3, NSA_KV_GROUPS, NSA_REP), 2, -1)
    rs = (B, T, RET_HEADS, RET_HEAD_DIM)
    rq = rotary(rq.reshape(rs), pos)
    rk = rotary(rk.reshape(rs), pos) * RET_HEAD_DIM ** -0.5
    rv = rv.reshape(rs).astype(jnp.float32)
    return (q, kc.reshape(kvs), vc.reshape(kvs), ks, vs.reshape(kvs), kw, vw.reshape(kvs), gt, rq, rk, rv, rg)


def layer_finish(x, nsa, ret, w_out, n2, wg, wu, wd, p, wpg, wpp):
    h = x + jnp.concatenate([nsa, ret.astype(nsa.dtype)], axis=-1) @ w_out
    hn = rms_norm(h, n2)
    h = h + (jax.nn.silu(hn @ wg) * (hn @ wu)) @ wd
    return h + jax.nn.sigmoid(h @ wpg) * (p @ wpp)


def layer_prompt(x, p, lw, rel_bias):
    (n1, w_in, qn, kn, pe_k, w1_k, w2_k, pe_v, w1_v, w2_v, gn_w, w_out, n2, wg, wu, wd, wpg, wpp) = lw
    B, S, _ = x.shape
    pos = jnp.arange(S)
    q, kc, vc, ks, vs, kw, vw, gt, rq, rk, rv, rg = mixer_inputs(x, n1, w_in, qn, kn, pos)
    ckc = rms_norm(compress(kc, pe_k, w1_k, w2_k), kn)
    cvc = compress(vc, pe_v, w1_v, w2_v)
    kb, vb = to_blocks(ks), to_blocks(vs)
    pad_w = ((0, 0), (WINDOW, 0), (0, 0), (0, 0))
    kw_pad, vw_pad = jnp.pad(kw, pad_w), jnp.pad(vw, pad_w)

    def q_block(i):
        s0 = i * Q_BLOCK
        qb = lax.dynamic_slice_in_dim(q, s0, Q_BLOCK, axis=1)
        gb = lax.dynamic_slice_in_dim(gt, s0, Q_BLOCK, axis=1)
        kwb = lax.dynamic_slice_in_dim(kw_pad, s0, WINDOW + Q_BLOCK, axis=1)
        vwb = lax.dynamic_slice_in_dim(vw_pad, s0, WINDOW + Q_BLOCK, axis=1)
        q_pos = s0 + jnp.arange(Q_BLOCK)
        kw_pos = s0 - WINDOW + jnp.arange(WINDOW + Q_BLOCK)
        return nsa_attend(qb, q_pos, ckc, cvc, kb, vb, kwb, vwb, kw_pos, gb, rel_bias)

    nsa = lax.map(q_block, jnp.arange(S // Q_BLOCK))
    nsa = jnp.moveaxis(nsa, 0, 1).reshape(B, S, NSA_WIDTH)
    nch = S // RET_CHUNK

    def chunks(a):
        return jnp.moveaxis(a.reshape(B, nch, RET_CHUNK, RET_HEADS, RET_HEAD_DIM), 1, 0)

    def step(st, inp):
        o, st = retention_chunk(st, *inp)
        return st, o

    st0 = jnp.zeros((B, RET_HEADS, RET_HEAD_DIM, RET_HEAD_DIM), jnp.float32)
    st_fin, o = lax.scan(step, st0, (chunks(rq), chunks(rk), chunks(rv)))
    o = jnp.moveaxis(o, 0, 1).reshape(B, S, RET_HEADS, RET_HEAD_DIM)
    ret = ret_output(o, rg, gn_w)
    y = layer_finish(x, nsa, ret, w_out, n2, wg, wu, wd, p, wpg, wpp)
    w = min(WINDOW, S)
    return y, (kc, vc, ks, vs, kw[:, S - w:], vw[:, S - w:], st_fin.astype(x.dtype))


def layer_sample(x, p, caches, page_table, lw, rel_bias):
    (n1, w_in, qn, kn, pe_k, w1_k, w2_k, pe_v, w1_v, w2_v, gn_w, w_out, n2, wg, wu, wd, wpg, wpp) = lw
    c_ck, c_cv, c_sk, c_sv, c_wk, c_wv, st = caches
    B, T, _ = x.shape
    n_past = page_table.shape[1] * c_ck.shape[1]
    pos = n_past + jnp.arange(T)
    q, kc, vc, ks, vs, kw, vw, gt, rq, rk, rv, rg = mixer_inputs(x, n1, w_in, qn, kn, pos)

    def past(c, new):
        rows = c[page_table].reshape(B, n_past, NSA_KV_GROUPS, HEAD_DIM)
        return jnp.concatenate([rows, new.astype(rows.dtype)], axis=1)

    ckc = rms_norm(compress(past(c_ck, kc), pe_k, w1_k, w2_k), kn)
    cvc = compress(past(c_cv, vc), pe_v, w1_v, w2_v)
    kb, vb = to_blocks(past(c_sk, ks)), to_blocks(past(c_sv, vs))
    kw_all = jnp.concatenate([c_wk, kw.astype(c_wk.dtype)], axis=1)
    vw_all = jnp.concatenate([c_wv, vw.astype(c_wv.dtype)], axis=1)
    wbuf = c_wk.shape[1]
    kw_pos = n_past - wbuf + jnp.arange(wbuf + T)
    nsa = nsa_attend(q, pos, ckc, cvc, kb, vb, kw_all, vw_all, kw_pos, gt, rel_bias)
    o, st_new = retention_chunk(st.astype(jnp.float32), rq, rk, rv)
    ret = ret_output(o, rg, gn_w)
    y = layer_finish(x, nsa, ret, w_out, n2, wg, wu, wd, p, wpg, wpp)
    return y, (kc, vc, ks, vs, kw_all[:, T:], vw_all[:, T:], st_new.astype(st.dtype))


def setup_inputs(seed: int = 0) -> dict:
    key = jax.random.key(seed)
    ks = jax.random.split(key, 32)
    n_pages = PAST_LEN // PAGE_SIZE
    n_pool = (5 * DEC_BATCH * n_pages + 3) // 4
    w_buf = min(WINDOW, PAST_LEN)

    def nrm(i, shape, scale=1.0):
        return scale * jax.random.normal(ks[i], shape, jnp.float32)

    def gain(i, shape):
        return 1.0 + nrm(i, shape, 0.02)

    pool = (DEPTH, n_pool, PAGE_SIZE, NSA_KV_GROUPS, HEAD_DIM)
    win = (DEPTH, DEC_BATCH, w_buf, NSA_KV_GROUPS, HEAD_DIM)
    page_table = jax.random.permutation(ks[0], n_pool)[:DEC_BATCH * n_pages].reshape(DEC_BATCH, n_pages).astype(jnp.int32)
    cmp_in = CMP_BLOCK * HEAD_DIM
    return {
        'x_prompt': nrm(1, (BATCH, SEQ, D_MODEL)),
        'x_sample': nrm(2, (DEC_BATCH, DEC_SEQ, D_MODEL)),
        'cache_cmp_k': nrm(3, pool),
        'cache_cmp_v': nrm(4, pool),
        'cache_slc_k': nrm(5, pool),
        'cache_slc_v': nrm(6, pool),
        'cache_win_k': nrm(7, win),
        'cache_win_v': nrm(8, win),
        'state_ret': nrm(9, (DEPTH, DEC_BATCH, RET_HEADS, RET_HEAD_DIM, RET_HEAD_DIM), 0.3),
        'page_table': page_table,
        'p_prompt': nrm(10, (DEPTH, BATCH, SEQ, PLE_DIM)),
        'p_sample': nrm(11, (DEPTH, DEC_BATCH, DEC_SEQ, PLE_DIM)),
        'norm1_w': gain(12, (DEPTH, D_MODEL)),
        'w_in': nrm(13, (DEPTH, D_MODEL, IN_COLS), D_MODEL ** -0.5),
        'q_norm_w': gain(14, (DEPTH, HEAD_DIM)),
        'k_norm_w': gain(15, (DEPTH, HEAD_DIM)),
        'cmp_pe_k': nrm(16, (DEPTH, CMP_BLOCK, HEAD_DIM), 0.1),
        'cmp_w1_k': nrm(17, (DEPTH, cmp_in, CMP_HIDDEN), cmp_in ** -0.5),
        'cmp_w2_k': nrm(18, (DEPTH, CMP_HIDDEN, HEAD_DIM), CMP_HIDDEN ** -0.5),
        'cmp_pe_v': nrm(19, (DEPTH, CMP_BLOCK, HEAD_DIM), 0.1),
        'cmp_w1_v': nrm(20, (DEPTH, cmp_in, CMP_HIDDEN), cmp_in ** -0.5),
        'cmp_w2_v': nrm(21, (DEPTH, CMP_HIDDEN, HEAD_DIM), CMP_HIDDEN ** -0.5),
        'ret_gn_w': gain(22, (DEPTH, RET_WIDTH)),
        'w_out': nrm(23, (DEPTH, D_MODEL, D_MODEL), D_MODEL ** -0.5),
        'norm2_w': gain(24, (DEPTH, D_MODEL)),
        'w_gate': nrm(25, (DEPTH, D_MODEL, D_FF), D_MODEL ** -0.5),
        'w_up': nrm(26, (DEPTH, D_MODEL, D_FF), D_MODEL ** -0.5),
        'w_down': nrm(27, (DEPTH, D_FF, D_MODEL), D_FF ** -0.5),
        'w_ple_gate': nrm(28, (DEPTH, D_MODEL, D_MODEL), D_MODEL ** -0.5),
        'w_ple_proj': nrm(29, (DEPTH, PLE_DIM, D_MODEL), PLE_DIM ** -0.5),
        'rel_bias': nrm(30, (NUM_BUCKETS, NSA_HEADS), 0.5),
    }


def reference(x_prompt, x_sample, cache_cmp_k, cache_cmp_v, cache_slc_k, cache_slc_v,
              cache_win_k, cache_win_v, state_ret, page_table, p_prompt, p_sample,
              norm1_w, w_in, q_norm_w, k_norm_w, cmp_pe_k, cmp_w1_k, cmp_w2_k,
              cmp_pe_v, cmp_w1_v, cmp_w2_v, ret_gn_w, w_out, norm2_w, w_gate, w_up, w_down,
              w_ple_gate, w_ple_proj, rel_bias):
    h_p, h_s = x_prompt, x_sample
    new_p, new_s = [], []
    for i in range(DEPTH):
        lw = (norm1_w[i], w_in[i], q_norm_w[i], k_norm_w[i], cmp_pe_k[i], cmp_w1_k[i], cmp_w2_k[i],
              cmp_pe_v[i], cmp_w1_v[i], cmp_w2_v[i], ret_gn_w[i], w_out[i], norm2_w[i],
              w_gate[i], w_up[i], w_down[i], w_ple_gate[i], w_ple_proj[i])
        h_p, st_p = layer_prompt(h_p, p_prompt[i], lw, rel_bias)
        caches = (cache_cmp_k[i], cache_cmp_v[i], cache_slc_k[i], cache_slc_v[i],
                  cache_win_k[i], cache_win_v[i], state_ret[i])
        h_s, st_s = layer_sample(h_s, p_sample[i], caches, page_table, lw, rel_bias)
        new_p.append(st_p)
        new_s.append(st_s)
    p_cmp_k, p_cmp_v, p_slc_k, p_slc_v, p_win_k, p_win_v, p_ret = [jnp.stack(a) for a in zip(*new_p)]
    s_cmp_k, s_cmp_v, s_slc_k, s_slc_v, s_win_k, s_win_v, s_ret = [jnp.stack(a) for a in zip(*new_s)]
    return (h_p, h_s, p_cmp_k, p_cmp_v, p_slc_k, p_slc_v, p_win_k, p_win_v, p_ret,
            s_cmp_k, s_cmp_v, s_slc_k, s_slc_v, s_win_k, s_win_v, s_ret)
```

```python
import math
import numpy as np
import concourse.bass as bass
import concourse.mybir as mybir
from concourse.bass_utils import run_bass_kernel_spmd

F32 = mybir.dt.float32
BF16 = mybir.dt.bfloat16
I32 = mybir.dt.int32
ALU = mybir.AluOpType
AF = mybir.ActivationFunctionType
AX = mybir.AxisListType

NCORES = 8
D = 1024
SEQ = 2048
NT = SEQ // 128
DB = 4
INC = 3352
DFF = 2816
NFF = DFF // 128
EPS = 1e-6
NPOOL = 5120
import os
ENABLE_DECODE_NSA = True
POOL_PAGES = NPOOL
SB_LO, SB_HI = 16512, 229376


class _Op:
    __slots__ = ("eng", "fn", "deps", "idx", "cidx", "is_dma", "key", "semval", "signal", "sigval")


class _Rec:
    def __getattr__(self, name):
        def f(*a, **k):
            self.call = (name, a, k)
            return self
        return f


class Sched:
    ENGS = ("pe", "act", "dve", "pool", "sp")

    def __init__(self, nc):
        self.nc = nc
        self.ops = {e: [] for e in self.ENGS}
        self.ccount = {e: 0 for e in self.ENGS}
        self.tw = {}
        self.tr = {}
        self.dma_cum = {}
        self.sb_off = SB_LO
        self.sb_peak = SB_LO
        self.nalloc = 0
        self.last = {e: None for e in self.ENGS}
        self.barrier_deps = []
        self.last_dma = {}

    def sb(self, shape, dt, name=None):
        nbytes = int(np.prod(shape[1:])) * mybir.dt.size(dt)
        nbytes = (nbytes + 63) // 64 * 64
        off = self.sb_off
        assert off + nbytes <= SB_HI, f"SBUF overflow {name} {off + nbytes}"
        self.sb_off += nbytes
        self.sb_peak = max(self.sb_peak, self.sb_off)
        self.nalloc += 1
        return self.nc.alloc_sbuf_tensor_at(f"{name or 't'}_{self.nalloc}", list(shape), dt, offset=off)

    def mark(self):
        return self.sb_off

    def release(self, m):
        self.sb_off = m

    def _deps(self, r, w):
        deps = []
        for t in r:
            if t in self.tw:
                deps.append(self.tw[t])
        for t in w:
            if t in self.tw:
                deps.append(self.tw[t])
            deps.extend(self.tr.get(t, ()))
        deps.extend(self.barrier_deps)
        deps = [self.last_dma[d.key] if d.is_dma else d for d in deps]
        return deps

    def _reg(self, op, r, w):
        for t in r:
            self.tr.setdefault(t, []).append(op)
        for t in w:
            self.tw[t] = op
            self.tr[t] = []
        self.last[op.eng] = op

    def op(self, eng, fn, r=(), w=()):
        o = _Op()
        rec = _Rec()
        fn(rec)
        name_, a_, k_ = rec.call
        o.eng, o.is_dma, o.key = eng, False, None
        o.fn = lambda e: getattr(e, name_)(*a_, **k_)
        o.deps = self._deps(r, w)
        o.idx = len(self.ops[eng])
        o.cidx = self.ccount[eng]
        self.ccount[eng] += 1
        o.signal = False
        self.ops[eng].append(o)
        self._reg(o, r, w)
        return o

    def dma(self, q, out, in_, r=(), w=(), key="d", **kw):
        o = _Op()
        o.eng, o.is_dma, o.key = q, True, key
        o.fn = lambda e: e.dma_start(out=out, in_=in_, **kw)
        o.deps = self._deps(r, w)
        o.idx = len(self.ops[q])
        o.cidx = self.ccount[q]
        self.dma_cum[key] = self.dma_cum.get(key, 0) + 16
        o.semval = self.dma_cum[key]
        o.signal = True
        self.last_dma[key] = o
        self.ops[q].append(o)
        self._reg(o, r, w)
        return o

    def dmafn(self, q, fn, r=(), w=(), key="d"):
        o = self.dma(q, None, None, r, w, key)
        o.fn = fn
        return o

    def barrier(self):
        self.barrier_deps = [o for o in self.last.values() if o is not None]
        seen = {}
        for e in self.ENGS:
            for o in self.ops[e]:
                if o.is_dma:
                    seen[o.key] = o
        self.barrier_deps += list(seen.values())

    def emit(self, final_wait=True):
        nc = self.nc
        for e in self.ENGS:
            for o in self.ops[e]:
                for d in o.deps:
                    if not d.is_dma:
                        d.signal = True
        sems = {}
        for e in self.ENGS:
            sems[e] = nc.alloc_semaphore(f"s_{e}")
            c = 0
            for o in self.ops[e]:
                if not o.is_dma and o.signal:
                    c += 1
                    o.sigval = c
        for k in self.dma_cum:
            sems["dma:" + k] = nc.alloc_semaphore(f"sd_{k}")
        engmap = {"pe": "tensor", "act": "scalar", "dve": "vector", "pool": "gpsimd", "sp": "sync"}

        def emit_engine(ename, eh):
            seen = {}
            for o in self.ops[ename]:
                for d in o.deps:
                    if d.is_dma:
                        sk, val = "dma:" + d.key, d.semval
                    else:
                        if d.eng == ename:
                            if ename == "pe" and not o.is_dma:
                                continue
                            if (not o.is_dma) and o.cidx - d.cidx >= 2:
                                continue
                        sk, val = d.eng, d.sigval
                    if seen.get(sk, 0) >= val:
                        continue
                    seen[sk] = val
                    eh.wait_ge(sems[sk], val)
                ins = o.fn(eh)
                if o.is_dma:
                    ins.then_inc(sems["dma:" + o.key], 16)
                elif o.signal:
                    ins.then_inc(sems[ename], 1)
            if final_wait and ename == "sp":
                for k, v in self.dma_cum.items():
                    eh.wait_ge(sems["dma:" + k], v)
                for e2 in self.ENGS:
                    c = sum(1 for o in self.ops[e2] if not o.is_dma and o.signal)
                    if c and e2 != "sp":
                        eh.wait_ge(sems[e2], c)

        with nc.Block() as block:
            @block.tensor
            def _(eh):
                emit_engine("pe", eh)

            @block.scalar
            def _(eh):
                emit_engine("act", eh)

            @block.vector
            def _(eh):
                emit_engine("dve", eh)

            @block.gpsimd
            def _(eh):
                emit_engine("pool", eh)

            @block.sync
            def _(eh):
                emit_engine("sp", eh)


def _rel_bucket_np(dist):
    n = np.maximum(dist, 0)
    nf = np.maximum(n, 16).astype(np.float32)
    large = 16 + (np.log(nf / np.float32(16)) / np.float32(math.log(8.0)) * np.float32(16)).astype(np.int32)
    return np.where(n < 16, n, np.minimum(large, 31)).astype(np.int64)


def _consts():
    c = {}
    c["ident"] = np.eye(128, dtype=np.float32)
    half = 64
    inv = (10000.0 ** (-np.arange(half, dtype=np.float32) / half)).astype(np.float32)
    pos = np.concatenate([np.arange(SEQ), np.array([16384])]).astype(np.float32)
    ang = pos[:, None] * inv[None, :]
    cs = np.zeros((SEQ + 128, 2, 64), np.float32)
    cs[:SEQ + 1, 0] = np.cos(ang)
    cs[:SEQ + 1, 1] = np.sin(ang)
    cs[SEQ:SEQ + 128] = cs[SEQ]
    c["cossin"] = cs.reshape(SEQ + 128, 128)
    lg = np.log1p(-np.exp2(-5.0 - np.arange(4, dtype=np.float64)))
    n = np.arange(128)
    diff = n[None, :] - n[:, None]
    dec = np.where(diff >= 0, np.exp(np.maximum(diff, 0)[None] * lg[:, None, None]), 0.0)
    c["decT"] = np.ascontiguousarray(dec.transpose(1, 0, 2)).astype(np.float32).reshape(128, 512)
    cq = np.exp((n[None, :] + 1) * lg[:, None])
    c["cq"] = np.broadcast_to(cq[None], (128, 4, 128)).astype(np.float32).reshape(128, 512).copy()
    kd = np.exp((127 - n)[:, None] * lg[None, :])
    g128 = np.exp(128 * lg)
    g1 = np.exp(lg)
    misc = np.zeros((128, 16), np.float32)
    misc[:, 0:4] = kd
    misc[:, 4:8] = g128[None, :]
    misc[:, 8:12] = g1[None, :]
    c["rmisc"] = misc
    import ml_dtypes
    bf = ml_dtypes.bfloat16
    LA = 4111
    m = np.arange(LA)
    dA = m - 2063
    oh = np.zeros((32, LA), np.float32)
    oh[_rel_bucket_np(dA), m] = 1.0
    c["ohA"] = oh.astype(bf)
    c["validA"] = np.broadcast_to((dA >= 0).astype(np.float32)[None], (8, LA)).copy()
    c["Jb"] = np.eye(128, dtype=np.float32)[::-1].copy().astype(bf)
    kk = np.arange(128)
    c["LT"] = (kk[None, :] < kk[:, None]).astype(np.float32).astype(bf)
    E = np.zeros((32, 16, 128), np.float32)
    for kt in range(16):
        E[2 * kt + kk // 64, kt, kk] = 1.0
    c["Eoh"] = E.reshape(32, 2048).astype(bf)
    A = np.zeros((128, 8, 32), np.float32)
    Bm = np.zeros((128, 8, 32), np.float32)
    blk = np.arange(32)
    for q8 in range(8):
        qpos = (8 + q8) * 128 + kk
        qb = qpos // 64
        valid = blk[None, :] <= qb[:, None]
        forced = (blk[None, :] == 0) | (blk[None, :] == qb[:, None]) | (blk[None, :] == qb[:, None] - 1)
        A[:, q8] = (valid & ~forced)
        Bm[:, q8] = np.where(valid, np.where(forced, 1e4, 0.0), -1.0)
    c["selA"] = A.reshape(128, 256)
    c["selB"] = Bm.reshape(128, 256)
    W = np.zeros((128, 32), np.float32)
    for j in range(32):
        for cc, wv in ((4 * j - 1, 0.5), (4 * j, 1.0), (4 * j + 1, 1.0), (4 * j + 2, 1.0), (4 * j + 3, 0.5)):
            if 0 <= cc < 127:
                W[cc, j] += wv
    c["Wimp"] = W.astype(bf)
    q = np.arange(128)
    SelC = np.zeros((128, 8, 128), np.float32)
    for j in range(128):
        for r in range(8):
            n_ = 8 * j + r
            if n_ <= 1022:
                SelC[min(16353 - 16 * n_, 127), r, j] = 1.0
    c["SelC"] = SelC.reshape(128, 1024).astype(bf)
    SelW = np.zeros((128, 4, 128), np.float32)
    for p in range(128):
        for u in range(4):
            SelW[min(511 - 4 * p - u, 127), u, p] = 1.0
    c["SelW"] = SelW.reshape(128, 512).astype(bf)
    OHt = np.zeros((128, 2, 128), np.float32)
    for t in range(128):
        OHt[min(128 - t, 127), 0, t] = 1.0
    OHt[127, 1, :] = 1.0
    c["OHt"] = OHt.reshape(128, 256)
    LAB = np.zeros((128, 2, 128), np.float32)
    LAB[:, 0, 127] = 1.0
    LAB[:, 1, :127] = 1.0
    c["LAB"] = LAB.reshape(128, 256).astype(bf)
    A2 = np.ones((8, 2, 128), np.float32)
    B2 = np.zeros((8, 2, 128), np.float32)
    A2[:, 0, 0] = 0.0; A2[:, 1, 127] = 0.0
    B2[:, 0, 0] = 1e4; B2[:, 1, 127] = 1e4
    c["A2"] = A2.reshape(8, 256)
    c["B2"] = B2.reshape(8, 256)
    Wr = np.zeros((128, 2, 8), np.float32)
    Wr[:, 0] = np.array([1, 1, 1, .5, 0, 0, 0, 0], np.float32)
    Wr[:, 1] = np.array([0, 0, 0, .5, 1, 1, 1, .5], np.float32)
    c["Wr"] = Wr.reshape(128, 16)
    return c


CONST_SHAPES = {
    "ident": ([128, 128], F32), "cossin": ([SEQ + 128, 128], F32), "decT": ([128, 512], F32), "cq": ([128, 512], F32),
    "rmisc": ([128, 16], F32),
    "ohA": ([32, 4111], BF16), "validA": ([8, 4111], F32), "Jb": ([128, 128], BF16), "LT": ([128, 128], BF16),
    "Eoh": ([32, 2048], BF16), "selA": ([128, 256], F32), "selB": ([128, 256], F32), "Wimp": ([128, 32], BF16),
    "SelC": ([128, 1024], BF16), "SelW": ([128, 512], BF16), "OHt": ([128, 256], F32), "LAB": ([128, 256], BF16),
    "A2": ([8, 256], F32), "B2": ([8, 256], F32), "Wr": ([128, 16], F32),
}

IN_SPECS = {
    "x_prompt": ([SEQ, D], F32), "x_sample": ([DB, D], F32),
    "state_ret": ([DB, 4, 128, 128], F32),
    "cache_win_k": ([DB, 512, 128], F32), "cache_win_v": ([DB, 512, 128], F32),
    "norm1_w": ([D], F32), "w_in": ([D, INC], F32), "q_norm_w": ([64], F32), "k_norm_w": ([64], F32),
    "p_prompt": ([SEQ, 256], F32), "p_sample": ([DB, 256], F32),
    "cmp_pe_k": ([32, 64], F32), "cmp_w1_k": ([2048, 128], F32), "cmp_w2_k": ([128, 64], F32),
    "cmp_pe_v": ([32, 64], F32), "cmp_w1_v": ([2048, 128], F32), "cmp_w2_v": ([128, 64], F32),
    "ret_gn_w": ([512], F32), "w_out": ([D, D], F32), "norm2_w": ([D], F32),
    "w_gate": ([D, DFF], F32), "w_up": ([D, DFF], F32), "w_down": ([DFF, D], F32),
    "w_ple_gate": ([D, D], F32), "w_ple_proj": ([256, D], F32), "rel_bias": ([32, 8], F32),
}
if ENABLE_DECODE_NSA:
    IN_SPECS.update({
        "page_table": ([DB, 128], I32),
        "cache_cmp_k": ([POOL_PAGES * 8, 2048], F32), "cache_cmp_v": ([POOL_PAGES * 8, 2048], F32),
        "cache_slc_k": ([POOL_PAGES * 8, 2048], F32), "cache_slc_v": ([POOL_PAGES * 8, 2048], F32),
    })
OUT_SPECS = {
    "y_prompt": [SEQ, D], "y_sample": [DB, D],
    "p_cmp_k": [SEQ, 128], "p_cmp_v": [SEQ, 128], "p_slc_k": [SEQ, 128], "p_slc_v": [SEQ, 128],
    "p_win_k": [512, 128], "p_win_v": [512, 128], "p_ret": [4, 128, 128],
    "s_cmp_k": [DB, 128], "s_cmp_v": [DB, 128], "s_slc_k": [DB, 128], "s_slc_v": [DB, 128],
    "s_win_k": [DB, 512, 128], "s_win_v": [DB, 512, 128], "s_ret": [DB, 4, 128, 128],
}
OUT_ORDER = ["y_prompt", "y_sample", "p_cmp_k", "p_cmp_v", "p_slc_k", "p_slc_v", "p_win_k", "p_win_v",
             "p_ret", "s_cmp_k", "s_cmp_v", "s_slc_k", "s_slc_v", "s_win_k", "s_win_v", "s_ret"]


def build_program():
    nc = bass.Bass("TRN2", target_bir_lowering=False)
    din = {k: nc.dram_tensor(k, sh, dt, kind="ExternalInput").ap() for k, (sh, dt) in IN_SPECS.items()}
    for k, (sh, dt) in CONST_SHAPES.items():
        din[k] = nc.dram_tensor("c_" + k, sh, dt, kind="ExternalInput").ap()
    tblA_h = nc.dram_tensor("tblA", [8, 4111], F32, kind="Internal")
    dsc_h = nc.dram_tensor("dsc", [4, 3, DB, 2, 64], F32, kind="Internal")
    dout = {k: nc.dram_tensor(k, sh, F32, kind="ExternalOutput").ap() for k, sh in OUT_SPECS.items()}

    S = Sched(nc)
    P = 128
    ident = S.sb([P, 128], F32, "ident")
    identb = S.sb([P, 128], BF16, "identb")
    decT = S.sb([P, 512], F32, "decT")
    cq = S.sb([P, 512], F32, "cq")
    rmisc = S.sb([P, 16], F32, "rmisc")
    qnw = S.sb([P, 64], F32, "qnw")
    knw = S.sb([P, 64], F32, "knw")
    n1c = S.sb([P, 8], F32, "n1c")
    epsc = S.sb([P, 1], F32, "epsc")
    S.op("pool", lambda e: e.memset(epsc[:], EPS), w=["epsc"])
    S.dma("sp", ident[:], din["ident"], w=["ident"], key="c")
    S.dma("sp", decT[:], din["decT"], w=["decT"], key="c")
    S.dma("sp", cq[:], din["cq"], w=["cq"], key="c")
    S.dma("sp", rmisc[:], din["rmisc"], w=["rmisc"], key="c")
    S.dma("sp", qnw[:], din["q_norm_w"].partition_broadcast(P), w=["qnw"], key="c")
    S.dma("sp", knw[:], din["k_norm_w"].partition_broadcast(P), w=["knw"], key="c")
    S.dma("sp", n1c[:], din["norm1_w"].rearrange("(k p) -> p k", p=P), w=["n1c"], key="c",
          allow_slow_non_contiguous=True)
    S.op("dve", lambda e: e.tensor_copy(identb[:], ident[:]), r=["ident"], w=["identb"])
    S.op("dve", lambda e: e.tensor_scalar(qnw[:], qnw[:], 0.125, None, ALU.mult), r=["qnw"], w=["qnw"])

    catT = S.sb([P, 8, SEQ + 128], BF16, "catT")
    mP1 = S.mark()
    QT = S.sb([P, NT + 1, 4, 128], BF16, "QT")
    KsT = S.sb([P, SEQ], BF16, "KsT")
    KwT = S.sb([P, SEQ], BF16, "KwT")
    kcT = S.sb([P, SEQ], BF16, "kcT")
    vcT = S.sb([P, SEQ], BF16, "vcT")
    Vs = S.sb([P, NT, 2, 65], BF16, "Vs")
    Vw = S.sb([P, NT, 2, 65], BF16, "Vw")
    gates = S.sb([P, NT + 1, 24], F32, "gates")
    knewb = S.sb([P, 128], BF16, "knewb")
    Vn = S.sb([P, 2, 64], BF16, "Vn")
    mP2 = S.mark()
    Sst = S.sb([P, 512], F32, "Sst")
    Sbf = S.sb([P, 512], BF16, "Sbf")

    win = S.sb([P, 8, INC], BF16, "win")
    m0 = S.mark()
    wst = [S.sb([P, INC], F32, f"wst{i}") for i in range(2)]
    qperm = [0, 4, 1, 5, 2, 6, 3, 7]
    for kc in range(8):
        b = kc % 2
        S.dma("sp", wst[b][:], din["w_in"][kc * P:(kc + 1) * P, :], w=[f"wst{b}"], key=f"wst{b}")
        eng = "dve" if kc % 2 == 0 else "pool"
        sc = n1c[:, kc:kc + 1]
        for j, h in enumerate(qperm):
            S.op(eng, lambda e, j=j, h=h, b=b, kc=kc, sc=sc: e.tensor_scalar(
                win[:, kc, j * 64:(j + 1) * 64], wst[b][:, h * 64:(h + 1) * 64], sc, None, ALU.mult),
                r=[f"wst{b}", "n1c"], w=[f"win{kc}"])
        S.op(eng, lambda e, b=b, kc=kc, sc=sc: e.tensor_scalar(
            win[:, kc, 512:1816], wst[b][:, 512:1816], sc, None, ALU.mult), r=[f"wst{b}", "n1c"], w=[f"win{kc}"])
        S.op(eng, lambda e, b=b, kc=kc, sc=sc: e.tensor_scalar(
            win[:, kc, 1816:2328], wst[b][:, 1816:2328], sc, 128.0 ** -0.5, ALU.mult, ALU.mult),
            r=[f"wst{b}", "n1c"], w=[f"win{kc}"])
        S.op(eng, lambda e, b=b, kc=kc, sc=sc: e.tensor_scalar(
            win[:, kc, 2328:INC], wst[b][:, 2328:INC], sc, None, ALU.mult), r=[f"wst{b}", "n1c"], w=[f"win{kc}"])
    WIN_ALL = [f"win{kc}" for kc in range(8)]
    S.barrier()
    S.release(m0)

    xt = [S.sb([P, D], F32, f"xt{i}") for i in range(2)]
    junk = S.sb([P, D], BF16, "junk")
    xn = S.sb([P, D], BF16, "xn")
    xnT = [S.sb([P, 8, 128], BF16, f"xnT{i}") for i in range(2)]
    st = S.sb([P, 32], F32, "st")
    sq = S.sb([P, 512], F32, "sq")
    q32 = S.sb([P, 512], F32, "q32")
    qbf = S.sb([P, 512], BF16, "qbf")
    kv32 = [S.sb([P, 512], F32, f"kv32{i}") for i in range(2)]
    kw32 = [S.sb([P, 256], F32, f"kw32{i}") for i in range(2)]
    kbf = S.sb([P, 256], BF16, "kbf")
    gt32 = S.sb([P, 24], F32, "gt32")
    r32 = S.sb([P, 1024], F32, "r32")
    rtmp = S.sb([P, 1024], F32, "rtmp")
    rqk = S.sb([P, 1024], BF16, "rqk")
    kdec = S.sb([P, 512], BF16, "kdec")
    rvb = S.sb([P, 512], BF16, "rvb")
    srg = S.sb([P, 512], F32, "srg")
    rqT = S.sb([P, 512], BF16, "rqT")
    rkT = S.sb([P, 512], BF16, "rkT")
    qsT = S.sb([P, 512], BF16, "qsT")
    attm = S.sb([P, 512], BF16, "attm")
    o32 = S.sb([P, 512], F32, "o32")
    retb = S.sb([P, 512], BF16, "retb")
    onrm = S.sb([P, 512], F32, "onrm")
    cs = [S.sb([P, 128], F32, f"cs{i}") for i in range(2)]

    pA = [nc.alloc_psum_tensor(f"pA{i}", [P, 512], F32) for i in range(4)]
    pT = [nc.alloc_psum_tensor(f"pT{i}", [P, 1024], BF16) for i in range(2)]
    pR = [nc.alloc_psum_tensor(f"pR{i}", [P, 512], F32) for i in range(2)]

    def rms_cols(eng_src_ap, ngrp, stcol):
        pass

    def rsqrt_cols(c0, c1, tok, add_eps=True):
        S.op("act", lambda e: e.activation(st[:, c0:c1], st[:, c0:c1], AF.Ln, bias=epsc[:, 0:1] if add_eps else 0.0),
             r=[tok, "epsc"], w=[tok])
        S.op("act", lambda e: e.activation(st[:, c0:c1], st[:, c0:c1], AF.Exp, scale=-0.5), r=[tok], w=[tok])

    pa_i = [0]

    def next_pa():
        i = pa_i[0] % 4
        pa_i[0] += 1
        return i

    pt_i = [0]

    def next_pt():
        i = pt_i[0] % 2
        pt_i[0] += 1
        return i

    for t in range(NT + 1):
        samp = (t == NT)
        nv = DB if samp else P
        xb = t % 2
        if samp:
            S.op("pool", lambda e, xb=xb: e.memset(xt[xb][:], 0.0), w=[f"xt{xb}"])
            S.dma("sp", xt[xb][0:DB, :], din["x_sample"], w=[f"xt{xb}"], key=f"xt{xb}")
            S.dma("sp", cs[xb][:], din["cossin"][SEQ:SEQ + P, :], w=[f"cs{xb}"], key=f"xt{xb}")
        else:
            S.dma("sp", xt[xb][:], din["x_prompt"][t * P:(t + 1) * P, :], w=[f"xt{xb}"], key=f"xt{xb}")
            S.dma("sp", cs[xb][:], din["cossin"][t * P:(t + 1) * P, :], w=[f"cs{xb}"], key=f"xt{xb}")
        S.op("act", lambda e, xb=xb: e.activation(junk[:], xt[xb][:], AF.Square, scale=1.0 / 32.0,
                                                   accum_out=st[:, 0:1]), r=[f"xt{xb}"], w=["junk", "st0"])
        rsqrt_cols(0, 1, "st0")
        S.op("act", lambda e, xb=xb: e.activation(xn[:], xt[xb][:], AF.Copy, scale=st[:, 0:1]),
             r=[f"xt{xb}", "st0"], w=["xn"])
        for hf in range(2):
            pi = next_pt()
            for k4 in range(4):
                kc = hf * 4 + k4
                S.op("pe", lambda e, pi=pi, k4=k4, kc=kc: e.transpose(
                    pT[pi][:, k4 * 128:(k4 + 1) * 128], xn[:, kc * 128:(kc + 1) * 128], identb[:]),
                    r=["xn", "identb"], w=[f"pT{pi}"])
            S.op("dve", lambda e, pi=pi, hf=hf, xb=xb: e.tensor_copy(
                xnT[xb][:, hf * 4:(hf + 1) * 4, :].rearrange("p a b -> p (a b)"), pT[pi][:, 0:512]),
                r=[f"pT{pi}"], w=[f"xnT{xb}_{hf}"])
        XN = [f"xnT{xb}_0", f"xnT{xb}_1"]

        def proj(c0, c1):
            pi = next_pa()
            for kc in range(8):
                S.op("pe", lambda e, pi=pi, kc=kc, c0=c0, c1=c1: e.matmul(
                    pA[pi][:, 0:c1 - c0], xnT[xb][:, kc, :], win[:, kc, c0:c1], start=(kc == 0), stop=(kc == 7)),
                    r=XN + [f"win{kc}"], w=[f"pA{pi}"])
            return pi

        pi = proj(0, 512)
        S.op("act", lambda e, pi=pi: e.activation(sq[:], pA[pi][:], AF.Square, scale=0.125),
             r=[f"pA{pi}"], w=["sq"])
        S.op("dve", lambda e: e.tensor_reduce(st[:, 2:10], sq[:].rearrange("p (h d) -> p h d", d=64), AX.X, ALU.add),
             r=["sq"], w=["st2"])
        rsqrt_cols(2, 10, "st2")
        S.op("dve", lambda e, pi=pi: e.tensor_tensor(
            q32[:].rearrange("p (h d) -> p h d", d=64), pA[pi][:].rearrange("p (h d) -> p h d", d=64),
            st[:, 2:10].unsqueeze(2).broadcast_to([P, 8, 64]), ALU.mult), r=[f"pA{pi}", "st2"], w=["q32"])
        S.op("dve", lambda e: e.tensor_tensor(
            qbf[:].rearrange("p (h d) -> p h d", d=64), q32[:].rearrange("p (h d) -> p h d", d=64),
            qnw[:].unsqueeze(1).broadcast_to([P, 8, 64]), ALU.mult), r=["q32", "qnw"], w=["qbf"])
        pi2 = next_pt()
        for j in range(4):
            S.op("pe", lambda e, pi2=pi2, j=j: e.transpose(
                pT[pi2][:, j * 128:(j + 1) * 128], qbf[:, j * 128:(j + 1) * 128], identb[:]),
                r=["qbf", "identb"], w=[f"pT{pi2}"])
        S.op("act", lambda e, pi2=pi2, t=t: e.copy(QT[:, t, :, :].rearrange("p a b -> p (a b)"), pT[pi2][:, 0:512]),
             r=[f"pT{pi2}"], w=[f"QT{t}"])

        kb = t % 2
        pi = proj(512, 1024)
        S.op("act", lambda e, pi=pi, kb=kb: e.copy(kv32[kb][:], pA[pi][:]), r=[f"pA{pi}"], w=[f"kv32{kb}"])
        pj = proj(1024, 1304)
        S.op("act", lambda e, pj=pj, kb=kb: e.copy(kw32[kb][:], pA[pj][:, 0:256]), r=[f"pA{pj}"], w=[f"kw32{kb}"])
        S.op("act", lambda e, pj=pj: e.activation(gt32[:], pA[pj][:, 256:280], AF.Exp, scale=-1.0),
             r=[f"pA{pj}"], w=["gt32"])
        S.op("dve", lambda e: e.tensor_scalar(gt32[:], gt32[:], 1.0, None, ALU.add), r=["gt32"], w=["gt32"])
        S.op("dve", lambda e, t=t: e.reciprocal(gates[:, t, :], gt32[:]), r=["gt32"], w=[f"gates{t}"])
        for (buf, c0, tok, stc, kcol) in ((kv32[kb], 256, f"kv32{kb}", 10, 0), (kw32[kb], 0, f"kw32{kb}", 12, 128)):
            S.op("dve", lambda e, buf=buf, c0=c0: e.tensor_tensor(sq[:, 0:128], buf[:, c0:c0 + 128], buf[:, c0:c0 + 128], ALU.mult),
                 r=[tok], w=["sq"])
            S.op("dve", lambda e, stc=stc: e.tensor_reduce(st[:, stc:stc + 2], sq[:, 0:128].rearrange("p (g d) -> p g d", d=64), AX.X, ALU.add),
                 r=["sq"], w=[f"st{stc}"])
            S.op("dve", lambda e, stc=stc: e.tensor_scalar(st[:, stc:stc + 2], st[:, stc:stc + 2], 1.0 / 64.0, EPS, ALU.mult, ALU.add),
                 r=[f"st{stc}"], w=[f"st{stc}"])
            rsqrt_cols(stc, stc + 2, f"st{stc}", add_eps=False)
            S.op("dve", lambda e, buf=buf, c0=c0, stc=stc: e.tensor_tensor(
                buf[:, c0:c0 + 128].rearrange("p (g d) -> p g d", d=64), buf[:, c0:c0 + 128].rearrange("p (g d) -> p g d", d=64),
                st[:, stc:stc + 2].unsqueeze(2).broadcast_to([P, 2, 64]), ALU.mult), r=[tok, f"st{stc}"], w=[tok])
            S.op("dve", lambda e, buf=buf, c0=c0: e.tensor_tensor(
                buf[:, c0:c0 + 128].rearrange("p (g d) -> p g d", d=64), buf[:, c0:c0 + 128].rearrange("p (g d) -> p g d", d=64),
                knw[:].unsqueeze(1).broadcast_to([P, 2, 64]), ALU.mult), r=[tok, "knw"], w=[tok])
            S.op("dve", lambda e, buf=buf, c0=c0, kcol=kcol: e.tensor_copy(kbf[:, kcol:kcol + 128], buf[:, c0:c0 + 128]),
                 r=[tok], w=["kbf"])
        if not samp:
            rows = slice(t * P, (t + 1) * P)
            S.dma("pool", dout["p_cmp_k"][rows, :], kv32[kb][:, 0:128], r=[f"kv32{kb}"], key=f"okv{kb}")
            S.dma("pool", dout["p_cmp_v"][rows, :], kv32[kb][:, 128:256], r=[f"kv32{kb}"], key=f"okv{kb}")
            S.dma("pool", dout["p_slc_k"][rows, :], kv32[kb][:, 256:384], r=[f"kv32{kb}"], key=f"okv{kb}")
            S.dma("pool", dout["p_slc_v"][rows, :], kv32[kb][:, 384:512], r=[f"kv32{kb}"], key=f"okv{kb}")
            if t >= NT - 4:
                wr = slice((t - (NT - 4)) * P, (t - (NT - 4) + 1) * P)
                S.dma("pool", dout["p_win_k"][wr, :], kw32[kb][:, 0:128], r=[f"kw32{kb}"], key=f"okw{kb}")
                S.dma("pool", dout["p_win_v"][wr, :], kw32[kb][:, 128:256], r=[f"kw32{kb}"], key=f"okw{kb}")
        else:
            S.dma("pool", dout["s_cmp_k"], kv32[kb][0:DB, 0:128], r=[f"kv32{kb}"], key=f"okv{kb}")
            S.dma("pool", dout["s_cmp_v"], kv32[kb][0:DB, 128:256], r=[f"kv32{kb}"], key=f"okv{kb}")
            S.dma("pool", dout["s_slc_k"], kv32[kb][0:DB, 256:384], r=[f"kv32{kb}"], key=f"okv{kb}")
            S.dma("pool", dout["s_slc_v"], kv32[kb][0:DB, 384:512], r=[f"kv32{kb}"], key=f"okv{kb}")
            for b in range(DB):
                S.dma("pool", dout["s_win_k"][b, 0:511, :], din["cache_win_k"][b, 1:512, :], w=[f"swk{b}"], key="owin")
                S.dma("pool", dout["s_win_v"][b, 0:511, :], din["cache_win_v"][b, 1:512, :], w=[f"swv{b}"], key="owin")
                S.dma("pool", dout["s_win_k"][b, 511:512, :], kw32[kb][b:b + 1, 0:128], r=[f"kw32{kb}"], w=[f"swk{b}"], key=f"okw{kb}")
                S.dma("pool", dout["s_win_v"][b, 511:512, :], kw32[kb][b:b + 1, 128:256], r=[f"kw32{kb}"], w=[f"swv{b}"], key=f"okw{kb}")
        if not samp:
            pk = next_pa()
            S.op("pe", lambda e, pk=pk, kb=kb: e.transpose(pA[pk][:, 0:128], kv32[kb][:, 0:128], ident[:]),
                 r=[f"kv32{kb}", "ident"], w=[f"pA{pk}"])
            S.op("pe", lambda e, pk=pk, kb=kb: e.transpose(pA[pk][:, 128:256], kv32[kb][:, 128:256], ident[:]),
                 r=[f"kv32{kb}", "ident"], w=[f"pA{pk}"])
            S.op("act", lambda e, pk=pk, t=t: e.copy(kcT[:, t * P:(t + 1) * P], pA[pk][:, 0:128]), r=[f"pA{pk}"], w=[f"kcT{t}"])
            S.op("act", lambda e, pk=pk, t=t: e.copy(vcT[:, t * P:(t + 1) * P], pA[pk][:, 128:256]), r=[f"pA{pk}"], w=[f"vcT{t}"])
            pi2 = next_pt()
            S.op("pe", lambda e, pi2=pi2: e.transpose(pT[pi2][:, 0:128], kbf[:, 0:128], identb[:]),
                 r=["kbf", "identb"], w=[f"pT{pi2}"])
            S.op("pe", lambda e, pi2=pi2: e.transpose(pT[pi2][:, 128:256], kbf[:, 128:256], identb[:]),
                 r=["kbf", "identb"], w=[f"pT{pi2}"])
            S.op("act", lambda e, pi2=pi2, t=t: e.copy(KsT[:, t * P:(t + 1) * P], pT[pi2][:, 0:128]),
                 r=[f"pT{pi2}"], w=[f"KsT{t}"])
            S.op("act", lambda e, pi2=pi2, t=t: e.copy(KwT[:, t * P:(t + 1) * P], pT[pi2][:, 128:256]),
                 r=[f"pT{pi2}"], w=[f"KwT{t}"])
            S.op("pool", lambda e, t=t: e.memset(Vs[:, t, :, 64:65], 1.0), w=[f"Vs{t}"])
            S.op("pool", lambda e, t=t: e.memset(Vw[:, t, :, 64:65], 1.0), w=[f"Vw{t}"])
            S.op("pool", lambda e, t=t, kb=kb: e.tensor_copy(Vs[:, t, :, 0:64], kv32[kb][:, 384:512].rearrange("p (g d) -> p g d", d=64)),
                 r=[f"kv32{kb}"], w=[f"Vs{t}"])
            S.op("pool", lambda e, t=t, kb=kb: e.tensor_copy(Vw[:, t, :, 0:64], kw32[kb][:, 128:256].rearrange("p (g d) -> p g d", d=64)),
                 r=[f"kw32{kb}"], w=[f"Vw{t}"])

        pi = proj(1304, 1816)
        S.op("act", lambda e, pi=pi: e.copy(r32[:, 0:512], pA[pi][:]), r=[f"pA{pi}"], w=["r32a"])
        pi = proj(1816, 2328)
        S.op("act", lambda e, pi=pi: e.copy(r32[:, 512:1024], pA[pi][:]), r=[f"pA{pi}"], w=["r32b"])
        x4 = r32[:].rearrange("p (a two d) -> p a two d", two=2, d=64)
        t4 = rtmp[:].rearrange("p (a two d) -> p a two d", two=2, d=64)
        o4 = rqk[:].rearrange("p (a two d) -> p a two d", two=2, d=64)
        cosb = cs[xb][:, 0:64].unsqueeze(1).broadcast_to([P, 8, 64])
        sinb = cs[xb][:, 64:128].unsqueeze(1).broadcast_to([P, 8, 64])
        RR = ["r32a", "r32b", f"cs{xb}"]
        S.op("pool", lambda e: e.tensor_tensor(t4[:, :, 0, :], x4[:, :, 0, :], cosb, ALU.mult), r=RR, w=["rt0"])
        S.op("pool", lambda e: e.tensor_tensor(t4[:, :, 1, :], x4[:, :, 1, :], sinb, ALU.mult), r=RR, w=["rt1"])
        S.op("pool", lambda e: e.tensor_tensor(o4[:, :, 0, :], t4[:, :, 0, :], t4[:, :, 1, :], ALU.subtract),
             r=["rt0", "rt1"], w=["rqk0"])
        S.op("dve", lambda e: e.tensor_tensor(x4[:, :, 0, :], x4[:, :, 0, :], sinb, ALU.mult), r=RR + ["rt0"], w=["r32a2"])
        S.op("dve", lambda e: e.tensor_tensor(x4[:, :, 1, :], x4[:, :, 1, :], cosb, ALU.mult), r=RR + ["rt1", "r32a2"], w=["r32b2"])
        S.op("dve", lambda e: e.tensor_tensor(o4[:, :, 1, :], x4[:, :, 0, :], x4[:, :, 1, :], ALU.add),
             r=["r32a2", "r32b2"], w=["rqk1"])
        pi = proj(2328, 2840)
        S.op("act", lambda e, pi=pi: e.copy(rvb[:], pA[pi][:]), r=[f"pA{pi}"], w=["rvb"])
        pi = proj(2840, INC)
        S.op("act", lambda e, pi=pi: e.activation(srg[:], pA[pi][:], AF.Exp, scale=-1.0), r=[f"pA{pi}"], w=["srg"])
        S.op("dve", lambda e: e.tensor_scalar(srg[:], srg[:], 1.0, None, ALU.add), r=["srg"], w=["srg"])
        S.op("dve", lambda e: e.reciprocal(srg[:], srg[:]), r=["srg"], w=["srg"])
        S.op("dve", lambda e, pi=pi: e.tensor_tensor(srg[:], srg[:], pA[pi][:], ALU.mult),
             r=["srg", f"pA{pi}"], w=["srg"])

        RQK = ["rqk0", "rqk1"]
        for half, dstT, nm in ((0, rqT, "rqT"), (1, rkT, "rkT")):
            pi2 = next_pt()
            for h in range(4):
                S.op("pe", lambda e, pi2=pi2, h=h, half=half: e.transpose(
                    pT[pi2][:, h * 128:(h + 1) * 128], rqk[:, half * 512 + h * 128: half * 512 + (h + 1) * 128], identb[:]),
                    r=RQK + ["identb"], w=[f"pT{pi2}"])
            S.op("act", lambda e, pi2=pi2, dstT=dstT: e.copy(dstT[:], pT[pi2][:, 0:512]), r=[f"pT{pi2}"], w=[nm])
        if samp:
            stS = S.sb([P, DB, 512], F32, "stS")
            kmb = S.sb([P, 512], BF16, "kmb")
            for b in range(DB):
                S.dma("sp", stS[:, b, :].rearrange("p (h e) -> p h e", e=128), din["state_ret"][b].rearrange("h d e -> d h e"),
                      w=[f"stS{b}"], key="stS")
            S.op("pool", lambda e, kb=kb: e.tensor_copy(knewb[:], kv32[kb][:, 256:384]), r=[f"kv32{kb}"], w=["knewb"])
            S.op("pool", lambda e, kb=kb: e.tensor_copy(Vn[:], kv32[kb][:, 384:512].rearrange("p (g d) -> p g d", d=64)),
                 r=[f"kv32{kb}"], w=["Vn"])
            ohrow = S.sb([P, DB, 128], F32, "ohrow")
            stSb = S.sb([P, DB, 512], BF16, "stSb")
            qmk = S.sb([P, 512], BF16, "qmk")
            qk4 = S.sb([P, 8], F32, "qk4")
            for b in range(DB):
                S.dma("sp", ohrow[:, b, :], din["ident"][b:b + 1, :].partition_broadcast(P), w=["ohrow"], key="ohrow")
                S.op("pool", lambda e, b=b: e.tensor_copy(stSb[:, b, :], stS[:, b, :]), r=[f"stS{b}"], w=[f"stSb{b}"])
            for b in range(DB):
                S.op("dve", lambda e, b=b: e.tensor_tensor(
                    qmk[:].rearrange("p (h t) -> p h t", t=128), rqT[:].rearrange("p (h t) -> p h t", t=128),
                    ohrow[:, b, :].unsqueeze(1).broadcast_to([P, 4, 128]), ALU.mult), r=["rqT", "ohrow"], w=["qmk"])
                for h in range(4):
                    hs = slice(h * 128, (h + 1) * 128)
                    S.op("pe", lambda e, hs=hs, b=b: e.matmul(pA[b % 4][:, hs] if False else pR[0][:, hs], qmk[:, hs], stSb[:, b, hs],
                                                              start=True, stop=True), r=["qmk", f"stSb{b}"], w=["pR0"])
                if b == 0:
                    S.op("dve", lambda e: e.tensor_copy(o32[:], pR[0][:]), r=["pR0"], w=["o32"])
                else:
                    S.op("dve", lambda e: e.tensor_tensor(o32[:], o32[:], pR[0][:], ALU.add), r=["pR0", "o32"], w=["o32"])
            S.op("dve", lambda e: e.tensor_tensor(
                o32[:].rearrange("p (h d) -> p h d", d=128), o32[:].rearrange("p (h d) -> p h d", d=128),
                rmisc[:, 8:12].unsqueeze(2).broadcast_to([P, 4, 128]), ALU.mult), r=["o32", "rmisc"], w=["o32"])
            S.op("dve", lambda e: e.tensor_tensor(rtmp[:, 0:512], rqk[:, 0:512], rqk[:, 512:1024], ALU.mult), r=RQK, w=["rtq"])
            S.op("dve", lambda e: e.tensor_reduce(qk4[:, 0:4], rtmp[:, 0:512].rearrange("p (h d) -> p h d", d=128), AX.X, ALU.add),
                 r=["rtq"], w=["qk4"])
            S.op("dve", lambda e: e.tensor_tensor(
                rtmp[:, 0:512].rearrange("p (h d) -> p h d", d=128), rvb[:].rearrange("p (h d) -> p h d", d=128),
                qk4[:, 0:4].unsqueeze(2).broadcast_to([P, 4, 128]), ALU.mult), r=["rvb", "qk4", "rtq"], w=["rtq"])
            S.op("dve", lambda e: e.tensor_tensor(o32[:], o32[:], rtmp[:, 0:512], ALU.add), r=["o32", "rtq"], w=["o32"])
            for b in range(DB):
                S.op("dve", lambda e, b=b: e.tensor_scalar(kmb[:], rqk[:, 512:1024], ident[:, b:b + 1], None, ALU.mult),
                     r=RQK + ["ident"], w=["kmb"])
                for h in range(4):
                    hs = slice(h * 128, (h + 1) * 128)
                    S.op("pe", lambda e, hs=hs: e.matmul(pR[1][:, hs], kmb[:, hs], rvb[:, hs], start=True, stop=True),
                         r=["kmb", "rvb"], w=["pR1"])
                S.op("dve", lambda e, b=b: e.tensor_tensor(
                    stS[:, b, :].rearrange("p (h d) -> p h d", d=128), stS[:, b, :].rearrange("p (h d) -> p h d", d=128),
                    rmisc[:, 8:12].unsqueeze(2).broadcast_to([P, 4, 128]), ALU.mult), r=[f"stS{b}", "rmisc"], w=[f"stS{b}"])
                S.op("dve", lambda e, b=b: e.tensor_tensor(stS[:, b, :], stS[:, b, :], pR[1][:], ALU.add),
                     r=[f"stS{b}", "pR1"], w=[f"stS{b}"])
                S.dma("pool", dout["s_ret"][b].rearrange("h d e -> d h e"), stS[:, b, :].rearrange("p (h e) -> p h e", e=128),
                      r=[f"stS{b}"], key="oret")
            osrc, OSRC = o32, "o32"
        else:
            osrc, OSRC = pR[0], "pR0"
        if not samp:
            S.op("pool", lambda e: e.tensor_tensor(
                kdec[:].rearrange("p (h d) -> p h d", d=128), rqk[:, 512:1024].rearrange("p (h d) -> p h d", d=128),
                rmisc[:, 0:4].unsqueeze(2).broadcast_to([P, 4, 128]), ALU.mult), r=RQK + ["rmisc"], w=["kdec"])
            S.op("pool", lambda e: e.tensor_tensor(qsT[:], rqT[:], cq[:], ALU.mult), r=["rqT", "cq"], w=["qsT"])
            pa = next_pa()
            for h in range(4):
                S.op("pe", lambda e, pa=pa, h=h: e.matmul(pA[pa][:, h * 128:(h + 1) * 128], rkT[:, h * 128:(h + 1) * 128],
                                                           rqT[:, h * 128:(h + 1) * 128], start=True, stop=True),
                     r=["rkT", "rqT"], w=[f"pA{pa}"])
            S.op("dve", lambda e, pa=pa: e.tensor_tensor(attm[:], pA[pa][:], decT[:], ALU.mult),
                 r=[f"pA{pa}", "decT"], w=["attm"])
            po = 0
            for h in range(4):
                hs = slice(h * 128, (h + 1) * 128)
                S.op("pe", lambda e, hs=hs: e.matmul(pR[0][:, hs], attm[:, hs], rvb[:, hs], start=True, stop=(t == 0)),
                     r=["attm", "rvb"], w=["pR0"])
                if t > 0:
                    S.op("pe", lambda e, hs=hs: e.matmul(pR[0][:, hs], qsT[:, hs], Sbf[:, hs], start=False, stop=True),
                         r=["qsT", "Sbf"], w=["pR0"])
            for h in range(4):
                hs = slice(h * 128, (h + 1) * 128)
                S.op("pe", lambda e, hs=hs: e.matmul(pR[1][:, hs], kdec[:, hs], rvb[:, hs], start=True, stop=True),
                     r=["kdec", "rvb"], w=["pR1"])
            if t == 0:
                S.op("dve", lambda e: e.tensor_copy(Sst[:], pR[1][:]), r=["pR1"], w=["Sst"])
            else:
                S.op("dve", lambda e: e.tensor_tensor(
                    Sst[:].rearrange("p (h d) -> p h d", d=128), Sst[:].rearrange("p (h d) -> p h d", d=128),
                    rmisc[:, 4:8].unsqueeze(2).broadcast_to([P, 4, 128]), ALU.mult), r=["Sst", "rmisc"], w=["Sst"])
                S.op("dve", lambda e: e.tensor_tensor(Sst[:], Sst[:], pR[1][:], ALU.add), r=["Sst", "pR1"], w=["Sst"])
            S.op("pool", lambda e: e.tensor_copy(Sbf[:], Sst[:]), r=["Sst"], w=["Sbf"])
            if t == NT - 1:
                S.dma("pool", dout["p_ret"].rearrange("h d e -> d h e"), Sst[:].rearrange("p (h e) -> p h e", e=128),
                      r=["Sst"], key="oret")

        S.op("act", lambda e: e.activation(sq[:], osrc[:], AF.Square), r=[OSRC], w=["sq"])
        S.op("dve", lambda e: e.tensor_reduce(st[:, 14:18], sq[:].rearrange("p (h d) -> p h d", d=128), AX.X, ALU.add),
             r=["sq"], w=["st14"])
        S.op("dve", lambda e: e.tensor_reduce(st[:, 18:22], osrc[:].rearrange("p (h d) -> p h d", d=128), AX.X, ALU.add),
             r=[OSRC], w=["st18"])
        S.op("dve", lambda e: e.tensor_scalar(st[:, 18:22], st[:, 18:22], 1.0 / 128.0, None, ALU.mult), r=["st18"], w=["st18"])
        S.op("dve", lambda e: e.tensor_tensor(st[:, 22:26], st[:, 18:22], st[:, 18:22], ALU.mult), r=["st18"], w=["st22"])
        S.op("dve", lambda e: e.scalar_tensor_tensor(st[:, 14:18], st[:, 14:18], 1.0 / 128.0, st[:, 22:26], ALU.mult, ALU.subtract),
             r=["st14", "st22"], w=["st14"])
        rsqrt_cols(14, 18, "st14")
        S.op("dve", lambda e: e.scalar_tensor_tensor(st[:, 22:26], st[:, 18:22], -1.0, st[:, 14:18], ALU.mult, ALU.mult),
             r=["st18", "st14"], w=["st22"])
        S.op("dve", lambda e: e.tensor_tensor(
            onrm[:].rearrange("p (h d) -> p h d", d=128), osrc[:].rearrange("p (h d) -> p h d", d=128),
            st[:, 14:18].unsqueeze(2).broadcast_to([P, 4, 128]), ALU.mult), r=[OSRC, "st14"], w=["onrm"])
        S.op("dve", lambda e: e.tensor_tensor(
            onrm[:].rearrange("p (h d) -> p h d", d=128), onrm[:].rearrange("p (h d) -> p h d", d=128),
            st[:, 22:26].unsqueeze(2).broadcast_to([P, 4, 128]), ALU.add), r=["onrm", "st22"], w=["onrm"])
        S.op("pool", lambda e: e.tensor_tensor(retb[:], onrm[:], srg[:], ALU.mult), r=["onrm", "srg"], w=["retb"])
        pi2 = next_pt()
        for h in range(4):
            S.op("pe", lambda e, pi2=pi2, h=h: e.transpose(pT[pi2][:, h * 128:(h + 1) * 128], retb[:, h * 128:(h + 1) * 128], identb[:]),
                 r=["retb", "identb"], w=[f"pT{pi2}"])
        for h in range(4):
            S.op("act", lambda e, pi2=pi2, h=h, t=t: e.copy(catT[:, 4 + h, t * P:(t + 1) * P], pT[pi2][:, h * 128:(h + 1) * 128]),
                 r=[f"pT{pi2}"], w=[f"catT{t}"])

    S.barrier()
    S.release(mP2)
    rbb = S.sb([32, 8], BF16, "rbb")
    rb32 = S.sb([32, 8], F32, "rb32")
    Jb = S.sb([P, 128], BF16, "Jb")
    LT = S.sb([P, 128], BF16, "LT")
    Eoh = S.sb([32, 2048], BF16, "Eoh")
    selA = S.sb([P, 256], F32, "selA")
    selB = S.sb([P, 256], F32, "selB")
    Wimp = S.sb([P, 32], BF16, "Wimp")
    Wtab = [S.sb([P, 4, 4, 128], BF16, f"Wtab{g}") for g in range(2)]
    WcT = [S.sb([P, 16, 4, 128], BF16, f"WcT{g}") for g in range(2)]
    selT = [S.sb([32, 8, 128], BF16, f"selT{g}") for g in range(2)]
    for nm, tl in (("Jb", Jb), ("LT", LT), ("Eoh", Eoh), ("selA", selA), ("selB", selB), ("Wimp", Wimp)):
        S.dma("sp", tl[:], din[nm], w=[nm], key="cB")
    S.dma("sp", rb32[:], din["rel_bias"], w=["rb32"], key="cB")
    S.op("dve", lambda e: e.tensor_copy(rbb[:], rb32[:]), r=["rb32"], w=["rbb"])

    mB = S.mark()
    ohA = S.sb([32, 4111], BF16, "ohA")
    S.dma("sp", ohA[:], din["ohA"], w=["ohA"], key="cB")
    tb32 = S.sb([8, 4111], F32, "tb32")
    vA = S.sb([8, 4111], F32, "vA")
    S.dma("sp", vA[:], din["validA"], w=["vA"], key="cB")
    for ci in range(9):
        c0 = ci * 512
        c1 = min(4111, c0 + 512)
        pa = next_pa()
        S.op("pe", lambda e, pa=pa, c0=c0, c1=c1: e.matmul(pA[pa][0:8, 0:c1 - c0], rbb[:, :], ohA[:, c0:c1], start=True, stop=True),
             r=["rbb", "ohA"], w=[f"pA{pa}"])
        S.op("act", lambda e, pa=pa, c0=c0, c1=c1: e.activation(tb32[:, c0:c1], pA[pa][0:8, 0:c1 - c0], AF.Exp),
             r=[f"pA{pa}"], w=["tb32"])
    S.op("dve", lambda e: e.tensor_tensor(tb32[:], tb32[:], vA[:], ALU.mult), r=["tb32", "vA"], w=["tb32"])
    S.dma("sp", tblA_h.ap(), tb32[:], r=["tb32"], w=["tblA"], key="tblA")
    Hs = S.sb([P, 640], F32, "Hs")
    Hsb = S.sb([P, 640], BF16, "Hsb")
    Hc = S.sb([P, 2048], F32, "Hc")
    Hcb = S.sb([P, 2048], BF16, "Hcb")
    for h in range(8):
        g, j = h // 4, h % 4
        S.dma("sp", Hs[:], bass.AP(tblA_h, h * 4111 + 1936, [[1, 128], [1, 640]]), r=["tblA"], w=["Hs"], key="Hs")
        S.dma("sp", Hc[:].rearrange("p (a b) -> p a b", b=128), bass.AP(tblA_h, h * 4111, [[16, 128], [128, 16], [1, 128]]),
              r=["tblA"], w=["Hc"], key="Hc")
        S.op("dve", lambda e: e.tensor_copy(Hsb[:], Hs[:]), r=["Hs"], w=["Hsb"])
        S.op("pool", lambda e: e.tensor_copy(Hcb[:], Hc[:]), r=["Hc"], w=["Hcb"])
        pa = next_pa()
        S.op("pe", lambda e, pa=pa: e.matmul(pA[pa][:, 0:384], Jb[:], Hsb[:, 0:384], start=True, stop=True),
             r=["Jb", "Hsb"], w=[f"pA{pa}"])
        S.op("act", lambda e, pa=pa, g=g, j=j: e.copy(Wtab[g][:, 0:3, j, :], pA[pa][:, 0:384].rearrange("p (c q) -> p c q", q=128)),
             r=[f"pA{pa}"], w=[f"Wtab{g}"])
        S.op("dve", lambda e, g=g, j=j: e.tensor_tensor(Wtab[g][:, 3, j, :], Wtab[g][:, 2, j, :], LT[:], ALU.mult),
             r=[f"Wtab{g}", "LT"], w=[f"Wtab{g}"])
        for q4 in range(4):
            pa = next_pa()
            S.op("pe", lambda e, pa=pa, q4=q4: e.matmul(pA[pa][:], Jb[:], Hcb[:, q4 * 512:(q4 + 1) * 512], start=True, stop=True),
                 r=["Jb", "Hcb"], w=[f"pA{pa}"])
            S.op("act", lambda e, pa=pa, g=g, j=j, q4=q4: e.copy(WcT[g][:, q4 * 4:(q4 + 1) * 4, j, :], pA[pa][:].rearrange("p (a q) -> p a q", q=128)),
                 r=[f"pA{pa}"], w=[f"WcT{g}"])
    S.barrier()
    S.release(mB)
    ckcT = S.sb([P, 128], BF16, "ckcT")
    cvc = S.sb([P, 2, 65], BF16, "cvc")
    Pbuf = S.sb([P, 16, 4, 128], BF16, "Pbuf")
    ebuf = [S.sb([P, 512], BF16, f"ebuf{i}") for i in range(2)]
    PcT = S.sb([P, 4, 128], BF16, "PcT")
    wmb = S.sb([P, 4, 128], BF16, "wmb")
    obuf = S.sb([P, 3, 4, 65], F32, "obuf")
    bst = S.sb([P, 64], F32, "bst")
    prod = S.sb([P, 3, 4, 64], F32, "prod")
    nsa32 = S.sb([P, 512], F32, "nsa32")
    nsab = S.sb([P, 512], BF16, "nsab")
    tU = S.sb([P, 4, 32], F32, "tU")
    sc = S.sb([P, 32], F32, "sc")
    sc2 = S.sb([P, 32], F32, "sc2")
    m8 = S.sb([P, 16], F32, "m8")
    mselb = S.sb([P, 32], BF16, "mselb")
    w1s = S.sb([P, 32, 128], F32, "w1s")
    w1b = S.sb([P, 32, 128], BF16, "w1b")
    w2s = S.sb([P, 64], F32, "w2s")
    w2b = S.sb([P, 64], BF16, "w2b")
    pes = S.sb([64, 32], F32, "pes")
    peb = S.sb([64, 32], BF16, "peb")
    hidb = S.sb([P, 2], F32, "hidb")
    actc = S.sb([P, 128], BF16, "actc")
    cmt = S.sb([P, 128], F32, "cmt")
    ckc32 = S.sb([P, 128], F32, "ckc32")
    ckcb = S.sb([P, 128], BF16, "ckcb")
    S.op("pool", lambda e: e.memset(ckcb[:], 0.0), w=["ckcb"])
    S.op("pool", lambda e: e.memset(cvc[:], 0.0), w=["cvc"])
    S.op("pool", lambda e: e.memset(cvc[:, :, 64:65], 1.0), w=["cvc"])
    for kind, (srcT, SRC) in enumerate(((kcT, "kcT"), (vcT, "vcT"))):
        sfx = "k" if kind == 0 else "v"
        for half in range(2):
            S.dma("sp", w1s[half * 64:(half + 1) * 64, :, :], din["cmp_w1_" + sfx].rearrange("(s d) h -> d s h", d=64),
                  w=["w1s"], key="w1s")
        S.op("dve", lambda e: e.tensor_copy(w1b[:], w1s[:]), r=["w1s"], w=["w1b"])
        S.dma("sp", w2s[:], din["cmp_w2_" + sfx], w=["w2s"], key="w1s")
        S.op("dve", lambda e: e.tensor_copy(w2b[:], w2s[:]), r=["w2s"], w=["w2b"])
        S.dma("sp", pes[:], din["cmp_pe_" + sfx].rearrange("s d -> d s"), w=["pes"], key="w1s", allow_slow_non_contiguous=True)
        S.op("dve", lambda e: e.tensor_copy(peb[:], pes[:]), r=["pes"], w=["peb"])
        pa = next_pa()
        for s in range(32):
            S.op("pe", lambda e, pa=pa, s=s: e.matmul(pA[pa][:, 0:1], w1b[0:64, s, :], peb[:, s:s + 1], start=(s == 0), stop=(s == 31)),
                 r=["w1b", "peb"], w=[f"pA{pa}"])
        S.op("dve", lambda e, pa=pa: e.tensor_copy(hidb[:, 0:1], pA[pa][:, 0:1]), r=[f"pA{pa}"], w=["hidb"])
        S.op("dve", lambda e: e.tensor_scalar(hidb[:, 1:2], hidb[:, 0:1], -1.0, None, ALU.mult), r=["hidb"], w=["hidb"])
        ALLT = [f"{SRC}{t}" for t in range(NT)]
        for g in range(2):
            gp = slice(g * 64, (g + 1) * 64)
            pa = next_pa()
            for s in range(32):
                S.op("pe", lambda e, pa=pa, s=s, gp=gp: e.matmul(pA[pa][:, 0:127], w1b[gp, s, :], srcT[gp, s:s + 2017:16],
                                                              start=(s == 0), stop=(s == 31)), r=ALLT + ["w1b"], w=[f"pA{pa}"])
            S.op("act", lambda e, pa=pa: e.activation(cmt[:, 0:127], pA[pa][:, 0:127], AF.Exp, scale=-1.0, bias=hidb[:, 1:2]),
                 r=[f"pA{pa}", "hidb"], w=["cmt"])
            S.op("dve", lambda e: e.tensor_scalar(cmt[:, 0:127], cmt[:, 0:127], 1.0, None, ALU.add), r=["cmt"], w=["cmt"])
            S.op("dve", lambda e: e.reciprocal(cmt[:, 0:127], cmt[:, 0:127]), r=["cmt"], w=["cmt"])
            S.op("dve", lambda e, pa=pa: e.scalar_tensor_tensor(actc[:, 0:127], pA[pa][:, 0:127], hidb[:, 0:1], cmt[:, 0:127], ALU.add, ALU.mult),
                 r=[f"pA{pa}", "hidb", "cmt"], w=["actc"])
            p2 = next_pa()
            S.op("pe", lambda e, p2=p2: e.matmul(pA[p2][0:127, 0:64], actc[:, 0:127], w2b[:], start=True, stop=True),
                 r=["actc", "w2b"], w=[f"pA{p2}"])
            if kind == 0:
                S.op("act", lambda e, p2=p2: e.activation(cmt[0:127, 0:64], pA[p2][0:127, 0:64], AF.Square, scale=0.125, accum_out=bst[0:127, 0:1]),
                     r=[f"pA{p2}"], w=["cmt", "bst0"])
                S.op("act", lambda e: e.activation(bst[0:127, 0:1], bst[0:127, 0:1], AF.Ln, bias=epsc[0:127, 0:1]), r=["bst0", "epsc"], w=["bst0"])
                S.op("act", lambda e: e.activation(bst[0:127, 0:1], bst[0:127, 0:1], AF.Exp, scale=-0.5), r=["bst0"], w=["bst0"])
                S.op("dve", lambda e, p2=p2: e.tensor_scalar(ckc32[0:127, 0:64], pA[p2][0:127, 0:64], bst[0:127, 0:1], None, ALU.mult),
                     r=[f"pA{p2}", "bst0"], w=["ckc32"])
                S.op("dve", lambda e, g=g: e.tensor_tensor(ckcb[0:127, g * 64:(g + 1) * 64], ckc32[0:127, 0:64], knw[0:127, :], ALU.mult),
                     r=["ckc32", "knw"], w=["ckcb"])
            else:
                S.op("act", lambda e, p2=p2, g=g: e.copy(cvc[0:127, g, 0:64], pA[p2][0:127, 0:64]), r=[f"pA{p2}"], w=["cvc"])
        if kind == 0:
            pi = next_pt()
            S.op("pe", lambda e, pi=pi: e.transpose(pT[pi][:, 0:128], ckcb[:], identb[:]), r=["ckcb", "identb"], w=[f"pT{pi}"])
            S.op("act", lambda e, pi=pi: e.copy(ckcT[:], pT[pi][:, 0:128]), r=[f"pT{pi}"], w=["ckcT"])

    ebi = [0]
    Oc = pA[3]
    for qt in range(NT):
        for g in range(2):
            gp = slice(g * 64, (g + 1) * 64)
            qrhs = QT[gp, qt, :, :].rearrange("p a b -> p (a b)")

            def score_tile(lhsT_ap, rtoks):
                ps = ebi[0] % 2
                S.op("pe", lambda e, ps=ps: e.matmul(pA[ps][:], lhsT_ap, qrhs, start=True, stop=True),
                     r=rtoks + [f"QT{qt}"], w=[f"pA{ps}"])
                eb = ebi[0] % 2
                ebi[0] += 1
                S.op("act", lambda e, ps=ps, eb=eb: e.activation(ebuf[eb][:], pA[ps][:], AF.Exp), r=[f"pA{ps}"], w=[f"ebuf{eb}"])
                return eb

            def pv(obank, oc0, kts, Vt, VTOK, br):
                for j in range(4):
                    for i, kt in enumerate(kts):
                        S.op("pe", lambda e, j=j, kt=kt, i=i: e.matmul(
                            obank[:, oc0 + j * 65: oc0 + (j + 1) * 65], Pbuf[:, kt, j, :], Vt[:, kt, g, :],
                            start=(i == 0), stop=(i == len(kts) - 1)), r=[f"Pb{kt}", f"{VTOK}{kt}"], w=[f"O{br}"])

            eb = score_tile(ckcT[gp, :], ["ckcT"])
            S.op("dve", lambda e, eb=eb: e.tensor_tensor(PcT[:].rearrange("p a b -> p (a b)"), ebuf[eb][:],
                                                          WcT[g][:, qt, :, :].rearrange("p a b -> p (a b)"), ALU.mult),
                 r=[f"ebuf{eb}", f"WcT{g}"], w=["PcT"])
            for j in range(4):
                S.op("pe", lambda e, j=j: e.matmul(Oc[:, j * 65:(j + 1) * 65], PcT[:, j, :], cvc[:, g, :], start=True, stop=True),
                     r=["PcT", "cvc"], w=["O0"])
            S.op("act", lambda e: e.copy(obuf[:, 0, :, :].rearrange("p a b -> p (a b)"), Oc[:, 0:260]), r=["O0"], w=["obuf0"])
            if qt >= 8:
                q8 = qt - 8
                for j in range(4):
                    S.op("pe", lambda e, j=j: e.matmul(Oc[:, 260 + j * 32: 260 + (j + 1) * 32], PcT[:, j, :], Wimp[:], start=True, stop=True),
                         r=["PcT", "Wimp"], w=["OU"])
                S.op("dve", lambda e: e.tensor_scalar(bst[:, 4:8], obuf[:, 0, :, 64], 1e-30, None, ALU.max), r=["obuf0"], w=["bst4"])
                S.op("dve", lambda e: e.reciprocal(bst[:, 4:8], bst[:, 4:8]), r=["bst4"], w=["bst4"])
                S.op("dve", lambda e: e.tensor_tensor(tU[:], Oc[:, 260:388].rearrange("p (j b) -> p j b", b=32),
                                                       bst[:, 4:8].unsqueeze(2).broadcast_to([P, 4, 32]), ALU.mult),
                     r=["OU", "bst4"], w=["tU"])
                S.op("dve", lambda e: e.tensor_reduce(sc[:], tU[:].rearrange("p j b -> p b j"), AX.X, ALU.add), r=["tU"], w=["sc"])
                S.op("dve", lambda e, q8=q8: e.tensor_tensor(sc[:], sc[:], selA[:, q8 * 32:(q8 + 1) * 32], ALU.mult), r=["sc", "selA"], w=["sc"])
                S.op("dve", lambda e, q8=q8: e.tensor_tensor(sc[:], sc[:], selB[:, q8 * 32:(q8 + 1) * 32], ALU.add), r=["sc", "selB"], w=["sc"])
                S.op("dve", lambda e: e.max(m8[:, 0:8], sc[:]), r=["sc"], w=["m8a"])
                S.op("dve", lambda e: e.match_replace(sc2[:], m8[:, 0:8], sc[:], -2.0), r=["sc", "m8a"], w=["sc2"])
                S.op("dve", lambda e: e.max(m8[:, 8:16], sc2[:]), r=["sc2"], w=["m8b"])
                S.op("dve", lambda e: e.tensor_scalar(mselb[:], sc[:], m8[:, 15:16], None, ALU.is_ge), r=["sc", "m8b"], w=["mselb"])
                pi = next_pt()
                S.op("pe", lambda e, pi=pi: e.transpose(pT[pi][0:32, 0:128], mselb[:], identb[:]), r=["mselb", "identb"], w=[f"pT{pi}"])
                S.op("act", lambda e, pi=pi, q8=q8: e.copy(selT[g][:, q8, :], pT[pi][0:32, 0:128]), r=[f"pT{pi}"], w=[f"selT{g}"])
            kts = list(range(qt + 1))
            for kt in kts:
                eb = score_tile(KsT[gp, kt * P:(kt + 1) * P], [f"KsT{kt}"])
                cls = min(qt - kt, 2)
                wt = Wtab[g][:, cls, :, :]
                if qt >= 8:
                    S.op("pe", lambda e, kt=kt, q8=qt - 8: e.matmul(pA[2][:, 0:128], Eoh[:, kt * P:(kt + 1) * P], selT[g][:, q8, :], start=True, stop=True),
                         r=["Eoh", f"selT{g}"], w=["pA2"])
                    S.op("dve", lambda e, wt=wt: e.tensor_tensor(wmb[:], wt, pA[2][:, 0:128].unsqueeze(1).broadcast_to([P, 4, 128]), ALU.mult),
                         r=["pA2", f"Wtab{g}"], w=["wmb"])
                    S.op("dve", lambda e, eb=eb, kt=kt: e.tensor_tensor(Pbuf[:, kt, :, :].rearrange("p a b -> p (a b)"), ebuf[eb][:],
                                                                     wmb[:].rearrange("p a b -> p (a b)"), ALU.mult),
                         r=[f"ebuf{eb}", "wmb"], w=[f"Pb{kt}"])
                else:
                    S.op("dve", lambda e, eb=eb, kt=kt, wt=wt: e.tensor_tensor(Pbuf[:, kt, :, :], ebuf[eb][:].rearrange("p (a b) -> p a b", b=128),
                                                                            wt, ALU.mult),
                         r=[f"ebuf{eb}", f"Wtab{g}"], w=[f"Pb{kt}"])
            pv(pR[0], 0, kts, Vs, "Vs", 1)
            S.op("act", lambda e: e.copy(obuf[:, 1, :, :].rearrange("p a b -> p (a b)"), pR[0][:, 0:260]), r=["O1"], w=["obuf1"])
            kts = list(range(max(0, qt - 4), qt + 1))
            for kt in kts:
                eb = score_tile(KwT[gp, kt * P:(kt + 1) * P], [f"KwT{kt}"])
                cls = {0: 0, 1: 1, 2: 2, 3: 2, 4: 3}[qt - kt]
                wt = Wtab[g][:, cls, :, :]
                S.op("dve", lambda e, eb=eb, kt=kt, wt=wt: e.tensor_tensor(Pbuf[:, kt, :, :], ebuf[eb][:].rearrange("p (a b) -> p a b", b=128),
                                                                        wt, ALU.mult),
                     r=[f"ebuf{eb}", f"Wtab{g}"], w=[f"Pb{kt}"])
            pv(pR[1], 0, kts, Vw, "Vw", 2)
            S.op("act", lambda e: e.copy(obuf[:, 2, :, :].rearrange("p a b -> p (a b)"), pR[1][:, 0:260]), r=["O2"], w=["obuf2"])
            OB = ["obuf0", "obuf1", "obuf2"]
            S.op("dve", lambda e: e.tensor_scalar(bst[:, 8:20].rearrange("p (a b) -> p a b", b=4), obuf[:, :, :, 64], 1e-30, None, ALU.max),
                 r=OB, w=["bst8"])
            S.op("dve", lambda e: e.reciprocal(bst[:, 8:20], bst[:, 8:20]), r=["bst8"], w=["bst8"])
            S.op("dve", lambda e: e.tensor_tensor(bst[:, 8:20].rearrange("p (a b) -> p a b", b=4), bst[:, 8:20].rearrange("p (a b) -> p a b", b=4),
                                                   gates[:, qt, :].rearrange("p (a h) -> p a h", h=8)[:, :, 4 * g:4 * g + 4], ALU.mult),
                 r=["bst8", f"gates{qt}"], w=["bst8"])
            S.op("dve", lambda e: e.tensor_tensor(prod[:], obuf[:, :, :, 0:64],
                                                   bst[:, 8:20].rearrange("p (a b) -> p a b", b=4).unsqueeze(3).broadcast_to([P, 3, 4, 64]), ALU.mult),
                 r=OB + ["bst8"], w=["prod"])
            S.op("dve", lambda e: e.tensor_reduce(nsa32[:, g * 256:(g + 1) * 256].rearrange("p (j d) -> p j d", d=64),
                                                   prod[:].rearrange("p a j d -> p j d a"), AX.X, ALU.add),
                 r=["prod"], w=[f"nsa32_{g}"])
        S.op("act", lambda e: e.copy(nsab[:], nsa32[:]), r=["nsa32_0", "nsa32_1"], w=["nsab"])
        pi = next_pt()
        for c4 in range(4):
            S.op("pe", lambda e, pi=pi, c4=c4: e.transpose(pT[pi][:, c4 * 128:(c4 + 1) * 128], nsab[:, c4 * 128:(c4 + 1) * 128], identb[:]),
                 r=["nsab", "identb"], w=[f"pT{pi}"])
        S.op("act", lambda e, pi=pi, qt=qt: e.copy(catT[:, 0:4, qt * P:(qt + 1) * P], pT[pi][:, 0:512].rearrange("p (a b) -> p a b", b=128)),
             r=[f"pT{pi}"], w=[f"catT{qt}"])

    class _StopC(Exception):
        pass
    KC = int(os.environ.get("KC", "9"))
    try:
        if not ENABLE_DECODE_NSA:
            for c4 in range(4):
                S.op("pool", lambda e, c4=c4: e.memset(catT[:, c4, SEQ:SEQ + P], 0.0), w=[f"catT{NT}"])
            raise _StopC()
        S.barrier()
        S.release(mP2)
        pt4 = S.sb([P, DB], I32, "pt4")
        idx8 = S.sb([P, DB, 8], I32, "idx8")
        rbb2 = S.sb([32, 8], BF16, "rbb2")
        Hd2 = S.sb([P, 8], F32, "Hd2")
        Hd2b = S.sb([P, 8], BF16, "Hd2b")
        w0b = S.sb([P, 8], F32, "w0b")
        SelC = S.sb([P, 8, 128], BF16, "SelC")
        SelW = S.sb([P, 4, 128], BF16, "SelW")
        OHt = S.sb([P, 2, 128], F32, "OHt")
        LAB = S.sb([P, 2, 128], BF16, "LAB")
        A2 = S.sb([8, 256], F32, "A2")
        B2 = S.sb([8, 256], F32, "B2")
        Wr = S.sb([P, 2, 8], F32, "Wr")
        onesb = S.sb([P, 128], BF16, "onesb")
        Rm = S.sb([P, 2, 128, 8], BF16, "Rm")
        EBs = S.sb([P, 128, 8], BF16, "EBs")
        EBc = S.sb([P, 8, 8], F32, "EBc")
        EBw = S.sb([P, 4, 8], F32, "EBw")
        EBn = S.sb([P, DB, 8], F32, "EBn")
        for nm, tl in (("SelC", SelC), ("SelW", SelW), ("OHt", OHt), ("LAB", LAB), ("A2", A2), ("B2", B2), ("Wr", Wr)):
            S.dma("sp", tl[:] if nm in ("A2", "B2") else tl[:].rearrange("p a b -> p (a b)"), din[nm], w=[nm], key="cC")
        S.dma("sp", pt4[:], din["page_table"].rearrange("b p -> p b"), w=["pt4"], key="cC", allow_slow_non_contiguous=True)
        S.dma("sp", Hd2[:], bass.AP(tblA_h, 2063, [[1, 128], [4111, 8]]), r=["tblA"], w=["Hd2"], key="cC", allow_slow_non_contiguous=True)
        S.dma("sp", w0b[:], bass.AP(tblA_h, 2063, [[0, 128], [4111, 8]]), r=["tblA"], w=["w0b"], key="cC", allow_slow_non_contiguous=True)
        S.op("dve", lambda e: e.tensor_copy(Hd2b[:], Hd2[:]), r=["Hd2"], w=["Hd2b"])
        S.op("pool", lambda e: e.memset(onesb[:], 1.0), w=["onesb"])
        for r in range(8):
            S.op("dve", lambda e, r=r: e.tensor_scalar(idx8[:, :, r], pt4[:], 8, r, ALU.mult, ALU.add), r=["pt4"], w=["idx8"])
        pa = next_pa()
        for r in range(8):
            S.op("pe", lambda e, pa=pa, r=r: e.matmul(pA[pa][:, r * 8:(r + 1) * 8], SelC[:, r, :], Hd2b[:], start=True, stop=True),
                 r=["SelC", "Hd2b"], w=[f"pA{pa}"])
        S.op("dve", lambda e, pa=pa: e.tensor_copy(EBc[:].rearrange("p a b -> p (a b)"), pA[pa][:, 0:64]), r=[f"pA{pa}"], w=["EBc"])
        pa = next_pa()
        for u in range(4):
            S.op("pe", lambda e, pa=pa, u=u: e.matmul(pA[pa][:, u * 8:(u + 1) * 8], SelW[:, u, :], Hd2b[:], start=True, stop=True),
                 r=["SelW", "Hd2b"], w=[f"pA{pa}"])
        S.op("dve", lambda e, pa=pa: e.tensor_copy(EBw[:].rearrange("p a b -> p (a b)"), pA[pa][:, 0:32]), r=[f"pA{pa}"], w=["EBw"])
        for ab in range(2):
            S.op("dve", lambda e, ab=ab: e.tensor_tensor(Rm[:, ab, :, :], OHt[:, ab, :].unsqueeze(2).broadcast_to([P, 128, 8]),
                                                          Hd2[:].unsqueeze(1).broadcast_to([P, 128, 8]), ALU.mult),
                 r=["OHt", "Hd2"], w=["Rm"])
        for hf in range(2):
            pa = next_pa()
            for ab in range(2):
                S.op("pe", lambda e, pa=pa, ab=ab, hf=hf: e.matmul(
                    pA[pa][:], LAB[:, ab, :], Rm[:, ab, hf * 64:(hf + 1) * 64, :].rearrange("p a b -> p (a b)"),
                    start=(ab == 0), stop=(ab == 1)), r=["LAB", "Rm"], w=[f"pA{pa}"])
            S.op("act", lambda e, pa=pa, hf=hf: e.copy(EBs[:, hf * 64:(hf + 1) * 64, :].rearrange("p a b -> p (a b)"), pA[pa][:]),
                 r=[f"pA{pa}"], w=["EBs"])
        S.op("dve", lambda e: e.tensor_tensor(EBn[:], w0b[:].unsqueeze(1).broadcast_to([P, DB, 8]),
                                               ident[:, 0:DB].unsqueeze(2).broadcast_to([P, DB, 8]), ALU.mult),
             r=["w0b", "ident"], w=["EBn"])

        obd = S.sb([4, 3, DB, 2, 65], F32, "obd")
        S.op("pool", lambda e: e.memset(obd[:], 0.0), w=["obd"])
        mC = S.mark()
        if KC < 1:
            raise _StopC()
        Xg = [S.sb([P, 2048], BF16, f"Xg{i}") for i in range(4)]
        KT = S.sb([P, 128, 128], BF16, "KT")
        w1b = S.sb([P, 32, 128], BF16, "w1b")
        w1s = S.sb([P, 8, 128], F32, "w1s")
        w2s = S.sb([P, 64], F32, "w2s")
        w2b = S.sb([P, 64], BF16, "w2b")
        pes = S.sb([64, 32], F32, "pes")
        peb = S.sb([64, 32], BF16, "peb")
        hidb = S.sb([P, 2], F32, "hidb")
        actd = S.sb([P, 8, 128], BF16, "actd")
        cmt = S.sb([P, 512], F32, "cmt")
        cst = S.sb([P, 64], F32, "cst")
        ck32 = S.sb([P, 8, 2, 64], F32, "ck32")
        ckb = S.sb([P, 8, 2, 64], BF16, "ckb")
        ckcTd = S.sb([P, DB, 8, 128], BF16, "ckcTd")
        cvcd = S.sb([P, DB, 2, 8, 64], BF16, "cvcd")
        xgi = [0]
        pools = {"ck": din["cache_cmp_k"], "cv": din["cache_cmp_v"], "sk": din["cache_slc_k"], "sv": din["cache_slc_v"]}

        def gather(kind, b, r):
            i = xgi[0] % 4
            xgi[0] += 1
            srcp = pools[kind]
            S.dmafn("pool", lambda e, i=i, b=b, r=r, srcp=srcp: e.indirect_dma_start(
                out=Xg[i][:], out_offset=None, in_=srcp,
                in_offset=bass.IndirectOffsetOnAxis(ap=idx8[:, b, r:r + 1], axis=0)), r=["idx8"], w=[f"Xg{i}"], key=f"Xg{i}")
            return i

        evi = [0]

        def build_KT(kind, b):
            for r in range(8):
                i = gather(kind, b, r)
                for hf in range(2):
                    pi = next_pt()
                    for k in range(8):
                        tt = hf * 8 + k
                        S.op("pe", lambda e, pi=pi, k=k, tt=tt, i=i: e.transpose(pT[pi][:, k * 128:(k + 1) * 128], Xg[i][:, tt * 128:(tt + 1) * 128], identb[:]),
                             r=[f"Xg{i}", "identb"], w=[f"pT{pi}"])
                    eng = "act" if evi[0] % 2 == 0 else "dve"
                    evi[0] += 1
                    t0 = r * 16 + hf * 8
                    if eng == "act":
                        S.op("act", lambda e, pi=pi, t0=t0: e.copy(KT[:, t0:t0 + 8, :].rearrange("p a b -> p (a b)"), pT[pi][:, 0:1024]),
                             r=[f"pT{pi}"], w=[f"KT{r}"])
                    else:
                        S.op("dve", lambda e, pi=pi, t0=t0: e.tensor_copy(KT[:, t0:t0 + 8, :].rearrange("p a b -> p (a b)"), pT[pi][:, 0:1024]),
                             r=[f"pT{pi}"], w=[f"KT{r}"])

        for kind, sfx in (("ck", "k"), ("cv", "v")):
            for s8 in range(4):
                for half in range(2):
                    S.dma("sp", w1s[half * 64:(half + 1) * 64, :, :], din["cmp_w1_" + sfx].rearrange("(s d) h -> d s h", d=64)[:, s8 * 8:(s8 + 1) * 8, :],
                          w=["w1s"], key="w1s")
                S.op("dve", lambda e, s8=s8: e.tensor_copy(w1b[:, s8 * 8:(s8 + 1) * 8, :], w1s[:]), r=["w1s"], w=["w1b"])
            S.dma("sp", w2s[:], din["cmp_w2_" + sfx], w=["w2s"], key="w1s")
            S.op("dve", lambda e: e.tensor_copy(w2b[:], w2s[:]), r=["w2s"], w=["w2b"])
            S.dma("sp", pes[:], din["cmp_pe_" + sfx].rearrange("s d -> d s"), w=["pes"], key="w1s", allow_slow_non_contiguous=True)
            S.op("dve", lambda e: e.tensor_copy(peb[:], pes[:]), r=["pes"], w=["peb"])
            pa = next_pa()
            for s in range(32):
                S.op("pe", lambda e, pa=pa, s=s: e.matmul(pA[pa][:, 0:1], w1b[0:64, s, :], peb[:, s:s + 1], start=(s == 0), stop=(s == 31)),
                     r=["w1b", "peb"], w=[f"pA{pa}"])
            S.op("dve", lambda e, pa=pa: e.tensor_copy(hidb[:, 0:1], pA[pa][:, 0:1]), r=[f"pA{pa}"], w=["hidb"])
            S.op("dve", lambda e: e.tensor_scalar(hidb[:, 1:2], hidb[:, 0:1], -1.0, None, ALU.mult), r=["hidb"], w=["hidb"])
            for b in range(DB):
                build_KT(kind, b)
                KTALL = [f"KT{r}" for r in range(8)]
                for g in range(2):
                    gp = slice(g * 64, (g + 1) * 64)
                    for r4 in range(2):
                        pa = next_pa()
                        for rr in range(4):
                            r = r4 * 4 + rr
                            for s in range(32):
                                if s < 16:
                                    rhs = KT[gp, 16 * r + s, :]
                                    ocs = slice(rr * 128, rr * 128 + 128)
                                elif r < 7:
                                    rhs = KT[gp, 16 * (r + 1) + s - 16, :]
                                    ocs = slice(rr * 128, rr * 128 + 128)
                                else:
                                    rhs = KT[gp, s - 16, 1:128]
                                    ocs = slice(rr * 128, rr * 128 + 127)
                                S.op("pe", lambda e, pa=pa, s=s, rhs=rhs, ocs=ocs, gp=gp: e.matmul(
                                    pA[pa][:, ocs], w1b[gp, s, :], rhs, start=(s == 0), stop=(s == 31)),
                                    r=KTALL + ["w1b"], w=[f"pA{pa}"])
                        S.op("act", lambda e, pa=pa: e.activation(cmt[:], pA[pa][:], AF.Exp, scale=-1.0, bias=hidb[:, 1:2]),
                             r=[f"pA{pa}", "hidb"], w=["cmt"])
                        S.op("dve", lambda e: e.tensor_scalar(cmt[:], cmt[:], 1.0, None, ALU.add), r=["cmt"], w=["cmt"])
                        S.op("dve", lambda e: e.reciprocal(cmt[:], cmt[:]), r=["cmt"], w=["cmt"])
                        S.op("dve", lambda e, pa=pa, r4=r4: e.scalar_tensor_tensor(
                            actd[:, r4 * 4:(r4 + 1) * 4, :].rearrange("p a b -> p (a b)"), pA[pa][:], hidb[:, 0:1], cmt[:], ALU.add, ALU.mult),
                            r=[f"pA{pa}", "hidb", "cmt"], w=["actd"])
                    p2 = next_pa()
                    for r in range(8):
                        S.op("pe", lambda e, p2=p2, r=r: e.matmul(pA[p2][:, r * 64:(r + 1) * 64], actd[:, r, :], w2b[:], start=True, stop=True),
                             r=["actd", "w2b"], w=[f"pA{p2}"])
                    if kind == "ck":
                        S.op("act", lambda e, p2=p2: e.activation(cmt[:], pA[p2][:], AF.Square, scale=0.125), r=[f"pA{p2}"], w=["cmt"])
                        S.op("dve", lambda e: e.tensor_reduce(cst[:, 0:8], cmt[:].rearrange("p (r d) -> p r d", d=64), AX.X, ALU.add),
                             r=["cmt"], w=["cst0"])
                        S.op("act", lambda e: e.activation(cst[:, 0:8], cst[:, 0:8], AF.Ln, bias=epsc[:, 0:1]), r=["cst0", "epsc"], w=["cst0"])
                        S.op("act", lambda e: e.activation(cst[:, 0:8], cst[:, 0:8], AF.Exp, scale=-0.5), r=["cst0"], w=["cst0"])
                        S.op("dve", lambda e, p2=p2, g=g: e.tensor_tensor(ck32[:, :, g, :], pA[p2][:].rearrange("p (r d) -> p r d", d=64),
                                                                       cst[:, 0:8].unsqueeze(2).broadcast_to([P, 8, 64]), ALU.mult),
                             r=[f"pA{p2}", "cst0"], w=["ck32"])
                        S.op("dve", lambda e, g=g: e.tensor_tensor(ckb[:, :, g, :], ck32[:, :, g, :], knw[:].unsqueeze(1).broadcast_to([P, 8, 64]), ALU.mult),
                             r=["ck32", "knw"], w=["ckb"])
                    else:
                        S.op("act", lambda e, p2=p2, b=b, g=g: e.copy(cvcd[:, b, g, :, :].rearrange("p a b -> p (a b)"), pA[p2][:]),
                             r=[f"pA{p2}"], w=["cvcd"])
                if kind == "ck":
                    pi = next_pt()
                    for r in range(8):
                        S.op("pe", lambda e, pi=pi, r=r: e.transpose(pT[pi][:, r * 128:(r + 1) * 128], ckb[:, r, :, :].rearrange("p a b -> p (a b)"), identb[:]),
                             r=["ckb", "identb"], w=[f"pT{pi}"])
                    S.op("act", lambda e, pi=pi, b=b: e.copy(ckcTd[:, b, :, :].rearrange("p a b -> p (a b)"), pT[pi][:, 0:1024]),
                         r=[f"pT{pi}"], w=["ckcTd"])

        if KC < 2:
            raise _StopC()
        PcD = S.sb([P, 8, 8], F32, "PcD")
        PcDb = S.sb([P, DB, 8, 8], BF16, "PcDb")
        tot = S.sb([P, 16], F32, "tot")
        pcn = S.sb([P, 8, 8], F32, "pcn")
        pcs = S.sb([P, 2, 8], F32, "pcs")
        pcw = S.sb([P, 2, 2, 8], F32, "pcw")
        shf = S.sb([P, 2], F32, "shf")
        impA = S.sb([P, DB, 2, 2], F32, "impA")
        scD = S.sb([8, 2, 128], F32, "scD")
        scD2 = S.sb([8, 2, 128], F32, "scD2")
        m8d = S.sb([8, 16], F32, "m8d")
        mkD = S.sb([8, 2, 128], F32, "mkD")
        MselD = S.sb([P, 2, 8], F32, "MselD")
        S.op("pool", lambda e: e.memset(shf[:], 0.0), w=["shf"])
        Shb = S.sb([P, 128], BF16, "Shb")
        pc7b = S.sb([P, 2], BF16, "pc7b")
        S.op("pool", lambda e: e.memset(Shb[:], 0.0), w=["Shb"])
        S.op("dve", lambda e: e.tensor_copy(Shb[:, 1:128], ident[:, 0:127]), r=["ident", "Shb"], w=["Shb"])

        if os.environ.get("KC4", "9") < "1":
            raise _StopC()
        qs = S.sb([P, DB, 4], BF16, "qs")
        S.op("dve", lambda e: e.tensor_copy(qs[:].rearrange("p b j -> p j b"), QT[:, NT, :, 0:DB]), r=[f"QT{NT}"], w=[f"QT{NT}"])

        bdq = S.sb([P, DB, 8], BF16, "bdq")
        S.op("pool", lambda e: e.memset(bdq[:], 0.0), w=["bdq"])
        S.op("dve", lambda e: e.tensor_copy(bdq[0:64, :, 0:4], qs[0:64, :, :]), r=[f"QT{NT}", "bdq"], w=["bdq"])
        S.op("dve", lambda e: e.tensor_copy(bdq[64:128, :, 4:8], qs[64:128, :, :]), r=[f"QT{NT}", "bdq"], w=["bdq"])
        if os.environ.get("KC4", "9") < "15":
            raise _StopC()

        for b in range(DB):
            pa = next_pa()
            for r in range(8):
                S.op("pe", lambda e, pa=pa, r=r, b=b: e.matmul(
                    pA[pa][:, r * 8:(r + 1) * 8], ckcTd[:, b, r, :], bdq[:, b, :], start=True, stop=True),
                    r=["ckcTd", "bdq"], w=[f"pA{pa}"])
            if os.environ.get("KC4", "9") < "2":
                raise _StopC()
            S.op("act", lambda e, pa=pa: e.activation(PcD[:].rearrange("p a b -> p (a b)"), pA[pa][:, 0:64], AF.Exp), r=[f"pA{pa}"], w=["PcD"])
            S.op("dve", lambda e: e.tensor_tensor(PcD[:], PcD[:], EBc[:], ALU.mult), r=["PcD", "EBc"], w=["PcD"])
            if os.environ.get("KC3", "9") < "1":
                raise _StopC()
            S.op("dve", lambda e, b=b: e.tensor_copy(PcDb[:, b, :, :], PcD[:]), r=["PcD"], w=[f"PcDb{b}"])
            pa = next_pa()
            S.op("pe", lambda e, pa=pa, b=b: e.matmul(pA[pa][:, 0:64], onesb[:], PcDb[:, b, :, :].rearrange("p a b -> p (a b)"), start=True, stop=True),
                 r=[f"PcDb{b}", "onesb"], w=[f"pA{pa}"])
            S.op("dve", lambda e, pa=pa: e.tensor_reduce(tot[:, 0:8], pA[pa][:, 0:64].rearrange("p (r h) -> p h r", h=8), AX.X, ALU.add),
                 r=[f"pA{pa}"], w=["tot"])
            S.op("dve", lambda e: e.tensor_scalar(tot[:, 0:8], tot[:, 0:8], 1e-30, None, ALU.max), r=["tot"], w=["tot"])
            S.op("dve", lambda e: e.reciprocal(tot[:, 0:8], tot[:, 0:8]), r=["tot"], w=["tot"])
            if os.environ.get("KC3", "9") < "2":
                raise _StopC()
            S.op("dve", lambda e: e.tensor_tensor(pcn[:], PcD[:], tot[:, 0:8].unsqueeze(1).broadcast_to([P, 8, 8]), ALU.mult),
                 r=["PcD", "tot"], w=["pcn"])
            S.op("dve", lambda e: e.tensor_reduce(pcs[:], pcn[:].rearrange("p r (g j) -> p g r j", j=4), AX.X, ALU.add), r=["pcn"], w=["pcs"])
            S.op("dve", lambda e: e.tensor_tensor(pcw[:], pcs[:].unsqueeze(2).broadcast_to([P, 2, 2, 8]),
                                                   Wr[:].unsqueeze(1).broadcast_to([P, 2, 2, 8]), ALU.mult), r=["pcs", "Wr"], w=["pcw"])
            S.op("dve", lambda e, b=b: e.tensor_reduce(impA[:, b, :, :], pcw[:], AX.X, ALU.add), r=["pcw"], w=[f"impA{b}"])
            S.op("dve", lambda e: e.tensor_copy(pc7b[:], pcs[:, :, 7]), r=["pcs"], w=["pc7b"])
            psh = next_pa()
            S.op("pe", lambda e, psh=psh: e.matmul(pA[psh][:, 0:2], Shb[:], pc7b[:], start=True, stop=True), r=["Shb", "pc7b"], w=[f"pA{psh}"])
            S.op("dve", lambda e, psh=psh: e.tensor_copy(shf[:], pA[psh][:, 0:2]), r=[f"pA{psh}"], w=["shf"])
            S.op("dve", lambda e, b=b: e.scalar_tensor_tensor(impA[:, b, :, 0], shf[:], 0.5, impA[:, b, :, 0], ALU.mult, ALU.add),
                 r=["shf", f"impA{b}"], w=[f"impA{b}"])
            if os.environ.get("KC3", "9") < "3":
                raise _StopC()
            for g in range(2):
                for r in range(8):
                    S.op("pe", lambda e, g=g, r=r, b=b: e.matmul(pR[g][0:4, 0:64], PcDb[:, b, r, g * 4:(g + 1) * 4], cvcd[:, b, g, r, :],
                                                               start=(r == 0), stop=(r == 7)), r=[f"PcDb{b}", "cvcd"], w=[f"pR{g}"])
                for r in range(8):
                    S.op("pe", lambda e, g=g, r=r, b=b: e.matmul(pR[g][0:4, 64:65], PcDb[:, b, r, g * 4:(g + 1) * 4], onesb[:, 0:1],
                                                               start=(r == 0), stop=(r == 7)), r=[f"PcDb{b}", "onesb"], w=[f"pR{g}"])
                S.op("act", lambda e, g=g, b=b: e.copy(obd[:, 0, b, g, :], pR[g][0:4, 0:65]), r=[f"pR{g}"], w=["obd"])
        if os.environ.get("KC2", "9") < "1":
            raise _StopC()
        IMPA = [f"impA{b}" for b in range(DB)]
        pa = next_pa()
        for u in range(2):
            S.op("pe", lambda e, pa=pa, u=u: e.transpose(pA[pa][0:8, u * 128:(u + 1) * 128], impA[:, :, :, u].rearrange("p b g -> p (b g)"), ident[:]),
                 r=IMPA + ["ident"], w=[f"pA{pa}"])
        S.op("dve", lambda e, pa=pa: e.tensor_copy(scD[:].rearrange("p a b -> p (a b)"), pA[pa][0:8, 0:256]), r=[f"pA{pa}"], w=["scD"])
        S.op("dve", lambda e: e.tensor_tensor(scD[:].rearrange("p a b -> p (a b)"), scD[:].rearrange("p a b -> p (a b)"), A2[:], ALU.mult), r=["scD", "A2"], w=["scD"])
        S.op("dve", lambda e: e.tensor_tensor(scD[:].rearrange("p a b -> p (a b)"), scD[:].rearrange("p a b -> p (a b)"), B2[:], ALU.add), r=["scD", "B2"], w=["scD"])
        if os.environ.get("KC2", "9") < "2":
            raise _StopC()
        S.op("dve", lambda e: e.max(m8d[:, 0:8], scD[:].rearrange("p a b -> p (a b)")), r=["scD"], w=["m8da"])
        S.op("dve", lambda e: e.match_replace(scD2[:].rearrange("p a b -> p (a b)"), m8d[:, 0:8], scD[:].rearrange("p a b -> p (a b)"), -2.0),
             r=["scD", "m8da"], w=["scD2"])
        S.op("dve", lambda e: e.max(m8d[:, 8:16], scD2[:].rearrange("p a b -> p (a b)")), r=["scD2"], w=["m8db"])
        S.op("dve", lambda e: e.tensor_scalar(mkD[:].rearrange("p a b -> p (a b)"), scD[:].rearrange("p a b -> p (a b)"), m8d[:, 14:15], None, ALU.is_ge),
             r=["scD", "m8db"], w=["mkD"])
        if os.environ.get("KC2", "9") < "3":
            raise _StopC()
        pa = next_pa()
        for u in range(2):
            S.op("pe", lambda e, pa=pa, u=u: e.transpose(pA[pa][:, u * 8:(u + 1) * 8], mkD[:, u, :], ident[0:8, 0:8]),
                 r=["mkD", "ident"], w=[f"pA{pa}"])
        S.op("dve", lambda e, pa=pa: e.tensor_copy(MselD[:].rearrange("p a b -> p (a b)"), pA[pa][:, 0:16]), r=[f"pA{pa}"], w=["MselD"])

        if KC < 3:
            raise _StopC()
        knT = S.sb([P, 128], BF16, "knT")
        Pn = S.sb([P, DB, 8], BF16, "Pn")
        pn32 = S.sb([P, DB, 8], F32, "pn32")
        pi = next_pt()
        S.op("pe", lambda e, pi=pi: e.transpose(pT[pi][:, 0:128], knewb[:], identb[:]), r=["knewb", "identb"], w=[f"pT{pi}"])
        S.op("act", lambda e, pi=pi: e.copy(knT[:], pT[pi][:, 0:128]), r=[f"pT{pi}"], w=["knT"])
        pa = next_pa()
        for b in range(DB):
            S.op("pe", lambda e, pa=pa, b=b: e.matmul(pA[pa][:, b * 8:(b + 1) * 8], knT[:, :], bdq[:, b, :], start=True, stop=True),
                 r=["knT", "bdq"], w=[f"pA{pa}"])
        S.op("act", lambda e, pa=pa: e.activation(pn32[:].rearrange("p a b -> p (a b)"), pA[pa][:, 0:32], AF.Exp), r=[f"pA{pa}"], w=["pn32"])
        S.op("dve", lambda e: e.tensor_tensor(Pn[:], pn32[:], EBn[:], ALU.mult), r=["pn32", "EBn"], w=["Pn"])

        if KC < 4:
            raise _StopC()
        es32 = S.sb([P, 512], F32, "es32")
        Pd = S.sb([P, 128, 8], BF16, "Pd")
        psm = S.sb([P, 8], F32, "psm")
        psmb = S.sb([P, 8], BF16, "psmb")
        for b in range(DB):
            build_KT("sk", b)
            KTALL = [f"KT{r}" for r in range(8)]
            for u in range(2):
                for tt in range(64):
                    t = u * 64 + tt
                    S.op("pe", lambda e, u=u, tt=tt, t=t, b=b: e.matmul(pA[u][:, tt * 8:(tt + 1) * 8], KT[:, t, :], bdq[:, b, :], start=True, stop=True),
                         r=KTALL + ["bdq"], w=[f"pA{u}"])
                S.op("act", lambda e, u=u: e.activation(es32[:], pA[u][:], AF.Exp), r=[f"pA{u}"], w=["es32"])
                S.op("dve", lambda e, u=u: e.tensor_tensor(es32[:].rearrange("p (t h) -> p t h", h=8), es32[:].rearrange("p (t h) -> p t h", h=8),
                                                            EBs[:, u * 64:(u + 1) * 64, :], ALU.mult), r=["es32", "EBs"], w=["es32"])
                S.op("dve", lambda e, u=u, b=b: e.tensor_tensor(
                    Pd[:, u * 64:(u + 1) * 64, :].rearrange("p t (g j) -> p t g j", j=4), es32[:].rearrange("p (t g j) -> p t g j", g=2, j=4),
                    MselD[:, u, b * 2:b * 2 + 2].unsqueeze(1).unsqueeze(3).broadcast_to([P, 64, 2, 4]), ALU.mult),
                    r=["es32", "MselD"], w=[f"Pd{u}"])
            S.op("dve", lambda e: e.tensor_reduce(psm[:], Pd[:].rearrange("p t h -> p h t"), AX.X, ALU.add), r=["Pd0", "Pd1"], w=["psm0", "psm1"])
            S.op("dve", lambda e: e.tensor_copy(psmb[:], psm[:]), r=["psm0", "psm1"], w=["psmb"])
            for r in range(8):
                i = gather("sv", b, r)
                for tt in range(16):
                    t = r * 16 + tt
                    for g in range(2):
                        S.op("pe", lambda e, g=g, t=t, tt=tt, i=i: e.matmul(
                            pR[g][0:4, 0:64], Pd[:, t, g * 4:(g + 1) * 4], Xg[i][:, tt * 128 + g * 64: tt * 128 + (g + 1) * 64],
                            start=(t == 0), stop=False), r=["Pd0", "Pd1", f"Xg{i}"], w=[f"pR{g}"])
            for g in range(2):
                S.op("pe", lambda e, g=g, b=b: e.matmul(pR[g][0:4, 0:64], Pn[:, b, g * 4:(g + 1) * 4], Vn[:, g, :], start=False, stop=True),
                     r=["Pn", "Vn"], w=[f"pR{g}"])
                S.op("pe", lambda e, g=g: e.matmul(pR[g][0:4, 64:65], psmb[:, g * 4:(g + 1) * 4], onesb[:, 0:1], start=True, stop=False),
                     r=["psmb", "onesb"], w=[f"pR{g}"])
                S.op("pe", lambda e, g=g, b=b: e.matmul(pR[g][0:4, 64:65], Pn[:, b, g * 4:(g + 1) * 4], onesb[:, 0:1], start=False, stop=True),
                     r=["Pn", "onesb"], w=[f"pR{g}"])
                S.op("act", lambda e, g=g, b=b: e.copy(obd[:, 1, b, g, :], pR[g][0:4, 0:65]), r=[f"pR{g}"], w=["obd"])

        if KC < 5:
            raise _StopC()
        kw32d = S.sb([P, 4, 128], F32, "kw32d")
        vw32d = S.sb([P, 4, 128], F32, "vw32d")
        vwbd = S.sb([P, 4, 128], BF16, "vwbd")
        KwTd = S.sb([P, 4, 128], BF16, "KwTd")
        pw32 = S.sb([P, 4, 8], F32, "pw32")
        Pw = S.sb([P, 4, 8], BF16, "Pw")
        for b in range(DB):
            S.dma("sp", kw32d[:], dout["s_win_k"][b].rearrange("(p u) f -> p u f", u=4), r=[f"swk{b}"], w=["kw32d"], key="kw32d")
            S.dma("sp", vw32d[:], dout["s_win_v"][b].rearrange("(p u) f -> p u f", u=4), r=[f"swv{b}"], w=["vw32d"], key="vw32d")
            S.op("dve", lambda e: e.tensor_copy(vwbd[:], vw32d[:]), r=["vw32d"], w=["vwbd"])
            pa = next_pa()
            for u in range(4):
                S.op("pe", lambda e, pa=pa, u=u: e.transpose(pA[pa][:, u * 128:(u + 1) * 128], kw32d[:, u, :], ident[:]),
                     r=["kw32d", "ident"], w=[f"pA{pa}"])
            S.op("act", lambda e, pa=pa: e.copy(KwTd[:].rearrange("p a b -> p (a b)"), pA[pa][:]), r=[f"pA{pa}"], w=["KwTd"])
            pa = next_pa()
            for u in range(4):
                S.op("pe", lambda e, pa=pa, u=u, b=b: e.matmul(pA[pa][:, u * 8:(u + 1) * 8], KwTd[:, u, :], bdq[:, b, :],
                                                             start=True, stop=True), r=["KwTd", "bdq"], w=[f"pA{pa}"])
            S.op("act", lambda e, pa=pa: e.activation(pw32[:].rearrange("p a b -> p (a b)"), pA[pa][:, 0:32], AF.Exp), r=[f"pA{pa}"], w=["pw32"])
            S.op("dve", lambda e: e.tensor_tensor(Pw[:], pw32[:], EBw[:], ALU.mult), r=["pw32", "EBw"], w=["Pw"])
            for g in range(2):
                for u in range(4):
                    S.op("pe", lambda e, g=g, u=u: e.matmul(pR[g][0:4, 0:64], Pw[:, u, g * 4:(g + 1) * 4], vwbd[:, u, g * 64:(g + 1) * 64],
                                                            start=(u == 0), stop=(u == 3)), r=["Pw", "vwbd"], w=[f"pR{g}"])
                for u in range(4):
                    S.op("pe", lambda e, g=g, u=u: e.matmul(pR[g][0:4, 64:65], Pw[:, u, g * 4:(g + 1) * 4], onesb[:, 0:1],
                                                            start=(u == 0), stop=(u == 3)), r=["Pw", "onesb"], w=[f"pR{g}"])
                S.op("act", lambda e, g=g, b=b: e.copy(obd[:, 2, b, g, :], pR[g][0:4, 0:65]), r=[f"pR{g}"], w=["obd"])

        if KC < 6:
            raise _StopC()
        S.barrier()
        S.release(mC)
        rrd = S.sb([4, 24], F32, "rrd")
        obn = S.sb([4, 24, 64], F32, "obn")
        ond = S.sb([P, 3, 8, 64], F32, "ond")
        prd = S.sb([P, 3, 8, 64], F32, "prd")
        nsas = S.sb([P, 512], F32, "nsas")
        nsasb = S.sb([P, 512], BF16, "nsasb")
        S.op("dve", lambda e: e.tensor_scalar(rrd[:], obd[:].rearrange("p a b c d -> p (a b c) d")[:, :, 64], 1e-30, None, ALU.max), r=["obd"], w=["rrd"])
        S.op("dve", lambda e: e.reciprocal(rrd[:], rrd[:]), r=["rrd"], w=["rrd"])
        S.op("dve", lambda e: e.tensor_tensor(obn[:], obd[:].rearrange("p a b c d -> p (a b c) d")[:, :, 0:64],
                                               rrd[:].unsqueeze(2).broadcast_to([4, 24, 64]), ALU.mult), r=["obd", "rrd"], w=["obn"])
        S.dma("sp", dsc_h.ap().rearrange("j a b g d -> j (a b g) d"), obn[:], r=["obn"], w=["dsc"], key="dsc")
        S.op("pool", lambda e: e.memset(ond[:], 0.0), w=["ond"])
        for br in range(3):
            for g in range(2):
                S.dma("sp", ond[0:DB, br, g * 4:(g + 1) * 4, :], dsc_h.ap()[:, br, :, g, :].rearrange("j b d -> b j d"),
                      r=["dsc"], w=["ond"], key="ond", allow_slow_non_contiguous=True)
        S.op("dve", lambda e: e.tensor_tensor(prd[:], ond[:], gates[:, NT, :].rearrange("p (a h) -> p a h", h=8).unsqueeze(3).broadcast_to([P, 3, 8, 64]),
                                               ALU.mult), r=["ond", f"gates{NT}"], w=["prd"])
        S.op("dve", lambda e: e.tensor_reduce(nsas[:].rearrange("p (h d) -> p h d", d=64), prd[:].rearrange("p a h d -> p h d a"), AX.X, ALU.add),
             r=["prd"], w=["nsas"])
        S.op("act", lambda e: e.copy(nsasb[:], nsas[:]), r=["nsas"], w=["nsasb"])
        pi = next_pt()
        for c4 in range(4):
            S.op("pe", lambda e, pi=pi, c4=c4: e.transpose(pT[pi][:, c4 * 128:(c4 + 1) * 128], nsasb[:, c4 * 128:(c4 + 1) * 128], identb[:]),
                 r=["nsasb", "identb"], w=[f"pT{pi}"])
        S.op("act", lambda e, pi=pi: e.copy(catT[:, 0:4, SEQ:SEQ + P], pT[pi][:, 0:512].rearrange("p (a b) -> p a b", b=128)),
             r=[f"pT{pi}"], w=[f"catT{NT}"])

    except _StopC:
        pass
    S.barrier()
    S.release(mP1)
    wo = S.sb([P, 8, D], BF16, "wo")
    wpg = S.sb([P, 8, D], BF16, "wpg")
    wpp = S.sb([P, 2, D], BF16, "wpp")
    n2c = S.sb([P, 8], F32, "n2c")
    gnc = S.sb([P, 4], F32, "gnc")
    stg = [S.sb([P, D], F32, f"stg{i}") for i in range(2)]
    dst = S.sb([P, 32], F32, "dst")
    S.dma("sp", n2c[:], din["norm2_w"].rearrange("(k p) -> p k", p=P), w=["n2c"], key="c2", allow_slow_non_contiguous=True)
    S.dma("sp", gnc[:], din["ret_gn_w"].rearrange("(k p) -> p k", p=P), w=["gnc"], key="c2", allow_slow_non_contiguous=True)
    si = [0]

    def stage_load(src_ap, shape_view=None):
        b = si[0] % 2
        si[0] += 1
        dst_ap = stg[b][:] if shape_view is None else shape_view(stg[b])
        S.dma("sp", dst_ap, src_ap, w=[f"stg{b}"], key=f"stg{b}")
        return b

    for kc in range(8):
        b = stage_load(din["w_out"][kc * P:(kc + 1) * P, :])
        eng = "dve" if kc % 2 == 0 else "pool"
        if kc < 4:
            S.op(eng, lambda e, b=b, kc=kc: e.tensor_copy(wo[:, kc, :], stg[b][:]), r=[f"stg{b}"], w=["wo"])
        else:
            S.op(eng, lambda e, b=b, kc=kc: e.tensor_scalar(wo[:, kc, :], stg[b][:], gnc[:, kc - 4:kc - 3], None, ALU.mult),
                 r=[f"stg{b}", "gnc"], w=["wo"])
    for kc in range(8):
        b = stage_load(din["w_ple_gate"][kc * P:(kc + 1) * P, :])
        eng = "dve" if kc % 2 == 0 else "pool"
        S.op(eng, lambda e, b=b, kc=kc: e.tensor_copy(wpg[:, kc, :], stg[b][:]), r=[f"stg{b}"], w=["wpg"])
    for kc in range(2):
        b = stage_load(din["w_ple_proj"][kc * P:(kc + 1) * P, :])
        S.op("dve", lambda e, b=b, kc=kc: e.tensor_copy(wpp[:, kc, :], stg[b][:]), r=[f"stg{b}"], w=["wpp"])

    CW = 640
    actT = S.sb([P, NFF, CW], BF16, "actT")
    h32 = S.sb([P, 5, D], F32, "h32")
    wdb = S.sb([P, NFF, 512], BF16, "wdb")
    wds = [S.sb([P, 2, 512], F32, f"wds{i}") for i in range(2)]
    wgs = [S.sb([P, 8, 128], F32, f"wgs{i}") for i in range(2)]
    wgb = [S.sb([P, 8, 128], BF16, f"wgb{i}") for i in range(4)]
    hb = S.sb([P, D], BF16, "hb")
    hpT = S.sb([P, 8, 128], BF16, "hpT")
    ptl = S.sb([P, 256], F32, "ptl")
    pbl = S.sb([P, 256], BF16, "pbl")
    ppT = S.sb([P, 2, 128], BF16, "ppT")
    tmpa = [S.sb([P, 512], F32, f"tmpa{i}") for i in range(2)]
    wg_dram = din["w_gate"].rearrange("(k p) n -> p k n", p=P)
    wu_dram = din["w_up"].rearrange("(k p) n -> p k n", p=P)
    wd_dram = din["w_down"].rearrange("(f p) n -> p f n", p=P)
    n2b = n2c[:, 0:8].unsqueeze(2).broadcast_to([P, 8, 128])
    wgi = [0]
    tai = [0]

    for c in range(4):
        tiles = [4 * c + i for i in range(4)] + ([NT] if c == 3 else [])
        nti = len(tiles)
        col0 = 4 * c * P
        ncols = nti * P
        for ti, t in enumerate(tiles):
            tc_ = slice(t * P, (t + 1) * P)
            if t == NT:
                b = si[0] % 2
                si[0] += 1
                S.op("pool", lambda e, b=b: e.memset(stg[b][:], 0.0), w=[f"stg{b}"])
                S.dma("sp", stg[b][0:DB, :], din["x_sample"], w=[f"stg{b}"], key=f"stg{b}")
            else:
                b = stage_load(din["x_prompt"][tc_, :])
            for g2 in range(2):
                pa = next_pa()
                for kc in range(8):
                    S.op("pe", lambda e, pa=pa, kc=kc, g2=g2, tc_=tc_: e.matmul(
                        pA[pa][:], catT[:, kc, tc_], wo[:, kc, g2 * 512:(g2 + 1) * 512], start=(kc == 0), stop=(kc == 7)),
                        r=[f"catT{t}", "wo"], w=[f"pA{pa}"])
                S.op("dve", lambda e, pa=pa, g2=g2, ti=ti, b=b: e.tensor_tensor(
                    h32[:, ti, g2 * 512:(g2 + 1) * 512], pA[pa][:], stg[b][:, g2 * 512:(g2 + 1) * 512], ALU.add),
                    r=[f"pA{pa}", f"stg{b}"], w=[f"h32_{ti}"])
            S.op("act", lambda e, ti=ti: e.activation(junk[:], h32[:, ti, :], AF.Square, scale=1.0 / 32.0, accum_out=dst[:, 0:1]),
                 r=[f"h32_{ti}"], w=["junk", "dst0"])
            S.op("act", lambda e: e.activation(dst[:, 0:1], dst[:, 0:1], AF.Ln, bias=epsc[:, 0:1]), r=["dst0", "epsc"], w=["dst0"])
            S.op("act", lambda e: e.activation(dst[:, 0:1], dst[:, 0:1], AF.Exp, scale=-0.5), r=["dst0"], w=["dst0"])
            S.op("act", lambda e, ti=ti: e.activation(hb[:], h32[:, ti, :], AF.Copy, scale=dst[:, 0:1]),
                 r=[f"h32_{ti}", "dst0"], w=["hb"])
            for hf in range(2):
                pi = next_pt()
                for k4 in range(4):
                    kc = hf * 4 + k4
                    S.op("pe", lambda e, pi=pi, k4=k4, kc=kc: e.transpose(
                        pT[pi][:, k4 * 128:(k4 + 1) * 128], hb[:, kc * 128:(kc + 1) * 128], identb[:]),
                        r=["hb", "identb"], w=[f"pT{pi}"])
                S.op("dve", lambda e, pi=pi, hf=hf, tc_=tc_: e.tensor_copy(
                    catT[:, hf * 4:(hf + 1) * 4, tc_], pT[pi][:, 0:512].rearrange("p (a b) -> p a b", b=128)),
                    r=[f"pT{pi}"], w=[f"catT{t}"])
        HN = [f"catT{t}" for t in tiles]
        subs = [(0, min(512, ncols))] + ([(512, ncols)] if ncols > 512 else [])
        for f in range(NFF):
            wb = []
            for wi, wdram in enumerate((wg_dram, wu_dram)):
                sb_ = wgi[0] % 2
                bb_ = wgi[0] % 4
                wgi[0] += 1
                S.dma("sp", wgs[sb_][:], wdram[:, :, f * P:(f + 1) * P], w=[f"wgs{sb_}"], key=f"wgs{sb_}")
                if wi == 0:
                    S.op("dve", lambda e, sb_=sb_, bb_=bb_: e.tensor_tensor(wgb[bb_][:], wgs[sb_][:], n2b, ALU.mult),
                         r=[f"wgs{sb_}", "n2c"], w=[f"wgb{bb_}"])
                else:
                    for kc in range(8):
                        S.op("act", lambda e, sb_=sb_, bb_=bb_, kc=kc: e.activation(
                            wgb[bb_][:, kc, :], wgs[sb_][:, kc, :], AF.Copy, scale=n2c[:, kc:kc + 1]),
                            r=[f"wgs{sb_}", "n2c"], w=[f"wgb{bb_}"])
                wb.append(bb_)
            for (lo, hi) in subs:
                n = hi - lo
                pg = next_pa()
                for kc in range(8):
                    S.op("pe", lambda e, pg=pg, kc=kc, lo=lo, hi=hi, n=n: e.matmul(
                        pA[pg][:, 0:n], wgb[wb[0]][:, kc, :], catT[:, kc, col0 + lo:col0 + hi], start=(kc == 0), stop=(kc == 7)),
                        r=HN + [f"wgb{wb[0]}"], w=[f"pA{pg}"])
                pu = next_pa()
                for kc in range(8):
                    S.op("pe", lambda e, pu=pu, kc=kc, lo=lo, hi=hi, n=n: e.matmul(
                        pA[pu][:, 0:n], wgb[wb[1]][:, kc, :], catT[:, kc, col0 + lo:col0 + hi], start=(kc == 0), stop=(kc == 7)),
                        r=HN + [f"wgb{wb[1]}"], w=[f"pA{pu}"])
                ta = tai[0] % 2
                tai[0] += 1
                S.op("act", lambda e, ta=ta, pg=pg, n=n: e.activation(tmpa[ta][:, 0:n], pA[pg][:, 0:n], AF.Exp, scale=-1.0),
                     r=[f"pA{pg}"], w=[f"tmpa{ta}"])
                S.op("pool", lambda e, ta=ta, n=n: e.tensor_scalar(tmpa[ta][:, 0:n], tmpa[ta][:, 0:n], 1.0, None, ALU.add),
                     r=[f"tmpa{ta}"], w=[f"tmpa{ta}"])
                S.op("dve", lambda e, ta=ta, n=n: e.reciprocal(tmpa[ta][:, 0:n], tmpa[ta][:, 0:n]), r=[f"tmpa{ta}"], w=[f"tmpa{ta}"])
                S.op("dve", lambda e, ta=ta, pg=pg, n=n: e.tensor_tensor(tmpa[ta][:, 0:n], tmpa[ta][:, 0:n], pA[pg][:, 0:n], ALU.mult),
                     r=[f"tmpa{ta}", f"pA{pg}"], w=[f"tmpa{ta}"])
                S.op("dve", lambda e, ta=ta, pu=pu, n=n, lo=lo, hi=hi, f=f: e.tensor_tensor(
                    actT[:, f, lo:hi], tmpa[ta][:, 0:n], pA[pu][:, 0:n], ALU.mult),
                    r=[f"tmpa{ta}", f"pA{pu}"], w=[f"actT{f}"])
        ACTT = [f"actT{f}" for f in range(NFF)]
        for cg in range(2):
            for fp in range(NFF // 2):
                b = fp % 2
                S.dma("sp", wds[b][:], wd_dram[:, 2 * fp:2 * fp + 2, cg * 512:(cg + 1) * 512], w=[f"wds{b}"], key=f"wds{b}")
                eng = "pool" if fp % 2 == 0 else "dve"
                S.op(eng, lambda e, b=b, fp=fp: e.tensor_copy(wdb[:, 2 * fp:2 * fp + 2, :], wds[b][:]),
                     r=[f"wds{b}"], w=["wdb"])
            for ti, t in enumerate(tiles):
                pa = next_pa()
                for f in range(NFF):
                    S.op("pe", lambda e, pa=pa, f=f, ti=ti: e.matmul(
                        pA[pa][:], actT[:, f, ti * P:(ti + 1) * P], wdb[:, f, :], start=(f == 0), stop=(f == NFF - 1)),
                        r=ACTT + ["wdb"], w=[f"pA{pa}"])
                S.op("dve", lambda e, pa=pa, ti=ti, cg=cg: e.tensor_tensor(
                    h32[:, ti, cg * 512:(cg + 1) * 512], h32[:, ti, cg * 512:(cg + 1) * 512], pA[pa][:], ALU.add),
                    r=[f"pA{pa}", f"h32_{ti}"], w=[f"h32_{ti}"])
        for ti, t in enumerate(tiles):
            S.op("act", lambda e, ti=ti: e.copy(hb[:], h32[:, ti, :]), r=[f"h32_{ti}"], w=["hb"])
            for hf in range(2):
                pi = next_pt()
                for k4 in range(4):
                    kc = hf * 4 + k4
                    S.op("pe", lambda e, pi=pi, k4=k4, kc=kc: e.transpose(
                        pT[pi][:, k4 * 128:(k4 + 1) * 128], hb[:, kc * 128:(kc + 1) * 128], identb[:]),
                        r=["hb", "identb"], w=[f"pT{pi}"])
                S.op("act", lambda e, pi=pi, hf=hf: e.copy(
                    hpT[:, hf * 4:(hf + 1) * 4, :].rearrange("p a b -> p (a b)"), pT[pi][:, 0:512]),
                    r=[f"pT{pi}"], w=[f"hpT{hf}"])
            if t == NT:
                S.op("pool", lambda e: e.memset(ptl[:], 0.0), w=["ptl"])
                S.dma("sp", ptl[0:DB, :], din["p_sample"], w=["ptl"], key="ptl")
            else:
                S.dma("sp", ptl[:], din["p_prompt"][t * P:(t + 1) * P, :], w=["ptl"], key="ptl")
            S.op("pool", lambda e: e.tensor_copy(pbl[:], ptl[:]), r=["ptl"], w=["pbl"])
            pi = next_pt()
            for k2 in range(2):
                S.op("pe", lambda e, pi=pi, k2=k2: e.transpose(pT[pi][:, k2 * 128:(k2 + 1) * 128], pbl[:, k2 * 128:(k2 + 1) * 128], identb[:]),
                     r=["pbl", "identb"], w=[f"pT{pi}"])
            S.op("act", lambda e, pi=pi: e.copy(ppT[:].rearrange("p a b -> p (a b)"), pT[pi][:, 0:256]), r=[f"pT{pi}"], w=["ppT"])
            for g2 in range(2):
                cs_ = slice(g2 * 512, (g2 + 1) * 512)
                pgt = next_pa()
                for kc in range(8):
                    S.op("pe", lambda e, pgt=pgt, kc=kc, cs_=cs_: e.matmul(pA[pgt][:], hpT[:, kc, :], wpg[:, kc, cs_], start=(kc == 0), stop=(kc == 7)),
                         r=["hpT0", "hpT1", "wpg"], w=[f"pA{pgt}"])
                ppp = next_pa()
                for kc in range(2):
                    S.op("pe", lambda e, ppp=ppp, kc=kc, cs_=cs_: e.matmul(pA[ppp][:], ppT[:, kc, :], wpp[:, kc, cs_], start=(kc == 0), stop=(kc == 1)),
                         r=["ppT", "wpp"], w=[f"pA{ppp}"])
                ta = tai[0] % 2
                tai[0] += 1
                S.op("act", lambda e, ta=ta, pgt=pgt: e.activation(tmpa[ta][:], pA[pgt][:], AF.Exp, scale=-1.0), r=[f"pA{pgt}"], w=[f"tmpa{ta}"])
                S.op("pool", lambda e, ta=ta: e.tensor_scalar(tmpa[ta][:], tmpa[ta][:], 1.0, None, ALU.add), r=[f"tmpa{ta}"], w=[f"tmpa{ta}"])
                S.op("dve", lambda e, ta=ta: e.reciprocal(tmpa[ta][:], tmpa[ta][:]), r=[f"tmpa{ta}"], w=[f"tmpa{ta}"])
                S.op("dve", lambda e, ta=ta, ppp=ppp: e.tensor_tensor(tmpa[ta][:], tmpa[ta][:], pA[ppp][:], ALU.mult),
                     r=[f"tmpa{ta}", f"pA{ppp}"], w=[f"tmpa{ta}"])
                S.op("dve", lambda e, ta=ta, ti=ti, cs_=cs_: e.tensor_tensor(h32[:, ti, cs_], h32[:, ti, cs_], tmpa[ta][:], ALU.add),
                     r=[f"tmpa{ta}", f"h32_{ti}"], w=[f"h32_{ti}"])
            if t == NT:
                S.dma("pool", dout["y_sample"], h32[0:DB, ti, :], r=[f"h32_{ti}"], key="oy")
            else:
                S.dma("pool", dout["y_prompt"][t * P:(t + 1) * P, :], h32[:, ti, :], r=[f"h32_{ti}"], key="oy")

    print("SBUF peak", S.sb_peak, "of", SB_HI)
    S.emit()
    return nc


_CACHE = {}


def kernel(**inputs):
    if "nc" not in _CACHE:
        _CACHE["nc"] = build_program()
    nc = _CACHE["nc"]
    consts = _consts()
    f = lambda a: np.ascontiguousarray(np.asarray(a))
    in_maps = []
    for c in range(NCORES):
        bs = slice(c * DB, (c + 1) * DB)
        m = {
            "x_prompt": f(inputs["x_prompt"][c]),
            "x_sample": f(inputs["x_sample"][bs, 0]),
            "state_ret": f(inputs["state_ret"][0, bs]),
            "cache_win_k": f(inputs["cache_win_k"][0, bs].reshape(DB, 512, 128)),
            "cache_win_v": f(inputs["cache_win_v"][0, bs].reshape(DB, 512, 128)),
            "norm1_w": f(inputs["norm1_w"][0]),
            "w_in": f(inputs["w_in"][0]),
            "q_norm_w": f(inputs["q_norm_w"][0]),
            "k_norm_w": f(inputs["k_norm_w"][0]),
            "p_prompt": f(inputs["p_prompt"][0, c]), "p_sample": f(inputs["p_sample"][0, bs, 0]),
            "rel_bias": f(inputs["rel_bias"]),
        }
        if ENABLE_DECODE_NSA:
            m["page_table"] = f(inputs["page_table"][bs]).astype(np.int32)
            for k in ("cache_cmp_k", "cache_cmp_v", "cache_slc_k", "cache_slc_v"):
                m[k] = f(inputs[k]).reshape(NPOOL * 8, 2048)
        for k in ("cmp_pe_k", "cmp_w1_k", "cmp_w2_k", "cmp_pe_v", "cmp_w1_v", "cmp_w2_v", "ret_gn_w", "w_out",
                  "norm2_w", "w_gate", "w_up", "w_down", "w_ple_gate", "w_ple_proj"):
            m[k] = f(inputs[k][0])
        for k, v in consts.items():
            m["c_" + k] = v
        in_maps.append(m)
    res = run_bass_kernel_spmd(nc, in_maps, core_ids=list(range(NCORES)))
    R = res.results
    cat = lambda k: np.stack([np.asarray(R[c][k]) for c in range(NCORES)])
    out = {}
    out["y_prompt"] = cat("y_prompt").reshape(8, SEQ, D)
    out["y_sample"] = cat("y_sample").reshape(32, 1, D)
    for k in ("p_cmp_k", "p_cmp_v", "p_slc_k", "p_slc_v"):
        out[k] = cat(k).reshape(1, 8, SEQ, 2, 64)
    for k in ("p_win_k", "p_win_v"):
        out[k] = cat(k).reshape(1, 8, 512, 2, 64)
    out["p_ret"] = cat("p_ret").reshape(1, 8, 4, 128, 128)
    for k in ("s_cmp_k", "s_cmp_v", "s_slc_k", "s_slc_v"):
        out[k] = cat(k).reshape(1, 32, 1, 2, 64)
    for k in ("s_win_k", "s_win_v"):
        out[k] = cat(k).reshape(1, 32, 512, 2, 64)
    out["s_ret"] = cat("s_ret").reshape(1, 32, 4, 128, 128)
    return tuple(np.ascontiguousarray(out[k], dtype=np.float32) for k in OUT_ORDER)
```

```python
import math
import numpy as np
import concourse.bass as bass
import concourse.mybir as mybir
from concourse.bass_utils import run_bass_kernel_spmd

F32 = mybir.dt.float32
BF16 = mybir.dt.bfloat16
I32 = mybir.dt.int32
ALU = mybir.AluOpType
AF = mybir.ActivationFunctionType
AX = mybir.AxisListType

NCORES = 8
D = 1024
SEQ = 2048
NT = SEQ // 128
DB = 4
INC = 3352
DFF = 2816
NFF = DFF // 128
EPS = 1e-6
NPOOL = 5120
import os
ENABLE_DECODE_NSA = True
POOL_PAGES = NPOOL
SB_LO, SB_HI = 16512, 229376


class _Op:
    __slots__ = ("eng", "fn", "deps", "idx", "cidx", "is_dma", "key", "semval", "signal", "sigval")


class _Rec:
    def __getattr__(self, name):
        def f(*a, **k):
            self.call = (name, a, k)
            return self
        return f


class Sched:
    ENGS = ("pe", "act", "dve", "pool", "sp")

    def __init__(self, nc):
        self.nc = nc
        self.ops = {e: [] for e in self.ENGS}
        self.ccount = {e: 0 for e in self.ENGS}
        self.tw = {}
        self.tr = {}
        self.dma_cum = {}
        self.sb_off = SB_LO
        self.sb_peak = SB_LO
        self.nalloc = 0
        self.last = {e: None for e in self.ENGS}
        self.barrier_deps = []
        self.last_dma = {}

    def sb(self, shape, dt, name=None):
        nbytes = int(np.prod(shape[1:])) * mybir.dt.size(dt)
        nbytes = (nbytes + 63) // 64 * 64
        off = self.sb_off
        assert off + nbytes <= SB_HI, f"SBUF overflow {name} {off + nbytes}"
        self.sb_off += nbytes
        self.sb_peak = max(self.sb_peak, self.sb_off)
        self.nalloc += 1
        return self.nc.alloc_sbuf_tensor_at(f"{name or 't'}_{self.nalloc}", list(shape), dt, offset=off)

    def mark(self):
        return self.sb_off

    def release(self, m):
        self.sb_off = m

    def _deps(self, r, w):
        deps = []
        for t in r:
            if t in self.tw:
                deps.append(self.tw[t])
        for t in w:
            if t in self.tw:
                deps.append(self.tw[t])
            deps.extend(self.tr.get(t, ()))
        deps.extend(self.barrier_deps)
        deps = [self.last_dma[d.key] if d.is_dma else d for d in deps]
        return deps

    def _reg(self, op, r, w):
        for t in r:
            self.tr.setdefault(t, []).append(op)
        for t in w:
            self.tw[t] = op
            self.tr[t] = []
        self.last[op.eng] = op

    def op(self, eng, fn, r=(), w=()):
        o = _Op()
        rec = _Rec()
        fn(rec)
        name_, a_, k_ = rec.call
        o.eng, o.is_dma, o.key = eng, False, None
        o.fn = lambda e: getattr(e, name_)(*a_, **k_)
        o.deps = self._deps(r, w)
        o.idx = len(self.ops[eng])
        o.cidx = self.ccount[eng]
        self.ccount[eng] += 1
        o.signal = False
        self.ops[eng].append(o)
        self._reg(o, r, w)
        return o

    def dma(self, q, out, in_, r=(), w=(), key="d", **kw):
        o = _Op()
        o.eng, o.is_dma, o.key = q, True, key
        o.fn = lambda e: e.dma_start(out=out, in_=in_, **kw)
        o.deps = self._deps(r, w)
        o.idx = len(self.ops[q])
        o.cidx = self.ccount[q]
        self.dma_cum[key] = self.dma_cum.get(key, 0) + 16
        o.semval = self.dma_cum[key]
        o.signal = True
        self.last_dma[key] = o
        self.ops[q].append(o)
        self._reg(o, r, w)
        return o

    def dmafn(self, q, fn, r=(), w=(), key="d"):
        o = self.dma(q, None, None, r, w, key)
        o.fn = fn
        return o

    def barrier(self):
        self.barrier_deps = [o for o in self.last.values() if o is not None]
        seen = {}
        for e in self.ENGS:
            for o in self.ops[e]:
                if o.is_dma:
                    seen[o.key] = o
        self.barrier_deps += list(seen.values())

    def emit(self, final_wait=True):
        nc = self.nc
        for e in self.ENGS:
            for o in self.ops[e]:
                for d in o.deps:
                    if not d.is_dma:
                        d.signal = True
        sems = {}
        for e in self.ENGS:
            sems[e] = nc.alloc_semaphore(f"s_{e}")
            c = 0
            for o in self.ops[e]:
                if not o.is_dma and o.signal:
                    c += 1
                    o.sigval = c
        for k in self.dma_cum:
            sems["dma:" + k] = nc.alloc_semaphore(f"sd_{k}")
        engmap = {"pe": "tensor", "act": "scalar", "dve": "vector", "pool": "gpsimd", "sp": "sync"}

        def emit_engine(ename, eh):
            seen = {}
            for o in self.ops[ename]:
                for d in o.deps:
                    if d.is_dma:
                        sk, val = "dma:" + d.key, d.semval
                    else:
                        if d.eng == ename:
                            if ename == "pe" and not o.is_dma:
                                continue
                            if (not o.is_dma) and o.cidx - d.cidx >= 2:
                                continue
                        sk, val = d.eng, d.sigval
                    if seen.get(sk, 0) >= val:
                        continue
                    seen[sk] = val
                    eh.wait_ge(sems[sk], val)
                ins = o.fn(eh)
                if o.is_dma:
                    ins.then_inc(sems["dma:" + o.key], 16)
                elif o.signal:
                    ins.then_inc(sems[ename], 1)
            if final_wait and ename == "sp":
                for k, v in self.dma_cum.items():
                    eh.wait_ge(sems["dma:" + k], v)
                for e2 in self.ENGS:
                    c = sum(1 for o in self.ops[e2] if not o.is_dma and o.signal)
                    if c and e2 != "sp":
                        eh.wait_ge(sems[e2], c)

        with nc.Block() as block:
            @block.tensor
            def _(eh):
                emit_engine("pe", eh)

            @block.scalar
            def _(eh):
                emit_engine("act", eh)

            @block.vector
            def _(eh):
                emit_engine("dve", eh)

            @block.gpsimd
            def _(eh):
                emit_engine("pool", eh)

            @block.sync
            def _(eh):
                emit_engine("sp", eh)


def _rel_bucket_np(dist):
    n = np.maximum(dist, 0)
    nf = np.maximum(n, 16).astype(np.float32)
    large = 16 + (np.log(nf / np.float32(16)) / np.float32(math.log(8.0)) * np.float32(16)).astype(np.int32)
    return np.where(n < 16, n, np.minimum(large, 31)).astype(np.int64)


def _consts():
    c = {}
    c["ident"] = np.eye(128, dtype=np.float32)
    half = 64
    inv = (10000.0 ** (-np.arange(half, dtype=np.float32) / half)).astype(np.float32)
    pos = np.concatenate([np.arange(SEQ), np.array([16384])]).astype(np.float32)
    ang = pos[:, None] * inv[None, :]
    cs = np.zeros((SEQ + 128, 2, 64), np.float32)
    cs[:SEQ + 1, 0] = np.cos(ang)
    cs[:SEQ + 1, 1] = np.sin(ang)
    cs[SEQ:SEQ + 128] = cs[SEQ]
    c["cossin"] = cs.reshape(SEQ + 128, 128)
    lg = np.log1p(-np.exp2(-5.0 - np.arange(4, dtype=np.float64)))
    n = np.arange(128)
    diff = n[None, :] - n[:, None]
    dec = np.where(diff >= 0, np.exp(np.maximum(diff, 0)[None] * lg[:, None, None]), 0.0)
    c["decT"] = np.ascontiguousarray(dec.transpose(1, 0, 2)).astype(np.float32).reshape(128, 512)
    cq = np.exp((n[None, :] + 1) * lg[:, None])
    c["cq"] = np.broadcast_to(cq[None], (128, 4, 128)).astype(np.float32).reshape(128, 512).copy()
    kd = np.exp((127 - n)[:, None] * lg[None, :])
    g128 = np.exp(128 * lg)
    g1 = np.exp(lg)
    misc = np.zeros((128, 16), np.float32)
    misc[:, 0:4] = kd
    misc[:, 4:8] = g128[None, :]
    misc[:, 8:12] = g1[None, :]
    c["rmisc"] = misc
    import ml_dtypes
    bf = ml_dtypes.bfloat16
    LA = 4111
    m = np.arange(LA)
    dA = m - 2063
    oh = np.zeros((32, LA), np.float32)
    oh[_rel_bucket_np(dA), m] = 1.0
    c["ohA"] = oh.astype(bf)
    c["validA"] = np.broadcast_to((dA >= 0).astype(np.float32)[None], (8, LA)).copy()
    c["Jb"] = np.eye(128, dtype=np.float32)[::-1].copy().astype(bf)
    kk = np.arange(128)
    c["LT"] = (kk[None, :] < kk[:, None]).astype(np.float32).astype(bf)
    E = np.zeros((32, 16, 128), np.float32)
    for kt in range(16):
        E[2 * kt + kk // 64, kt, kk] = 1.0
    c["Eoh"] = E.reshape(32, 2048).astype(bf)
    A = np.zeros((128, 8, 32), np.float32)
    Bm = np.zeros((128, 8, 32), np.float32)
    blk = np.arange(32)
    for q8 in range(8):
        qpos = (8 + q8) * 128 + kk
        qb = qpos // 64
        valid = blk[None, :] <= qb[:, None]
        forced = (blk[None, :] == 0) | (blk[None, :] == qb[:, None]) | (blk[None, :] == qb[:, None] - 1)
        A[:, q8] = (valid & ~forced)
        Bm[:, q8] = np.where(valid, np.where(forced, 1e4, 0.0), -1.0)
    c["selA"] = A.reshape(128, 256)
    c["selB"] = Bm.reshape(128, 256)
    W = np.zeros((128, 32), np.float32)
    for j in range(32):
        for cc, wv in ((4 * j - 1, 0.5), (4 * j, 1.0), (4 * j + 1, 1.0), (4 * j + 2, 1.0), (4 * j + 3, 0.5)):
            if 0 <= cc < 127:
                W[cc, j] += wv
    c["Wimp"] = W.astype(bf)
    q = np.arange(128)
    SelC = np.zeros((128, 8, 128), np.float32)
    for j in range(128):
        for r in range(8):
            n_ = 8 * j + r
            if n_ <= 1022:
                SelC[min(16353 - 16 * n_, 127), r, j] = 1.0
    c["SelC"] = SelC.reshape(128, 1024).astype(bf)
    SelW = np.zeros((128, 4, 128), np.float32)
    for p in range(128):
        for u in range(4):
            SelW[min(511 - 4 * p - u, 127), u, p] = 1.0
    c["SelW"] = SelW.reshape(128, 512).astype(bf)
    OHt = np.zeros((128, 2, 128), np.float32)
    for t in range(128):
        OHt[min(128 - t, 127), 0, t] = 1.0
    OHt[127, 1, :] = 1.0
    c["OHt"] = OHt.reshape(128, 256)
    LAB = np.zeros((128, 2, 128), np.float32)
    LAB[:, 0, 127] = 1.0
    LAB[:, 1, :127] = 1.0
    c["LAB"] = LAB.reshape(128, 256).astype(bf)
    A2 = np.ones((8, 2, 128), np.float32)
    B2 = np.zeros((8, 2, 128), np.float32)
    A2[:, 0, 0] = 0.0; A2[:, 1, 127] = 0.0
    B2[:, 0, 0] = 1e4; B2[:, 1, 127] = 1e4
    c["A2"] = A2.reshape(8, 256)
    c["B2"] = B2.reshape(8, 256)
    Wr = np.zeros((128, 2, 8), np.float32)
    Wr[:, 0] = np.array([1, 1, 1, .5, 0, 0, 0, 0], np.float32)
    Wr[:, 1] = np.array([0, 0, 0, .5, 1, 1, 1, .5], np.float32)
    c["Wr"] = Wr.reshape(128, 16)
    return c


CONST_SHAPES = {
    "ident": ([128, 128], F32), "cossin": ([SEQ + 128, 128], F32), "decT": ([128, 512], F32), "cq": ([128, 512], F32),
    "rmisc": ([128, 16], F32),
    "ohA": ([32, 4111], BF16), "validA": ([8, 4111], F32), "Jb": ([128, 128], BF16), "LT": ([128, 128], BF16),
    "Eoh": ([32, 2048], BF16), "selA": ([128, 256], F32), "selB": ([128, 256], F32), "Wimp": ([128, 32], BF16),
    "SelC": ([128, 1024], BF16), "SelW": ([128, 512], BF16), "OHt": ([128, 256], F32), "LAB": ([128, 256], BF16),
    "A2": ([8, 256], F32), "B2": ([8, 256], F32), "Wr": ([128, 16], F32),
}

IN_SPECS = {
    "x_prompt": ([SEQ, D], F32), "x_sample": ([DB, D], F32),
    "state_ret": ([DB, 4, 128, 128], F32),
    "cache_win_k": ([DB, 512, 128], F32), "cache_win_v": ([DB, 512, 128], F32),
    "norm1_w": ([D], F32), "w_in": ([D, INC], F32), "q_norm_w": ([64], F32), "k_norm_w": ([64], F32),
    "p_prompt": ([SEQ, 256], F32), "p_sample": ([DB, 256], F32),
    "cmp_pe_k": ([32, 64], F32), "cmp_w1_k": ([2048, 128], F32), "cmp_w2_k": ([128, 64], F32),
    "cmp_pe_v": ([32, 64], F32), "cmp_w1_v": ([2048, 128], F32), "cmp_w2_v": ([128, 64], F32),
    "ret_gn_w": ([512], F32), "w_out": ([D, D], F32), "norm2_w": ([D], F32),
    "w_gate": ([D, DFF], F32), "w_up": ([D, DFF], F32), "w_down": ([DFF, D], F32),
    "w_ple_gate": ([D, D], F32), "w_ple_proj": ([256, D], F32), "rel_bias": ([32, 8], F32),
}
if ENABLE_DECODE_NSA:
    IN_SPECS.update({
        "page_table": ([DB, 128], I32),
        "cache_cmp_k": ([POOL_PAGES * 8, 2048], F32), "cache_cmp_v": ([POOL_PAGES * 8, 2048], F32),
        "cache_slc_k": ([POOL_PAGES * 8, 2048], F32), "cache_slc_v": ([POOL_PAGES * 8, 2048], F32),
    })
OUT_SPECS = {
    "y_prompt": [SEQ, D], "y_sample": [DB, D],
    "p_cmp_k": [SEQ, 128], "p_cmp_v": [SEQ, 128], "p_slc_k": [SEQ, 128], "p_slc_v": [SEQ, 128],
    "p_win_k": [512, 128], "p_win_v": [512, 128], "p_ret": [4, 128, 128],
    "s_cmp_k": [DB, 128], "s_cmp_v": [DB, 128], "s_slc_k": [DB, 128], "s_slc_v": [DB, 128],
    "s_win_k": [DB, 512, 128], "s_win_v": [DB, 512, 128], "s_ret": [DB, 4, 128, 128],
}
OUT_ORDER = ["y_prompt", "y_sample", "p_cmp_k", "p_cmp_v", "p_slc_k", "p_slc_v", "p_win_k", "p_win_v",
             "p_ret", "s_cmp_k", "s_cmp_v", "s_slc_k", "s_slc_v", "s_win_k", "s_win_v", "s_ret"]


def build_program():
    nc = bass.Bass("TRN2", target_bir_lowering=False)
    din = {k: nc.dram_tensor(k, sh, dt, kind="ExternalInput").ap() for k, (sh, dt) in IN_SPECS.items()}
    for k, (sh, dt) in CONST_SHAPES.items():
        din[k] = nc.dram_tensor("c_" + k, sh, dt, kind="ExternalInput").ap()
    tblA_h = nc.dram_tensor("tblA", [8, 4111], F32, kind="Internal")
    dsc_h = nc.dram_tensor("dsc", [4, 3, DB, 2, 64], F32, kind="Internal")
    dout = {k: nc.dram_tensor(k, sh, F32, kind="ExternalOutput").ap() for k, sh in OUT_SPECS.items()}

    S = Sched(nc)
    P = 128
    ident = S.sb([P, 128], F32, "ident")
    identb = S.sb([P, 128], BF16, "identb")
    decT = S.sb([P, 512], F32, "decT")
    cq = S.sb([P, 512], F32, "cq")
    rmisc = S.sb([P, 16], F32, "rmisc")
    qnw = S.sb([P, 64], F32, "qnw")
    knw = S.sb([P, 64], F32, "knw")
    n1c = S.sb([P, 8], F32, "n1c")
    epsc = S.sb([P, 1], F32, "epsc")
    S.op("pool", lambda e: e.memset(epsc[:], EPS), w=["epsc"])
    S.dma("sp", ident[:], din["ident"], w=["ident"], key="c")
    S.dma("sp", decT[:], din["decT"], w=["decT"], key="c")
    S.dma("sp", cq[:], din["cq"], w=["cq"], key="c")
    S.dma("sp", rmisc[:], din["rmisc"], w=["rmisc"], key="c")
    S.dma("sp", qnw[:], din["q_norm_w"].partition_broadcast(P), w=["qnw"], key="c")
    S.dma("sp", knw[:], din["k_norm_w"].partition_broadcast(P), w=["knw"], key="c")
    S.dma("sp", n1c[:], din["norm1_w"].rearrange("(k p) -> p k", p=P), w=["n1c"], key="c",
          allow_slow_non_contiguous=True)
    S.op("dve", lambda e: e.tensor_copy(identb[:], ident[:]), r=["ident"], w=["identb"])
    S.op("dve", lambda e: e.tensor_scalar(qnw[:], qnw[:], 0.125, None, ALU.mult), r=["qnw"], w=["qnw"])

    catT = S.sb([P, 8, SEQ + 128], BF16, "catT")
    mP1 = S.mark()
    QT = S.sb([P, NT + 1, 4, 128], BF16, "QT")
    KsT = S.sb([P, SEQ], BF16, "KsT")
    KwT = S.sb([P, SEQ], BF16, "KwT")
    kcT = S.sb([P, SEQ], BF16, "kcT")
    vcT = S.sb([P, SEQ], BF16, "vcT")
    Vs = S.sb([P, NT, 2, 65], BF16, "Vs")
    Vw = S.sb([P, NT, 2, 65], BF16, "Vw")
    gates = S.sb([P, NT + 1, 24], F32, "gates")
    knewb = S.sb([P, 128], BF16, "knewb")
    Vn = S.sb([P, 2, 64], BF16, "Vn")
    mP2 = S.mark()
    Sst = S.sb([P, 512], F32, "Sst")
    Sbf = S.sb([P, 512], BF16, "Sbf")

    win = S.sb([P, 8, INC], BF16, "win")
    m0 = S.mark()
    wst = [S.sb([P, INC], F32, f"wst{i}") for i in range(2)]
    qperm = [0, 4, 1, 5, 2, 6, 3, 7]
    for kc in range(8):
        b = kc % 2
        S.dma("sp", wst[b][:], din["w_in"][kc * P:(kc + 1) * P, :], w=[f"wst{b}"], key=f"wst{b}")
        eng = "dve" if kc % 2 == 0 else "pool"
        sc = n1c[:, kc:kc + 1]
        for j, h in enumerate(qperm):
            S.op(eng, lambda e, j=j, h=h, b=b, kc=kc, sc=sc: e.tensor_scalar(
                win[:, kc, j * 64:(j + 1) * 64], wst[b][:, h * 64:(h + 1) * 64], sc, None, ALU.mult),
                r=[f"wst{b}", "n1c"], w=[f"win{kc}"])
        S.op(eng, lambda e, b=b, kc=kc, sc=sc: e.tensor_scalar(
            win[:, kc, 512:1816], wst[b][:, 512:1816], sc, None, ALU.mult), r=[f"wst{b}", "n1c"], w=[f"win{kc}"])
        S.op(eng, lambda e, b=b, kc=kc, sc=sc: e.tensor_scalar(
            win[:, kc, 1816:2328], wst[b][:, 1816:2328], sc, 128.0 ** -0.5, ALU.mult, ALU.mult),
            r=[f"wst{b}", "n1c"], w=[f"win{kc}"])
        S.op(eng, lambda e, b=b, kc=kc, sc=sc: e.tensor_scalar(
            win[:, kc, 2328:INC], wst[b][:, 2328:INC], sc, None, ALU.mult), r=[f"wst{b}", "n1c"], w=[f"win{kc}"])
    WIN_ALL = [f"win{kc}" for kc in range(8)]
    S.barrier()
    S.release(m0)

    xt = [S.sb([P, D], F32, f"xt{i}") for i in range(2)]
    junk = S.sb([P, D], BF16, "junk")
    xn = S.sb([P, D], BF16, "xn")
    xnT = [S.sb([P, 8, 128], BF16, f"xnT{i}") for i in range(2)]
    st = S.sb([P, 32], F32, "st")
    sq = S.sb([P, 512], F32, "sq")
    q32 = S.sb([P, 512], F32, "q32")
    qbf = S.sb([P, 512], BF16, "qbf")
    kv32 = [S.sb([P, 512], F32, f"kv32{i}") for i in range(2)]
    kw32 = [S.sb([P, 256], F32, f"kw32{i}") for i in range(2)]
    kbf = S.sb([P, 256], BF16, "kbf")
    gt32 = S.sb([P, 24], F32, "gt32")
    r32 = S.sb([P, 1024], F32, "r32")
    rtmp = S.sb([P, 1024], F32, "rtmp")
    rqk = S.sb([P, 1024], BF16, "rqk")
    kdec = S.sb([P, 512], BF16, "kdec")
    rvb = S.sb([P, 512], BF16, "rvb")
    srg = S.sb([P, 512], F32, "srg")
    rqT = S.sb([P, 512], BF16, "rqT")
    rkT = S.sb([P, 512], BF16, "rkT")
    qsT = S.sb([P, 512], BF16, "qsT")
    attm = S.sb([P, 512], BF16, "attm")
    o32 = S.sb([P, 512], F32, "o32")
    retb = S.sb([P, 512], BF16, "retb")
    onrm = S.sb([P, 512], F32, "onrm")
    cs = [S.sb([P, 128], F32, f"cs{i}") for i in range(2)]

    pA = [nc.alloc_psum_tensor(f"pA{i}", [P, 512], F32) for i in range(4)]
    pT = [nc.alloc_psum_tensor(f"pT{i}", [P, 1024], BF16) for i in range(2)]
    pR = [nc.alloc_psum_tensor(f"pR{i}", [P, 512], F32) for i in range(2)]

    def rms_cols(eng_src_ap, ngrp, stcol):
        pass

    def rsqrt_cols(c0, c1, tok, add_eps=True):
        S.op("act", lambda e: e.activation(st[:, c0:c1], st[:, c0:c1], AF.Ln, bias=epsc[:, 0:1] if add_eps else 0.0),
             r=[tok, "epsc"], w=[tok])
        S.op("act", lambda e: e.activation(st[:, c0:c1], st[:, c0:c1], AF.Exp, scale=-0.5), r=[tok], w=[tok])

    pa_i = [0]

    def next_pa():
        i = pa_i[0] % 4
        pa_i[0] += 1
        return i

    pt_i = [0]

    def next_pt():
        i = pt_i[0] % 2
        pt_i[0] += 1
        return i

    for t in range(NT + 1):
        samp = (t == NT)
        nv = DB if samp else P
        xb = t % 2
        if samp:
            S.op("pool", lambda e, xb=xb: e.memset(xt[xb][:], 0.0), w=[f"xt{xb}"])
            S.dma("sp", xt[xb][0:DB, :], din["x_sample"], w=[f"xt{xb}"], key=f"xt{xb}")
            S.dma("sp", cs[xb][:], din["cossin"][SEQ:SEQ + P, :], w=[f"cs{xb}"], key=f"xt{xb}")
        else:
            S.dma("sp", xt[xb][:], din["x_prompt"][t * P:(t + 1) * P, :], w=[f"xt{xb}"], key=f"xt{xb}")
            S.dma("sp", cs[xb][:], din["cossin"][t * P:(t + 1) * P, :], w=[f"cs{xb}"], key=f"xt{xb}")
        S.op("act", lambda e, xb=xb: e.activation(junk[:], xt[xb][:], AF.Square, scale=1.0 / 32.0,
                                                   accum_out=st[:, 0:1]), r=[f"xt{xb}"], w=["junk", "st0"])
        rsqrt_cols(0, 1, "st0")
        S.op("act", lambda e, xb=xb: e.activation(xn[:], xt[xb][:], AF.Copy, scale=st[:, 0:1]),
             r=[f"xt{xb}", "st0"], w=["xn"])
        for hf in range(2):
            pi = next_pt()
            for k4 in range(4):
                kc = hf * 4 + k4
                S.op("pe", lambda e, pi=pi, k4=k4, kc=kc: e.transpose(
                    pT[pi][:, k4 * 128:(k4 + 1) * 128], xn[:, kc * 128:(kc + 1) * 128], identb[:]),
                    r=["xn", "identb"], w=[f"pT{pi}"])
            S.op("dve", lambda e, pi=pi, hf=hf, xb=xb: e.tensor_copy(
                xnT[xb][:, hf * 4:(hf + 1) * 4, :].rearrange("p a b -> p (a b)"), pT[pi][:, 0:512]),
                r=[f"pT{pi}"], w=[f"xnT{xb}_{hf}"])
        XN = [f"xnT{xb}_0", f"xnT{xb}_1"]

        def proj(c0, c1):
            pi = next_pa()
            for kc in range(8):
                S.op("pe", lambda e, pi=pi, kc=kc, c0=c0, c1=c1: e.matmul(
                    pA[pi][:, 0:c1 - c0], xnT[xb][:, kc, :], win[:, kc, c0:c1], start=(kc == 0), stop=(kc == 7)),
                    r=XN + [f"win{kc}"], w=[f"pA{pi}"])
            return pi

        pi = proj(0, 512)
        S.op("act", lambda e, pi=pi: e.activation(sq[:], pA[pi][:], AF.Square, scale=0.125),
             r=[f"pA{pi}"], w=["sq"])
        S.op("dve", lambda e: e.tensor_reduce(st[:, 2:10], sq[:].rearrange("p (h d) -> p h d", d=64), AX.X, ALU.add),
             r=["sq"], w=["st2"])
        rsqrt_cols(2, 10, "st2")
        S.op("dve", lambda e, pi=pi: e.tensor_tensor(
            q32[:].rearrange("p (h d) -> p h d", d=64), pA[pi][:].rearrange("p (h d) -> p h d", d=64),
            st[:, 2:10].unsqueeze(2).broadcast_to([P, 8, 64]), ALU.mult), r=[f"pA{pi}", "st2"], w=["q32"])
        S.op("dve", lambda e: e.tensor_tensor(
            qbf[:].rearrange("p (h d) -> p h d", d=64), q32[:].rearrange("p (h d) -> p h d", d=64),
            qnw[:].unsqueeze(1).broadcast_to([P, 8, 64]), ALU.mult), r=["q32", "qnw"], w=["qbf"])
        pi2 = next_pt()
        for j in range(4):
            S.op("pe", lambda e, pi2=pi2, j=j: e.transpose(
                pT[pi2][:, j * 128:(j + 1) * 128], qbf[:, j * 128:(j + 1) * 128], identb[:]),
                r=["qbf", "identb"], w=[f"pT{pi2}"])
        S.op("act", lambda e, pi2=pi2, t=t: e.copy(QT[:, t, :, :].rearrange("p a b -> p (a b)"), pT[pi2][:, 0:512]),
             r=[f"pT{pi2}"], w=[f"QT{t}"])

        kb = t % 2
        pi = proj(512, 1024)
        S.op("act", lambda e, pi=pi, kb=kb: e.copy(kv32[kb][:], pA[pi][:]), r=[f"pA{pi}"], w=[f"kv32{kb}"])
        pj = proj(1024, 1304)
        S.op("act", lambda e, pj=pj, kb=kb: e.copy(kw32[kb][:], pA[pj][:, 0:256]), r=[f"pA{pj}"], w=[f"kw32{kb}"])
        S.op("act", lambda e, pj=pj: e.activation(gt32[:], pA[pj][:, 256:280], AF.Exp, scale=-1.0),
             r=[f"pA{pj}"], w=["gt32"])
        S.op("dve", lambda e: e.tensor_scalar(gt32[:], gt32[:], 1.0, None, ALU.add), r=["gt32"], w=["gt32"])
        S.op("dve", lambda e, t=t: e.reciprocal(gates[:, t, :], gt32[:]), r=["gt32"], w=[f"gates{t}"])
        for (buf, c0, tok, stc, kcol) in ((kv32[kb], 256, f"kv32{kb}", 10, 0), (kw32[kb], 0, f"kw32{kb}", 12, 128)):
            S.op("dve", lambda e, buf=buf, c0=c0: e.tensor_tensor(sq[:, 0:128], buf[:, c0:c0 + 128], buf[:, c0:c0 + 128], ALU.mult),
                 r=[tok], w=["sq"])
            S.op("dve", lambda e, stc=stc: e.tensor_reduce(st[:, stc:stc + 2], sq[:, 0:128].rearrange("p (g d) -> p g d", d=64), AX.X, ALU.add),
                 r=["sq"], w=[f"st{stc}"])
            S.op("dve", lambda e, stc=stc: e.tensor_scalar(st[:, stc:stc + 2], st[:, stc:stc + 2], 1.0 / 64.0, EPS, ALU.mult, ALU.add),
                 r=[f"st{stc}"], w=[f"st{stc}"])
            rsqrt_cols(stc, stc + 2, f"st{stc}", add_eps=False)
            S.op("dve", lambda e, buf=buf, c0=c0, stc=stc: e.tensor_tensor(
                buf[:, c0:c0 + 128].rearrange("p (g d) -> p g d", d=64), buf[:, c0:c0 + 128].rearrange("p (g d) -> p g d", d=64),
                st[:, stc:stc + 2].unsqueeze(2).broadcast_to([P, 2, 64]), ALU.mult), r=[tok, f"st{stc}"], w=[tok])
            S.op("dve", lambda e, buf=buf, c0=c0: e.tensor_tensor(
                buf[:, c0:c0 + 128].rearrange("p (g d) -> p g d", d=64), buf[:, c0:c0 + 128].rearrange("p (g d) -> p g d", d=64),
                knw[:].unsqueeze(1).broadcast_to([P, 2, 64]), ALU.mult), r=[tok, "knw"], w=[tok])
            S.op("dve", lambda e, buf=buf, c0=c0, kcol=kcol: e.tensor_copy(kbf[:, kcol:kcol + 128], buf[:, c0:c0 + 128]),
                 r=[tok], w=["kbf"])
        if not samp:
            rows = slice(t * P, (t + 1) * P)
            S.dma("pool", dout["p_cmp_k"][rows, :], kv32[kb][:, 0:128], r=[f"kv32{kb}"], key=f"okv{kb}")
            S.dma("pool", dout["p_cmp_v"][rows, :], kv32[kb][:, 128:256], r=[f"kv32{kb}"], key=f"okv{kb}")
            S.dma("pool", dout["p_slc_k"][rows, :], kv32[kb][:, 256:384], r=[f"kv32{kb}"], key=f"okv{kb}")
            S.dma("pool", dout["p_slc_v"][rows, :], kv32[kb][:, 384:512], r=[f"kv32{kb}"], key=f"okv{kb}")
            if t >= NT - 4:
                wr = slice((t - (NT - 4)) * P, (t - (NT - 4) + 1) * P)
                S.dma("pool", dout["p_win_k"][wr, :], kw32[kb][:, 0:128], r=[f"kw32{kb}"], key=f"okw{kb}")
                S.dma("pool", dout["p_win_v"][wr, :], kw32[kb][:, 128:256], r=[f"kw32{kb}"], key=f"okw{kb}")
        else:
            S.dma("pool", dout["s_cmp_k"], kv32[kb][0:DB, 0:128], r=[f"kv32{kb}"], key=f"okv{kb}")
            S.dma("pool", dout["s_cmp_v"], kv32[kb][0:DB, 128:256], r=[f"kv32{kb}"], key=f"okv{kb}")
            S.dma("pool", dout["s_slc_k"], kv32[kb][0:DB, 256:384], r=[f"kv32{kb}"], key=f"okv{kb}")
            S.dma("pool", dout["s_slc_v"], kv32[kb][0:DB, 384:512], r=[f"kv32{kb}"], key=f"okv{kb}")
            for b in range(DB):
                S.dma("pool", dout["s_win_k"][b, 0:511, :], din["cache_win_k"][b, 1:512, :], w=[f"swk{b}"], key="owin")
                S.dma("pool", dout["s_win_v"][b, 0:511, :], din["cache_win_v"][b, 1:512, :], w=[f"swv{b}"], key="owin")
                S.dma("pool", dout["s_win_k"][b, 511:512, :], kw32[kb][b:b + 1, 0:128], r=[f"kw32{kb}"], w=[f"swk{b}"], key=f"okw{kb}")
                S.dma("pool", dout["s_win_v"][b, 511:512, :], kw32[kb][b:b + 1, 128:256], r=[f"kw32{kb}"], w=[f"swv{b}"], key=f"okw{kb}")
        if not samp:
            pk = next_pa()
            S.op("pe", lambda e, pk=pk, kb=kb: e.transpose(pA[pk][:, 0:128], kv32[kb][:, 0:128], ident[:]),
                 r=[f"kv32{kb}", "ident"], w=[f"pA{pk}"])
            S.op("pe", lambda e, pk=pk, kb=kb: e.transpose(pA[pk][:, 128:256], kv32[kb][:, 128:256], ident[:]),
                 r=[f"kv32{kb}", "ident"], w=[f"pA{pk}"])
            S.op("act", lambda e, pk=pk, t=t: e.copy(kcT[:, t * P:(t + 1) * P], pA[pk][:, 0:128]), r=[f"pA{pk}"], w=[f"kcT{t}"])
            S.op("act", lambda e, pk=pk, t=t: e.copy(vcT[:, t * P:(t + 1) * P], pA[pk][:, 128:256]), r=[f"pA{pk}"], w=[f"vcT{t}"])
            pi2 = next_pt()
            S.op("pe", lambda e, pi2=pi2: e.transpose(pT[pi2][:, 0:128], kbf[:, 0:128], identb[:]),
                 r=["kbf", "identb"], w=[f"pT{pi2}"])
            S.op("pe", lambda e, pi2=pi2: e.transpose(pT[pi2][:, 128:256], kbf[:, 128:256], identb[:]),
                 r=["kbf", "identb"], w=[f"pT{pi2}"])
            S.op("act", lambda e, pi2=pi2, t=t: e.copy(KsT[:, t * P:(t + 1) * P], pT[pi2][:, 0:128]),
                 r=[f"pT{pi2}"], w=[f"KsT{t}"])
            S.op("act", lambda e, pi2=pi2, t=t: e.copy(KwT[:, t * P:(t + 1) * P], pT[pi2][:, 128:256]),
                 r=[f"pT{pi2}"], w=[f"KwT{t}"])
            S.op("pool", lambda e, t=t: e.memset(Vs[:, t, :, 64:65], 1.0), w=[f"Vs{t}"])
            S.op("pool", lambda e, t=t: e.memset(Vw[:, t, :, 64:65], 1.0), w=[f"Vw{t}"])
            S.op("pool", lambda e, t=t, kb=kb: e.tensor_copy(Vs[:, t, :, 0:64], kv32[kb][:, 384:512].rearrange("p (g d) -> p g d", d=64)),
                 r=[f"kv32{kb}"], w=[f"Vs{t}"])
            S.op("pool", lambda e, t=t, kb=kb: e.tensor_copy(Vw[:, t, :, 0:64], kw32[kb][:, 128:256].rearrange("p (g d) -> p g d", d=64)),
                 r=[f"kw32{kb}"], w=[f"Vw{t}"])

        pi = proj(1304, 1816)
        S.op("act", lambda e, pi=pi: e.copy(r32[:, 0:512], pA[pi][:]), r=[f"pA{pi}"], w=["r32a"])
        pi = proj(1816, 2328)
        S.op("act", lambda e, pi=pi: e.copy(r32[:, 512:1024], pA[pi][:]), r=[f"pA{pi}"], w=["r32b"])
        x4 = r32[:].rearrange("p (a two d) -> p a two d", two=2, d=64)
        t4 = rtmp[:].rearrange("p (a two d) -> p a two d", two=2, d=64)
        o4 = rqk[:].rearrange("p (a two d) -> p a two d", two=2, d=64)
        cosb = cs[xb][:, 0:64].unsqueeze(1).broadcast_to([P, 8, 64])
        sinb = cs[xb][:, 64:128].unsqueeze(1).broadcast_to([P, 8, 64])
        RR = ["r32a", "r32b", f"cs{xb}"]
        S.op("pool", lambda e: e.tensor_tensor(t4[:, :, 0, :], x4[:, :, 0, :], cosb, ALU.mult), r=RR, w=["rt0"])
        S.op("pool", lambda e: e.tensor_tensor(t4[:, :, 1, :], x4[:, :, 1, :], sinb, ALU.mult), r=RR, w=["rt1"])
        S.op("pool", lambda e: e.tensor_tensor(o4[:, :, 0, :], t4[:, :, 0, :], t4[:, :, 1, :], ALU.subtract),
             r=["rt0", "rt1"], w=["rqk0"])
        S.op("dve", lambda e: e.tensor_tensor(x4[:, :, 0, :], x4[:, :, 0, :], sinb, ALU.mult), r=RR + ["rt0"], w=["r32a2"])
        S.op("dve", lambda e: e.tensor_tensor(x4[:, :, 1, :], x4[:, :, 1, :], cosb, ALU.mult), r=RR + ["rt1", "r32a2"], w=["r32b2"])
        S.op("dve", lambda e: e.tensor_tensor(o4[:, :, 1, :], x4[:, :, 0, :], x4[:, :, 1, :], ALU.add),
             r=["r32a2", "r32b2"], w=["rqk1"])
        pi = proj(2328, 2840)
        S.op("act", lambda e, pi=pi: e.copy(rvb[:], pA[pi][:]), r=[f"pA{pi}"], w=["rvb"])
        pi = proj(2840, INC)
        S.op("act", lambda e, pi=pi: e.activation(srg[:], pA[pi][:], AF.Exp, scale=-1.0), r=[f"pA{pi}"], w=["srg"])
        S.op("dve", lambda e: e.tensor_scalar(srg[:], srg[:], 1.0, None, ALU.add), r=["srg"], w=["srg"])
        S.op("dve", lambda e: e.reciprocal(srg[:], srg[:]), r=["srg"], w=["srg"])
        S.op("dve", lambda e, pi=pi: e.tensor_tensor(srg[:], srg[:], pA[pi][:], ALU.mult),
             r=["srg", f"pA{pi}"], w=["srg"])

        RQK = ["rqk0", "rqk1"]
        for half, dstT, nm in ((0, rqT, "rqT"), (1, rkT, "rkT")):
            pi2 = next_pt()
            for h in range(4):
                S.op("pe", lambda e, pi2=pi2, h=h, half=half: e.transpose(
                    pT[pi2][:, h * 128:(h + 1) * 128], rqk[:, half * 512 + h * 128: half * 512 + (h + 1) * 128], identb[:]),
                    r=RQK + ["identb"], w=[f"pT{pi2}"])
            S.op("act", lambda e, pi2=pi2, dstT=dstT: e.copy(dstT[:], pT[pi2][:, 0:512]), r=[f"pT{pi2}"], w=[nm])
        if samp:
            stS = S.sb([P, DB, 512], F32, "stS")
            kmb = S.sb([P, 512], BF16, "kmb")
            for b in range(DB):
                S.dma("sp", stS[:, b, :].rearrange("p (h e) -> p h e", e=128), din["state_ret"][b].rearrange("h d e -> d h e"),
                      w=[f"stS{b}"], key="stS")
            S.op("pool", lambda e, kb=kb: e.tensor_copy(knewb[:], kv32[kb][:, 256:384]), r=[f"kv32{kb}"], w=["knewb"])
            S.op("pool", lambda e, kb=kb: e.tensor_copy(Vn[:], kv32[kb][:, 384:512].rearrange("p (g d) -> p g d", d=64)),
                 r=[f"kv32{kb}"], w=["Vn"])
            ohrow = S.sb([P, DB, 128], F32, "ohrow")
            stSb = S.sb([P, DB, 512], BF16, "stSb")
            qmk = S.sb([P, 512], BF16, "qmk")
            qk4 = S.sb([P, 8], F32, "qk4")
            for b in range(DB):
                S.dma("sp", ohrow[:, b, :], din["ident"][b:b + 1, :].partition_broadcast(P), w=["ohrow"], key="ohrow")
                S.op("pool", lambda e, b=b: e.tensor_copy(stSb[:, b, :], stS[:, b, :]), r=[f"stS{b}"], w=[f"stSb{b}"])
            for b in range(DB):
                S.op("dve", lambda e, b=b: e.tensor_tensor(
                    qmk[:].rearrange("p (h t) -> p h t", t=128), rqT[:].rearrange("p (h t) -> p h t", t=128),
                    ohrow[:, b, :].unsqueeze(1).broadcast_to([P, 4, 128]), ALU.mult), r=["rqT", "ohrow"], w=["qmk"])
                for h in range(4):
                    hs = slice(h * 128, (h + 1) * 128)
                    S.op("pe", lambda e, hs=hs, b=b: e.matmul(pA[b % 4][:, hs] if False else pR[0][:, hs], qmk[:, hs], stSb[:, b, hs],
                                                              start=True, stop=True), r=["qmk", f"stSb{b}"], w=["pR0"])
                if b == 0:
                    S.op("dve", lambda e: e.tensor_copy(o32[:], pR[0][:]), r=["pR0"], w=["o32"])
                else:
                    S.op("dve", lambda e: e.tensor_tensor(o32[:], o32[:], pR[0][:], ALU.add), r=["pR0", "o32"], w=["o32"])
            S.op("dve", lambda e: e.tensor_tensor(
                o32[:].rearrange("p (h d) -> p h d", d=128), o32[:].rearrange("p (h d) -> p h d", d=128),
                rmisc[:, 8:12].unsqueeze(2).broadcast_to([P, 4, 128]), ALU.mult), r=["o32", "rmisc"], w=["o32"])
            S.op("dve", lambda e: e.tensor_tensor(rtmp[:, 0:512], rqk[:, 0:512], rqk[:, 512:1024], ALU.mult), r=RQK, w=["rtq"])
            S.op("dve", lambda e: e.tensor_reduce(qk4[:, 0:4], rtmp[:, 0:512].rearrange("p (h d) -> p h d", d=128), AX.X, ALU.add),
                 r=["rtq"], w=["qk4"])
            S.op("dve", lambda e: e.tensor_tensor(
                rtmp[:, 0:512].rearrange("p (h d) -> p h d", d=128), rvb[:].rearrange("p (h d) -> p h d", d=128),
                qk4[:, 0:4].unsqueeze(2).broadcast_to([P, 4, 128]), ALU.mult), r=["rvb", "qk4", "rtq"], w=["rtq"])
            S.op("dve", lambda e: e.tensor_tensor(o32[:], o32[:], rtmp[:, 0:512], ALU.add), r=["o32", "rtq"], w=["o32"])
            for b in range(DB):
                S.op("dve", lambda e, b=b: e.tensor_scalar(kmb[:], rqk[:, 512:1024], ident[:, b:b + 1], None, ALU.mult),
                     r=RQK + ["ident"], w=["kmb"])
                for h in range(4):
                    hs = slice(h * 128, (h + 1) * 128)
                    S.op("pe", lambda e, hs=hs: e.matmul(pR[1][:, hs], kmb[:, hs], rvb[:, hs], start=True, stop=True),
                         r=["kmb", "rvb"], w=["pR1"])
                S.op("dve", lambda e, b=b: e.tensor_tensor(
                    stS[:, b, :].rearrange("p (h d) -> p h d", d=128), stS[:, b, :].rearrange("p (h d) -> p h d", d=128),
                    rmisc[:, 8:12].unsqueeze(2).broadcast_to([P, 4, 128]), ALU.mult), r=[f"stS{b}", "rmisc"], w=[f"stS{b}"])
                S.op("dve", lambda e, b=b: e.tensor_tensor(stS[:, b, :], stS[:, b, :], pR[1][:], ALU.add),
                     r=[f"stS{b}", "pR1"], w=[f"stS{b}"])
                S.dma("pool", dout["s_ret"][b].rearrange("h d e -> d h e"), stS[:, b, :].rearrange("p (h e) -> p h e", e=128),
                      r=[f"stS{b}"], key="oret")
            osrc, OSRC = o32, "o32"
        else:
            osrc, OSRC = pR[0], "pR0"
        if not samp:
            S.op("pool", lambda e: e.tensor_tensor(
                kdec[:].rearrange("p (h d) -> p h d", d=128), rqk[:, 512:1024].rearrange("p (h d) -> p h d", d=128),
                rmisc[:, 0:4].unsqueeze(2).broadcast_to([P, 4, 128]), ALU.mult), r=RQK + ["rmisc"], w=["kdec"])
            S.op("pool", lambda e: e.tensor_tensor(qsT[:], rqT[:], cq[:], ALU.mult), r=["rqT", "cq"], w=["qsT"])
            pa = next_pa()
            for h in range(4):
                S.op("pe", lambda e, pa=pa, h=h: e.matmul(pA[pa][:, h * 128:(h + 1) * 128], rkT[:, h * 128:(h + 1) * 128],
                                                           rqT[:, h * 128:(h + 1) * 128], start=True, stop=True),
                     r=["rkT", "rqT"], w=[f"pA{pa}"])
            S.op("dve", lambda e, pa=pa: e.tensor_tensor(attm[:], pA[pa][:], decT[:], ALU.mult),
                 r=[f"pA{pa}", "decT"], w=["attm"])
            po = 0
            for h in range(4):
                hs = slice(h * 128, (h + 1) * 128)
                S.op("pe", lambda e, hs=hs: e.matmul(pR[0][:, hs], attm[:, hs], rvb[:, hs], start=True, stop=(t == 0)),
                     r=["attm", "rvb"], w=["pR0"])
                if t > 0:
                    S.op("pe", lambda e, hs=hs: e.matmul(pR[0][:, hs], qsT[:, hs], Sbf[:, hs], start=False, stop=True),
                         r=["qsT", "Sbf"], w=["pR0"])
            for h in range(4):
                hs = slice(h * 128, (h + 1) * 128)
                S.op("pe", lambda e, hs=hs: e.matmul(pR[1][:, hs], kdec[:, hs], rvb[:, hs], start=True, stop=True),
                     r=["kdec", "rvb"], w=["pR1"])
            if t == 0:
                S.op("dve", lambda e: e.tensor_copy(Sst[:], pR[1][:]), r=["pR1"], w=["Sst"])
            else:
                S.op("dve", lambda e: e.tensor_tensor(
                    Sst[:].rearrange("p (h d) -> p h d", d=128), Sst[:].rearrange("p (h d) -> p h d", d=128),
                    rmisc[:, 4:8].unsqueeze(2).broadcast_to([P, 4, 128]), ALU.mult), r=["Sst", "rmisc"], w=["Sst"])
                S.op("dve", lambda e: e.tensor_tensor(Sst[:], Sst[:], pR[1][:], ALU.add), r=["Sst", "pR1"], w=["Sst"])
            S.op("pool", lambda e: e.tensor_copy(Sbf[:], Sst[:]), r=["Sst"], w=["Sbf"])
            if t == NT - 1:
                S.dma("pool", dout["p_ret"].rearrange("h d e -> d h e"), Sst[:].rearrange("p (h e) -> p h e", e=128),
                      r=["Sst"], key="oret")

        S.op("act", lambda e: e.activation(sq[:], osrc[:], AF.Square), r=[OSRC], w=["sq"])
        S.op("dve", lambda e: e.tensor_reduce(st[:, 14:18], sq[:].rearrange("p (h d) -> p h d", d=128), AX.X, ALU.add),
             r=["sq"], w=["st14"])
        S.op("dve", lambda e: e.tensor_reduce(st[:, 18:22], osrc[:].rearrange("p (h d) -> p h d", d=128), AX.X, ALU.add),
             r=[OSRC], w=["st18"])
        S.op("dve", lambda e: e.tensor_scalar(st[:, 18:22], st[:, 18:22], 1.0 / 128.0, None, ALU.mult), r=["st18"], w=["st18"])
        S.op("dve", lambda e: e.tensor_tensor(st[:, 22:26], st[:, 18:22], st[:, 18:22], ALU.mult), r=["st18"], w=["st22"])
        S.op("dve", lambda e: e.scalar_tensor_tensor(st[:, 14:18], st[:, 14:18], 1.0 / 128.0, st[:, 22:26], ALU.mult, ALU.subtract),
             r=["st14", "st22"], w=["st14"])
        rsqrt_cols(14, 18, "st14")
        S.op("dve", lambda e: e.scalar_tensor_tensor(st[:, 22:26], st[:, 18:22], -1.0, st[:, 14:18], ALU.mult, ALU.mult),
             r=["st18", "st14"], w=["st22"])
        S.op("dve", lambda e: e.tensor_tensor(
            onrm[:].rearrange("p (h d) -> p h d", d=128), osrc[:].rearrange("p (h d) -> p h d", d=128),
            st[:, 14:18].unsqueeze(2).broadcast_to([P, 4, 128]), ALU.mult), r=[OSRC, "st14"], w=["onrm"])
        S.op("dve", lambda e: e.tensor_tensor(
            onrm[:].rearrange("p (h d) -> p h d", d=128), onrm[:].rearrange("p (h d) -> p h d", d=128),
            st[:, 22:26].unsqueeze(2).broadcast_to([P, 4, 128]), ALU.add), r=["onrm", "st22"], w=["onrm"])
        S.op("pool", lambda e: e.tensor_tensor(retb[:], onrm[:], srg[:], ALU.mult), r=["onrm", "srg"], w=["retb"])
        pi2 = next_pt()
        for h in range(4):
            S.op("pe", lambda e, pi2=pi2, h=h: e.transpose(pT[pi2][:, h * 128:(h + 1) * 128], retb[:, h * 128:(h + 1) * 128], identb[:]),
                 r=["retb", "identb"], w=[f"pT{pi2}"])
        for h in range(4):
            S.op("act", lambda e, pi2=pi2, h=h, t=t: e.copy(catT[:, 4 + h, t * P:(t + 1) * P], pT[pi2][:, h * 128:(h + 1) * 128]),
                 r=[f"pT{pi2}"], w=[f"catT{t}"])

    S.barrier()
    S.release(mP2)
    rbb = S.sb([32, 8], BF16, "rbb")
    rb32 = S.sb([32, 8], F32, "rb32")
    Jb = S.sb([P, 128], BF16, "Jb")
    LT = S.sb([P, 128], BF16, "LT")
    Eoh = S.sb([32, 2048], BF16, "Eoh")
    selA = S.sb([P, 256], F32, "selA")
    selB = S.sb([P, 256], F32, "selB")
    Wimp = S.sb([P, 32], BF16, "Wimp")
    Wtab = [S.sb([P, 4, 4, 128], BF16, f"Wtab{g}") for g in range(2)]
    WcT = [S.sb([P, 16, 4, 128], BF16, f"WcT{g}") for g in range(2)]
    selT = [S.sb([32, 8, 128], BF16, f"selT{g}") for g in range(2)]
    for nm, tl in (("Jb", Jb), ("LT", LT), ("Eoh", Eoh), ("selA", selA), ("selB", selB), ("Wimp", Wimp)):
        S.dma("sp", tl[:], din[nm], w=[nm], key="cB")
    S.dma("sp", rb32[:], din["rel_bias"], w=["rb32"], key="cB")
    S.op("dve", lambda e: e.tensor_copy(rbb[:], rb32[:]), r=["rb32"], w=["rbb"])

    mB = S.mark()
    ohA = S.sb([32, 4111], BF16, "ohA")
    S.dma("sp", ohA[:], din["ohA"], w=["ohA"], key="cB")
    tb32 = S.sb([8, 4111], F32, "tb32")
    vA = S.sb([8, 4111], F32, "vA")
    S.dma("sp", vA[:], din["validA"], w=["vA"], key="cB")
    for ci in range(9):
        c0 = ci * 512
        c1 = min(4111, c0 + 512)
        pa = next_pa()
        S.op("pe", lambda e, pa=pa, c0=c0, c1=c1: e.matmul(pA[pa][0:8, 0:c1 - c0], rbb[:, :], ohA[:, c0:c1], start=True, stop=True),
             r=["rbb", "ohA"], w=[f"pA{pa}"])
        S.op("act", lambda e, pa=pa, c0=c0, c1=c1: e.activation(tb32[:, c0:c1], pA[pa][0:8, 0:c1 - c0], AF.Exp),
             r=[f"pA{pa}"], w=["tb32"])
    S.op("dve", lambda e: e.tensor_tensor(tb32[:], tb32[:], vA[:], ALU.mult), r=["tb32", "vA"], w=["tb32"])
    S.dma("sp", tblA_h.ap(), tb32[:], r=["tb32"], w=["tblA"], key="tblA")
    Hs = S.sb([P, 640], F32, "Hs")
    Hsb = S.sb([P, 640], BF16, "Hsb")
    Hc = S.sb([P, 2048], F32, "Hc")
    Hcb = S.sb([P, 2048], BF16, "Hcb")
    for h in range(8):
        g, j = h // 4, h % 4
        S.dma("sp", Hs[:], bass.AP(tblA_h, h * 4111 + 1936, [[1, 128], [1, 640]]), r=["tblA"], w=["Hs"], key="Hs")
        S.dma("sp", Hc[:].rearrange("p (a b) -> p a b", b=128), bass.AP(tblA_h, h * 4111, [[16, 128], [128, 16], [1, 128]]),
              r=["tblA"], w=["Hc"], key="Hc")
        S.op("dve", lambda e: e.tensor_copy(Hsb[:], Hs[:]), r=["Hs"], w=["Hsb"])
        S.op("pool", lambda e: e.tensor_copy(Hcb[:], Hc[:]), r=["Hc"], w=["Hcb"])
        pa = next_pa()
        S.op("pe", lambda e, pa=pa: e.matmul(pA[pa][:, 0:384], Jb[:], Hsb[:, 0:384], start=True, stop=True),
             r=["Jb", "Hsb"], w=[f"pA{pa}"])
        S.op("act", lambda e, pa=pa, g=g, j=j: e.copy(Wtab[g][:, 0:3, j, :], pA[pa][:, 0:384].rearrange("p (c q) -> p c q", q=128)),
             r=[f"pA{pa}"], w=[f"Wtab{g}"])
        S.op("dve", lambda e, g=g, j=j: e.tensor_tensor(Wtab[g][:, 3, j, :], Wtab[g][:, 2, j, :], LT[:], ALU.mult),
             r=[f"Wtab{g}", "LT"], w=[f"Wtab{g}"])
        for q4 in range(4):
            pa = next_pa()
            S.op("pe", lambda e, pa=pa, q4=q4: e.matmul(pA[pa][:], Jb[:], Hcb[:, q4 * 512:(q4 + 1) * 512], start=True, stop=True),
                 r=["Jb", "Hcb"], w=[f"pA{pa}"])
            S.op("act", lambda e, pa=pa, g=g, j=j, q4=q4: e.copy(WcT[g][:, q4 * 4:(q4 + 1) * 4, j, :], pA[pa][:].rearrange("p (a q) -> p a q", q=128)),
                 r=[f"pA{pa}"], w=[f"WcT{g}"])
    S.barrier()
    S.release(mB)
    ckcT = S.sb([P, 128], BF16, "ckcT")
    cvc = S.sb([P, 2, 65], BF16, "cvc")
    Pbuf = S.sb([P, 16, 4, 128], BF16, "Pbuf")
    ebuf = [S.sb([P, 512], BF16, f"ebuf{i}") for i in range(2)]
    PcT = S.sb([P, 4, 128], BF16, "PcT")
    wmb = S.sb([P, 4, 128], BF16, "wmb")
    obuf = S.sb([P, 3, 4, 65], F32, "obuf")
    bst = S.sb([P, 64], F32, "bst")
    prod = S.sb([P, 3, 4, 64], F32, "prod")
    nsa32 = S.sb([P, 512], F32, "nsa32")
    nsab = S.sb([P, 512], BF16, "nsab")
    tU = S.sb([P, 4, 32], F32, "tU")
    sc = S.sb([P, 32], F32, "sc")
    sc2 = S.sb([P, 32], F32, "sc2")
    m8 = S.sb([P, 16], F32, "m8")
    mselb = S.sb([P, 32], BF16, "mselb")
    w1s = S.sb([P, 32, 128], F32, "w1s")
    w1b = S.sb([P, 32, 128], BF16, "w1b")
    w2s = S.sb([P, 64], F32, "w2s")
    w2b = S.sb([P, 64], BF16, "w2b")
    pes = S.sb([64, 32], F32, "pes")
    peb = S.sb([64, 32], BF16, "peb")
    hidb = S.sb([P, 2], F32, "hidb")
    actc = S.sb([P, 128], BF16, "actc")
    cmt = S.sb([P, 128], F32, "cmt")
    ckc32 = S.sb([P, 128], F32, "ckc32")
    ckcb = S.sb([P, 128], BF16, "ckcb")
    S.op("pool", lambda e: e.memset(ckcb[:], 0.0), w=["ckcb"])
    S.op("pool", lambda e: e.memset(cvc[:], 0.0), w=["cvc"])
    S.op("pool", lambda e: e.memset(cvc[:, :, 64:65], 1.0), w=["cvc"])
    for kind, (srcT, SRC) in enumerate(((kcT, "kcT"), (vcT, "vcT"))):
        sfx = "k" if kind == 0 else "v"
        for half in range(2):
            S.dma("sp", w1s[half * 64:(half + 1) * 64, :, :], din["cmp_w1_" + sfx].rearrange("(s d) h -> d s h", d=64),
                  w=["w1s"], key="w1s")
        S.op("dve", lambda e: e.tensor_copy(w1b[:], w1s[:]), r=["w1s"], w=["w1b"])
        S.dma("sp", w2s[:], din["cmp_w2_" + sfx], w=["w2s"], key="w1s")
        S.op("dve", lambda e: e.tensor_copy(w2b[:], w2s[:]), r=["w2s"], w=["w2b"])
        S.dma("sp", pes[:], din["cmp_pe_" + sfx].rearrange("s d -> d s"), w=["pes"], key="w1s", allow_slow_non_contiguous=True)
        S.op("dve", lambda e: e.tensor_copy(peb[:], pes[:]), r=["pes"], w=["peb"])
        pa = next_pa()
        for s in range(32):
            S.op("pe", lambda e, pa=pa, s=s: e.matmul(pA[pa][:, 0:1], w1b[0:64, s, :], peb[:, s:s + 1], start=(s == 0), stop=(s == 31)),
                 r=["w1b", "peb"], w=[f"pA{pa}"])
        S.op("dve", lambda e, pa=pa: e.tensor_copy(hidb[:, 0:1], pA[pa][:, 0:1]), r=[f"pA{pa}"], w=["hidb"])
        S.op("dve", lambda e: e.tensor_scalar(hidb[:, 1:2], hidb[:, 0:1], -1.0, None, ALU.mult), r=["hidb"], w=["hidb"])
        ALLT = [f"{SRC}{t}" for t in range(NT)]
        for g in range(2):
            gp = slice(g * 64, (g + 1) * 64)
            pa = next_pa()
            for s in range(32):
                S.op("pe", lambda e, pa=pa, s=s, gp=gp: e.matmul(pA[pa][:, 0:127], w1b[gp, s, :], srcT[gp, s:s + 2017:16],
                                                              start=(s == 0), stop=(s == 31)), r=ALLT + ["w1b"], w=[f"pA{pa}"])
            S.op("act", lambda e, pa=pa: e.activation(cmt[:, 0:127], pA[pa][:, 0:127], AF.Exp, scale=-1.0, bias=hidb[:, 1:2]),
                 r=[f"pA{pa}", "hidb"], w=["cmt"])
            S.op("dve", lambda e: e.tensor_scalar(cmt[:, 0:127], cmt[:, 0:127], 1.0, None, ALU.add), r=["cmt"], w=["cmt"])
            S.op("dve", lambda e: e.reciprocal(cmt[:, 0:127], cmt[:, 0:127]), r=["cmt"], w=["cmt"])
            S.op("dve", lambda e, pa=pa: e.scalar_tensor_tensor(actc[:, 0:127], pA[pa][:, 0:127], hidb[:, 0:1], cmt[:, 0:127], ALU.add, ALU.mult),
                 r=[f"pA{pa}", "hidb", "cmt"], w=["actc"])
            p2 = next_pa()
            S.op("pe", lambda e, p2=p2: e.matmul(pA[p2][0:127, 0:64], actc[:, 0:127], w2b[:], start=True, stop=True),
                 r=["actc", "w2b"], w=[f"pA{p2}"])
            if kind == 0:
                S.op("act", lambda e, p2=p2: e.activation(cmt[0:127, 0:64], pA[p2][0:127, 0:64], AF.Square, scale=0.125, accum_out=bst[0:127, 0:1]),
                     r=[f"pA{p2}"], w=["cmt", "bst0"])
                S.op("act", lambda e: e.activation(bst[0:127, 0:1], bst[0:127, 0:1], AF.Ln, bias=epsc[0:127, 0:1]), r=["bst0", "epsc"], w=["bst0"])
                S.op("act", lambda e: e.activation(bst[0:127, 0:1], bst[0:127, 0:1], AF.Exp, scale=-0.5), r=["bst0"], w=["bst0"])
                S.op("dve", lambda e, p2=p2: e.tensor_scalar(ckc32[0:127, 0:64], pA[p2][0:127, 0:64], bst[0:127, 0:1], None, ALU.mult),
                     r=[f"pA{p2}", "bst0"], w=["ckc32"])
                S.op("dve", lambda e, g=g: e.tensor_tensor(ckcb[0:127, g * 64:(g + 1) * 64], ckc32[0:127, 0:64], knw[0:127, :], ALU.mult),
                     r=["ckc32", "knw"], w=["ckcb"])
            else:
                S.op("act", lambda e, p2=p2, g=g: e.copy(cvc[0:127, g, 0:64], pA[p2][0:127, 0:64]), r=[f"pA{p2}"], w=["cvc"])
        if kind == 0:
            pi = next_pt()
            S.op("pe", lambda e, pi=pi: e.transpose(pT[pi][:, 0:128], ckcb[:], identb[:]), r=["ckcb", "identb"], w=[f"pT{pi}"])
            S.op("act", lambda e, pi=pi: e.copy(ckcT[:], pT[pi][:, 0:128]), r=[f"pT{pi}"], w=["ckcT"])

    ebi = [0]
    Oc = pA[3]
    for qt in range(NT):
        for g in range(2):
            gp = slice(g * 64, (g + 1) * 64)
            qrhs = QT[gp, qt, :, :].rearrange("p a b -> p (a b)")

            def score_tile(lhsT_ap, rtoks):
                ps = ebi[0] % 2
                S.op("pe", lambda e, ps=ps: e.matmul(pA[ps][:], lhsT_ap, qrhs, start=True, stop=True),
                     r=rtoks + [f"QT{qt}"], w=[f"pA{ps}"])
                eb = ebi[0] % 2
                ebi[0] += 1
                S.op("act", lambda e, ps=ps, eb=eb: e.activation(ebuf[eb][:], pA[ps][:], AF.Exp), r=[f"pA{ps}"], w=[f"ebuf{eb}"])
                return eb

            def pv(obank, oc0, kts, Vt, VTOK, br):
                for j in range(4):
                    for i, kt in enumerate(kts):
                        S.op("pe", lambda e, j=j, kt=kt, i=i: e.matmul(
                            obank[:, oc0 + j * 65: oc0 + (j + 1) * 65], Pbuf[:, kt, j, :], Vt[:, kt, g, :],
                            start=(i == 0), stop=(i == len(kts) - 1)), r=[f"Pb{kt}", f"{VTOK}{kt}"], w=[f"O{br}"])

            eb = score_tile(ckcT[gp, :], ["ckcT"])
            S.op("dve", lambda e, eb=eb: e.tensor_tensor(PcT[:].rearrange("p a b -> p (a b)"), ebuf[eb][:],
                                                          WcT[g][:, qt, :, :].rearrange("p a b -> p (a b)"), ALU.mult),
                 r=[f"ebuf{eb}", f"WcT{g}"], w=["PcT"])
            for j in range(4):
                S.op("pe", lambda e, j=j: e.matmul(Oc[:, j * 65:(j + 1) * 65], PcT[:, j, :], cvc[:, g, :], start=True, stop=True),
                     r=["PcT", "cvc"], w=["O0"])
            S.op("act", lambda e: e.copy(obuf[:, 0, :, :].rearrange("p a b -> p (a b)"), Oc[:, 0:260]), r=["O0"], w=["obuf0"])
            if qt >= 8:
                q8 = qt - 8
                for j in range(4):
                    S.op("pe", lambda e, j=j: e.matmul(Oc[:, 260 + j * 32: 260 + (j + 1) * 32], PcT[:, j, :], Wimp[:], start=True, stop=True),
                         r=["PcT", "Wimp"], w=["OU"])
                S.op("dve", lambda e: e.tensor_scalar(bst[:, 4:8], obuf[:, 0, :, 64], 1e-30, None, ALU.max), r=["obuf0"], w=["bst4"])
                S.op("dve", lambda e: e.reciprocal(bst[:, 4:8], bst[:, 4:8]), r=["bst4"], w=["bst4"])
                S.op("dve", lambda e: e.tensor_tensor(tU[:], Oc[:, 260:388].rearrange("p (j b) -> p j b", b=32),
                                                       bst[:, 4:8].unsqueeze(2).broadcast_to([P, 4, 32]), ALU.mult),
                     r=["OU", "bst4"], w=["tU"])
                S.op("dve", lambda e: e.tensor_reduce(sc[:], tU[:].rearrange("p j b -> p b j"), AX.X, ALU.add), r=["tU"], w=["sc"])
                S.op("dve", lambda e, q8=q8: e.tensor_tensor(sc[:], sc[:], selA[:, q8 * 32:(q8 + 1) * 32], ALU.mult), r=["sc", "selA"], w=["sc"])
                S.op("dve", lambda e, q8=q8: e.tensor_tensor(sc[:], sc[:], selB[:, q8 * 32:(q8 + 1) * 32], ALU.add), r=["sc", "selB"], w=["sc"])
                S.op("dve", lambda e: e.max(m8[:, 0:8], sc[:]), r=["sc"], w=["m8a"])
                S.op("dve", lambda e: e.match_replace(sc2[:], m8[:, 0:8], sc[:], -2.0), r=["sc", "m8a"], w=["sc2"])
                S.op("dve", lambda e: e.max(m8[:, 8:16], sc2[:]), r=["sc2"], w=["m8b"])
                S.op("dve", lambda e: e.tensor_scalar(mselb[:], sc[:], m8[:, 15:16], None, ALU.is_ge), r=["sc", "m8b"], w=["mselb"])
                pi = next_pt()
                S.op("pe", lambda e, pi=pi: e.transpose(pT[pi][0:32, 0:128], mselb[:], identb[:]), r=["mselb", "identb"], w=[f"pT{pi}"])
                S.op("act", lambda e, pi=pi, q8=q8: e.copy(selT[g][:, q8, :], pT[pi][0:32, 0:128]), r=[f"pT{pi}"], w=[f"selT{g}"])
            kts = list(range(qt + 1))
            for kt in kts:
                eb = score_tile(KsT[gp, kt * P:(kt + 1) * P], [f"KsT{kt}"])
                cls = min(qt - kt, 2)
                wt = Wtab[g][:, cls, :, :]
                if qt >= 8:
                    S.op("pe", lambda e, kt=kt, q8=qt - 8: e.matmul(pA[2][:, 0:128], Eoh[:, kt * P:(kt + 1) * P], selT[g][:, q8, :], start=True, stop=True),
                         r=["Eoh", f"selT{g}"], w=["pA2"])
                    S.op("dve", lambda e, wt=wt: e.tensor_tensor(wmb[:], wt, pA[2][:, 0:128].unsqueeze(1).broadcast_to([P, 4, 128]), ALU.mult),
                         r=["pA2", f"Wtab{g}"], w=["wmb"])
                    S.op("dve", lambda e, eb=eb, kt=kt: e.tensor_tensor(Pbuf[:, kt, :, :].rearrange("p a b -> p (a b)"), ebuf[eb][:],
                                                                     wmb[:].rearrange("p a b -> p (a b)"), ALU.mult),
                         r=[f"ebuf{eb}", "wmb"], w=[f"Pb{kt}"])
                else:
                    S.op("dve", lambda e, eb=eb, kt=kt, wt=wt: e.tensor_tensor(Pbuf[:, kt, :, :], ebuf[eb][:].rearrange("p (a b) -> p a b", b=128),
                                                                            wt, ALU.mult),
                         r=[f"ebuf{eb}", f"Wtab{g}"], w=[f"Pb{kt}"])
            pv(pR[0], 0, kts, Vs, "Vs", 1)
            S.op("act", lambda e: e.copy(obuf[:, 1, :, :].rearrange("p a b -> p (a b)"), pR[0][:, 0:260]), r=["O1"], w=["obuf1"])
            kts = list(range(max(0, qt - 4), qt + 1))
            for kt in kts:
                eb = score_tile(KwT[gp, kt * P:(kt + 1) * P], [f"KwT{kt}"])
                cls = {0: 0, 1: 1, 2: 2, 3: 2, 4: 3}[qt - kt]
                wt = Wtab[g][:, cls, :, :]
                S.op("dve", lambda e, eb=eb, kt=kt, wt=wt: e.tensor_tensor(Pbuf[:, kt, :, :], ebuf[eb][:].rearrange("p (a b) -> p a b", b=128),
                                                                        wt, ALU.mult),
                     r=[f"ebuf{eb}", f"Wtab{g}"], w=[f"Pb{kt}"])
            pv(pR[1], 0, kts, Vw, "Vw", 2)
            S.op("act", lambda e: e.copy(obuf[:, 2, :, :].rearrange("p a b -> p (a b)"), pR[1][:, 0:260]), r=["O2"], w=["obuf2"])
            OB = ["obuf0", "obuf1", "obuf2"]
            S.op("dve", lambda e: e.tensor_scalar(bst[:, 8:20].rearrange("p (a b) -> p a b", b=4), obuf[:, :, :, 64], 1e-30, None, ALU.max),
                 r=OB, w=["bst8"])
            S.op("dve", lambda e: e.reciprocal(bst[:, 8:20], bst[:, 8:20]), r=["bst8"], w=["bst8"])
            S.op("dve", lambda e: e.tensor_tensor(bst[:, 8:20].rearrange("p (a b) -> p a b", b=4), bst[:, 8:20].rearrange("p (a b) -> p a b", b=4),
                                                   gates[:, qt, :].rearrange("p (a h) -> p a h", h=8)[:, :, 4 * g:4 * g + 4], ALU.mult),
                 r=["bst8", f"gates{qt}"], w=["bst8"])
            S.op("dve", lambda e: e.tensor_tensor(prod[:], obuf[:, :, :, 0:64],
                                                   bst[:, 8:20].rearrange("p (a b) -> p a b", b=4).unsqueeze(3).broadcast_to([P, 3, 4, 64]), ALU.mult),
                 r=OB + ["bst8"], w=["prod"])
            S.op("dve", lambda e: e.tensor_reduce(nsa32[:, g * 256:(g + 1) * 256].rearrange("p (j d) -> p j d", d=64),
                                                   prod[:].rearrange("p a j d -> p j d a"), AX.X, ALU.add),
                 r=["prod"], w=[f"nsa32_{g}"])
        S.op("act", lambda e: e.copy(nsab[:], nsa32[:]), r=["nsa32_0", "nsa32_1"], w=["nsab"])
        pi = next_pt()
        for c4 in range(4):
            S.op("pe", lambda e, pi=pi, c4=c4: e.transpose(pT[pi][:, c4 * 128:(c4 + 1) * 128], nsab[:, c4 * 128:(c4 + 1) * 128], identb[:]),
                 r=["nsab", "identb"], w=[f"pT{pi}"])
        S.op("act", lambda e, pi=pi, qt=qt: e.copy(catT[:, 0:4, qt * P:(qt + 1) * P], pT[pi][:, 0:512].rearrange("p (a b) -> p a b", b=128)),
             r=[f"pT{pi}"], w=[f"catT{qt}"])

    class _StopC(Exception):
        pass
    KC = int(os.environ.get("KC", "9"))
    try:
        if not ENABLE_DECODE_NSA:
            for c4 in range(4):
                S.op("pool", lambda e, c4=c4: e.memset(catT[:, c4, SEQ:SEQ + P], 0.0), w=[f"catT{NT}"])
            raise _StopC()
        S.barrier()
        S.release(mP2)
        pt4 = S.sb([P, DB], I32, "pt4")
        idx8 = S.sb([P, DB, 8], I32, "idx8")
        rbb2 = S.sb([32, 8], BF16, "rbb2")
        Hd2 = S.sb([P, 8], F32, "Hd2")
        Hd2b = S.sb([P, 8], BF16, "Hd2b")
        w0b = S.sb([P, 8], F32, "w0b")
        SelC = S.sb([P, 8, 128], BF16, "SelC")
        SelW = S.sb([P, 4, 128], BF16, "SelW")
        OHt = S.sb([P, 2, 128], F32, "OHt")
        LAB = S.sb([P, 2, 128], BF16, "LAB")
        A2 = S.sb([8, 256], F32, "A2")
        B2 = S.sb([8, 256], F32, "B2")
        Wr = S.sb([P, 2, 8], F32, "Wr")
        onesb = S.sb([P, 128], BF16, "onesb")
        Rm = S.sb([P, 2, 128, 8], BF16, "Rm")
        EBs = S.sb([P, 128, 8], BF16, "EBs")
        EBc = S.sb([P, 8, 8], F32, "EBc")
        EBw = S.sb([P, 4, 8], F32, "EBw")
        EBn = S.sb([P, DB, 8], F32, "EBn")
        for nm, tl in (("SelC", SelC), ("SelW", SelW), ("OHt", OHt), ("LAB", LAB), ("A2", A2), ("B2", B2), ("Wr", Wr)):
            S.dma("sp", tl[:] if nm in ("A2", "B2") else tl[:].rearrange("p a b -> p (a b)"), din[nm], w=[nm], key="cC")
        S.dma("sp", pt4[:], din["page_table"].rearrange("b p -> p b"), w=["pt4"], key="cC", allow_slow_non_contiguous=True)
        S.dma("sp", Hd2[:], bass.AP(tblA_h, 2063, [[1, 128], [4111, 8]]), r=["tblA"], w=["Hd2"], key="cC", allow_slow_non_contiguous=True)
        S.dma("sp", w0b[:], bass.AP(tblA_h, 2063, [[0, 128], [4111, 8]]), r=["tblA"], w=["w0b"], key="cC", allow_slow_non_contiguous=True)
        S.op("dve", lambda e: e.tensor_copy(Hd2b[:], Hd2[:]), r=["Hd2"], w=["Hd2b"])
        S.op("pool", lambda e: e.memset(onesb[:], 1.0), w=["onesb"])
        for r in range(8):
            S.op("dve", lambda e, r=r: e.tensor_scalar(idx8[:, :, r], pt4[:], 8, r, ALU.mult, ALU.add), r=["pt4"], w=["idx8"])
        pa = next_pa()
        for r in range(8):
            S.op("pe", lambda e, pa=pa, r=r: e.matmul(pA[pa][:, r * 8:(r + 1) * 8], SelC[:, r, :], Hd2b[:], start=True, stop=True),
                 r=["SelC", "Hd2b"], w=[f"pA{pa}"])
        S.op("dve", lambda e, pa=pa: e.tensor_copy(EBc[:].rearrange("p a b -> p (a b)"), pA[pa][:, 0:64]), r=[f"pA{pa}"], w=["EBc"])
        pa = next_pa()
        for u in range(4):
            S.op("pe", lambda e, pa=pa, u=u: e.matmul(pA[pa][:, u * 8:(u + 1) * 8], SelW[:, u, :], Hd2b[:], start=True, stop=True),
                 r=["SelW", "Hd2b"], w=[f"pA{pa}"])
        S.op("dve", lambda e, pa=pa: e.tensor_copy(EBw[:].rearrange("p a b -> p (a b)"), pA[pa][:, 0:32]), r=[f"pA{pa}"], w=["EBw"])
        for ab in range(2):
            S.op("dve", lambda e, ab=ab: e.tensor_tensor(Rm[:, ab, :, :], OHt[:, ab, :].unsqueeze(2).broadcast_to([P, 128, 8]),
                                                          Hd2[:].unsqueeze(1).broadcast_to([P, 128, 8]), ALU.mult),
                 r=["OHt", "Hd2"], w=["Rm"])
        for hf in range(2):
            pa = next_pa()
            for ab in range(2):
                S.op("pe", lambda e, pa=pa, ab=ab, hf=hf: e.matmul(
                    pA[pa][:], LAB[:, ab, :], Rm[:, ab, hf * 64:(hf + 1) * 64, :].rearrange("p a b -> p (a b)"),
                    start=(ab == 0), stop=(ab == 1)), r=["LAB", "Rm"], w=[f"pA{pa}"])
            S.op("act", lambda e, pa=pa, hf=hf: e.copy(EBs[:, hf * 64:(hf + 1) * 64, :].rearrange("p a b -> p (a b)"), pA[pa][:]),
                 r=[f"pA{pa}"], w=["EBs"])
        S.op("dve", lambda e: e.tensor_tensor(EBn[:], w0b[:].unsqueeze(1).broadcast_to([P, DB, 8]),
                                               ident[:, 0:DB].unsqueeze(2).broadcast_to([P, DB, 8]), ALU.mult),
             r=["w0b", "ident"], w=["EBn"])

        obd = S.sb([4, 3, DB, 2, 65], F32, "obd")
        S.op("pool", lambda e: e.memset(obd[:], 0.0), w=["obd"])
        mC = S.mark()
        if KC < 1:
            raise _StopC()
        Xg = [S.sb([P, 2048], BF16, f"Xg{i}") for i in range(4)]
        KT = S.sb([P, 128, 128], BF16, "KT")
        w1b = S.sb([P, 32, 128], BF16, "w1b")
        w1s = S.sb([P, 8, 128], F32, "w1s")
        w2s = S.sb([P, 64], F32, "w2s")
        w2b = S.sb([P, 64], BF16, "w2b")
        pes = S.sb([64, 32], F32, "pes")
        peb = S.sb([64, 32], BF16, "peb")
        hidb = S.sb([P, 2], F32, "hidb")
        actd = S.sb([P, 8, 128], BF16, "actd")
        cmt = S.sb([P, 512], F32, "cmt")
        cst = S.sb([P, 64], F32, "cst")
        ck32 = S.sb([P, 8, 2, 64], F32, "ck32")
        ckb = S.sb([P, 8, 2, 64], BF16, "ckb")
        ckcTd = S.sb([P, DB, 8, 128], BF16, "ckcTd")
        cvcd = S.sb([P, DB, 2, 8, 64], BF16, "cvcd")
        xgi = [0]
        pools = {"ck": din["cache_cmp_k"], "cv": din["cache_cmp_v"], "sk": din["cache_slc_k"], "sv": din["cache_slc_v"]}

        def gather(kind, b, r):
            i = xgi[0] % 4
            xgi[0] += 1
            srcp = pools[kind]
            S.dmafn("pool", lambda e, i=i, b=b, r=r, srcp=srcp: e.indirect_dma_start(
                out=Xg[i][:], out_offset=None, in_=srcp,
                in_offset=bass.IndirectOffsetOnAxis(ap=idx8[:, b, r:r + 1], axis=0)), r=["idx8"], w=[f"Xg{i}"], key=f"Xg{i}")
            return i

        evi = [0]

        def build_KT(kind, b):
            for r in range(8):
                i = gather(kind, b, r)
                for hf in range(2):
                    pi = next_pt()
                    for k in range(8):
                        tt = hf * 8 + k
                        S.op("pe", lambda e, pi=pi, k=k, tt=tt, i=i: e.transpose(pT[pi][:, k * 128:(k + 1) * 128], Xg[i][:, tt * 128:(tt + 1) * 128], identb[:]),
                             r=[f"Xg{i}", "identb"], w=[f"pT{pi}"])
                    eng = "act" if evi[0] % 2 == 0 else "dve"
                    evi[0] += 1
                    t0 = r * 16 + hf * 8
                    if eng == "act":
                        S.op("act", lambda e, pi=pi, t0=t0: e.copy(KT[:, t0:t0 + 8, :].rearrange("p a b -> p (a b)"), pT[pi][:, 0:1024]),
                             r=[f"pT{pi}"], w=[f"KT{r}"])
                    else:
                        S.op("dve", lambda e, pi=pi, t0=t0: e.tensor_copy(KT[:, t0:t0 + 8, :].rearrange("p a b -> p (a b)"), pT[pi][:, 0:1024]),
                             r=[f"pT{pi}"], w=[f"KT{r}"])

        for kind, sfx in (("ck", "k"), ("cv", "v")):
            for s8 in range(4):
                for half in range(2):
                    S.dma("sp", w1s[half * 64:(half + 1) * 64, :, :], din["cmp_w1_" + sfx].rearrange("(s d) h -> d s h", d=64)[:, s8 * 8:(s8 + 1) * 8, :],
                          w=["w1s"], key="w1s")
                S.op("dve", lambda e, s8=s8: e.tensor_copy(w1b[:, s8 * 8:(s8 + 1) * 8, :], w1s[:]), r=["w1s"], w=["w1b"])
            S.dma("sp", w2s[:], din["cmp_w2_" + sfx], w=["w2s"], key="w1s")
            S.op("dve", lambda e: e.tensor_copy(w2b[:], w2s[:]), r=["w2s"], w=["w2b"])
            S.dma("sp", pes[:], din["cmp_pe_" + sfx].rearrange("s d -> d s"), w=["pes"], key="w1s", allow_slow_non_contiguous=True)
            S.op("dve", lambda e: e.tensor_copy(peb[:], pes[:]), r=["pes"], w=["peb"])
            pa = next_pa()
            for s in range(32):
                S.op("pe", lambda e, pa=pa, s=s: e.matmul(pA[pa][:, 0:1], w1b[0:64, s, :], peb[:, s:s + 1], start=(s == 0), stop=(s == 31)),
                     r=["w1b", "peb"], w=[f"pA{pa}"])
            S.op("dve", lambda e, pa=pa: e.tensor_copy(hidb[:, 0:1], pA[pa][:, 0:1]), r=[f"pA{pa}"], w=["hidb"])
            S.op("dve", lambda e: e.tensor_scalar(hidb[:, 1:2], hidb[:, 0:1], -1.0, None, ALU.mult), r=["hidb"], w=["hidb"])
            for b in range(DB):
                build_KT(kind, b)
                KTALL = [f"KT{r}" for r in range(8)]
                for g in range(2):
                    gp = slice(g * 64, (g + 1) * 64)
                    for r4 in range(2):
                        pa = next_pa()
                        for rr in range(4):
                            r = r4 * 4 + rr
                            for s in range(32):
                                if s < 16:
                                    rhs = KT[gp, 16 * r + s, :]
                                    ocs = slice(rr * 128, rr * 128 + 128)
                                elif r < 7:
                                    rhs = KT[gp, 16 * (r + 1) + s - 16, :]
                                    ocs = slice(rr * 128, rr * 128 + 128)
                                else:
                                    rhs = KT[gp, s - 16, 1:128]
                                    ocs = slice(rr * 128, rr * 128 + 127)
                                S.op("pe", lambda e, pa=pa, s=s, rhs=rhs, ocs=ocs, gp=gp: e.matmul(
                                    pA[pa][:, ocs], w1b[gp, s, :], rhs, start=(s == 0), stop=(s == 31)),
                                    r=KTALL + ["w1b"], w=[f"pA{pa}"])
                        S.op("act", lambda e, pa=pa: e.activation(cmt[:], pA[pa][:], AF.Exp, scale=-1.0, bias=hidb[:, 1:2]),
                             r=[f"pA{pa}", "hidb"], w=["cmt"])
                        S.op("dve", lambda e: e.tensor_scalar(cmt[:], cmt[:], 1.0, None, ALU.add), r=["cmt"], w=["cmt"])
                        S.op("dve", lambda e: e.reciprocal(cmt[:], cmt[:]), r=["cmt"], w=["cmt"])
                        S.op("dve", lambda e, pa=pa, r4=r4: e.scalar_tensor_tensor(
                            actd[:, r4 * 4:(r4 + 1) * 4, :].rearrange("p a b -> p (a b)"), pA[pa][:], hidb[:, 0:1], cmt[:], ALU.add, ALU.mult),
                            r=[f"pA{pa}", "hidb", "cmt"], w=["actd"])
                    p2 = next_pa()
                    for r in range(8):
                        S.op("pe", lambda e, p2=p2, r=r: e.matmul(pA[p2][:, r * 64:(r + 1) * 64], actd[:, r, :], w2b[:], start=True, stop=True),
                             r=["actd", "w2b"], w=[f"pA{p2}"])
                    if kind == "ck":
                        S.op("act", lambda e, p2=p2: e.activation(cmt[:], pA[p2][:], AF.Square, scale=0.125), r=[f"pA{p2}"], w=["cmt"])
                        S.op("dve", lambda e: e.tensor_reduce(cst[:, 0:8], cmt[:].rearrange("p (r d) -> p r d", d=64), AX.X, ALU.add),
                             r=["cmt"], w=["cst0"])
                        S.op("act", lambda e: e.activation(cst[:, 0:8], cst[:, 0:8], AF.Ln, bias=epsc[:, 0:1]), r=["cst0", "epsc"], w=["cst0"])
                        S.op("act", lambda e: e.activation(cst[:, 0:8], cst[:, 0:8], AF.Exp, scale=-0.5), r=["cst0"], w=["cst0"])
                        S.op("dve", lambda e, p2=p2, g=g: e.tensor_tensor(ck32[:, :, g, :], pA[p2][:].rearrange("p (r d) -> p r d", d=64),
                                                                       cst[:, 0:8].unsqueeze(2).broadcast_to([P, 8, 64]), ALU.mult),
                             r=[f"pA{p2}", "cst0"], w=["ck32"])
                        S.op("dve", lambda e, g=g: e.tensor_tensor(ckb[:, :, g, :], ck32[:, :, g, :], knw[:].unsqueeze(1).broadcast_to([P, 8, 64]), ALU.mult),
                             r=["ck32", "knw"], w=["ckb"])
                    else:
                        S.op("act", lambda e, p2=p2, b=b, g=g: e.copy(cvcd[:, b, g, :, :].rearrange("p a b -> p (a b)"), pA[p2][:]),
                             r=[f"pA{p2}"], w=["cvcd"])
                if kind == "ck":
                    pi = next_pt()
                    for r in range(8):
                        S.op("pe", lambda e, pi=pi, r=r: e.transpose(pT[pi][:, r * 128:(r + 1) * 128], ckb[:, r, :, :].rearrange("p a b -> p (a b)"), identb[:]),
                             r=["ckb", "identb"], w=[f"pT{pi}"])
                    S.op("act", lambda e, pi=pi, b=b: e.copy(ckcTd[:, b, :, :].rearrange("p a b -> p (a b)"), pT[pi][:, 0:1024]),
                         r=[f"pT{pi}"], w=["ckcTd"])

        if KC < 2:
            raise _StopC()
        PcD = S.sb([P, 8, 8], F32, "PcD")
        PcDb = S.sb([P, DB, 8, 8], BF16, "PcDb")
        tot = S.sb([P, 16], F32, "tot")
        pcn = S.sb([P, 8, 8], F32, "pcn")
        pcs = S.sb([P, 2, 8], F32, "pcs")
        pcw = S.sb([P, 2, 2, 8], F32, "pcw")
        shf = S.sb([P, 2], F32, "shf")
        impA = S.sb([P, DB, 2, 2], F32, "impA")
        scD = S.sb([8, 2, 128], F32, "scD")
        scD2 = S.sb([8, 2, 128], F32, "scD2")
        m8d = S.sb([8, 16], F32, "m8d")
        mkD = S.sb([8, 2, 128], F32, "mkD")
        MselD = S.sb([P, 2, 8], F32, "MselD")
        S.op("pool", lambda e: e.memset(shf[:], 0.0), w=["shf"])
        Shb = S.sb([P, 128], BF16, "Shb")
        pc7b = S.sb([P, 2], BF16, "pc7b")
        S.op("pool", lambda e: e.memset(Shb[:], 0.0), w=["Shb"])
        S.op("dve", lambda e: e.tensor_copy(Shb[:, 1:128], ident[:, 0:127]), r=["ident", "Shb"], w=["Shb"])

        if os.environ.get("KC4", "9") < "1":
            raise _StopC()
        qs = S.sb([P, DB, 4], BF16, "qs")
        S.op("dve", lambda e: e.tensor_copy(qs[:].rearrange("p b j -> p j b"), QT[:, NT, :, 0:DB]), r=[f"QT{NT}"], w=[f"QT{NT}"])

        bdq = S.sb([P, DB, 8], BF16, "bdq")
        S.op("pool", lambda e: e.memset(bdq[:], 0.0), w=["bdq"])
        S.op("dve", lambda e: e.tensor_copy(bdq[0:64, :, 0:4], qs[0:64, :, :]), r=[f"QT{NT}", "bdq"], w=["bdq"])
        S.op("dve", lambda e: e.tensor_copy(bdq[64:128, :, 4:8], qs[64:128, :, :]), r=[f"QT{NT}", "bdq"], w=["bdq"])
        if os.environ.get("KC4", "9") < "15":
            raise _StopC()

        for b in range(DB):
            pa = next_pa()
            for r in range(8):
                S.op("pe", lambda e, pa=pa, r=r, b=b: e.matmul(
                    pA[pa][:, r * 8:(r + 1) * 8], ckcTd[:, b, r, :], bdq[:, b, :], start=True, stop=True),
                    r=["ckcTd", "bdq"], w=[f"pA{pa}"])
            if os.environ.get("KC4", "9") < "2":
                raise _StopC()
            S.op("act", lambda e, pa=pa: e.activation(PcD[:].rearrange("p a b -> p (a b)"), pA[pa][:, 0:64], AF.Exp), r=[f"pA{pa}"], w=["PcD"])
            S.op("dve", lambda e: e.tensor_tensor(PcD[:], PcD[:], EBc[:], ALU.mult), r=["PcD", "EBc"], w=["PcD"])
            if os.environ.get("KC3", "9") < "1":
                raise _StopC()
            S.op("dve", lambda e, b=b: e.tensor_copy(PcDb[:, b, :, :], PcD[:]), r=["PcD"], w=[f"PcDb{b}"])
            pa = next_pa()
            S.op("pe", lambda e, pa=pa, b=b: e.matmul(pA[pa][:, 0:64], onesb[:], PcDb[:, b, :, :].rearrange("p a b -> p (a b)"), start=True, stop=True),
                 r=[f"PcDb{b}", "onesb"], w=[f"pA{pa}"])
            S.op("dve", lambda e, pa=pa: e.tensor_reduce(tot[:, 0:8], pA[pa][:, 0:64].rearrange("p (r h) -> p h r", h=8), AX.X, ALU.add),
                 r=[f"pA{pa}"], w=["tot"])
            S.op("dve", lambda e: e.tensor_scalar(tot[:, 0:8], tot[:, 0:8], 1e-30, None, ALU.max), r=["tot"], w=["tot"])
            S.op("dve", lambda e: e.reciprocal(tot[:, 0:8], tot[:, 0:8]), r=["tot"], w=["tot"])
            if os.environ.get("KC3", "9") < "2":
                raise _StopC()
            S.op("dve", lambda e: e.tensor_tensor(pcn[:], PcD[:], tot[:, 0:8].unsqueeze(1).broadcast_to([P, 8, 8]), ALU.mult),
                 r=["PcD", "tot"], w=["pcn"])
            S.op("dve", lambda e: e.tensor_reduce(pcs[:], pcn[:].rearrange("p r (g j) -> p g r j", j=4), AX.X, ALU.add), r=["pcn"], w=["pcs"])
            S.op("dve", lambda e: e.tensor_tensor(pcw[:], pcs[:].unsqueeze(2).broadcast_to([P, 2, 2, 8]),
                                                   Wr[:].unsqueeze(1).broadcast_to([P, 2, 2, 8]), ALU.mult), r=["pcs", "Wr"], w=["pcw"])
            S.op("dve", lambda e, b=b: e.tensor_reduce(impA[:, b, :, :], pcw[:], AX.X, ALU.add), r=["pcw"], w=[f"impA{b}"])
            S.op("dve", lambda e: e.tensor_copy(pc7b[:], pcs[:, :, 7]), r=["pcs"], w=["pc7b"])
            psh = next_pa()
            S.op("pe", lambda e, psh=psh: e.matmul(pA[psh][:, 0:2], Shb[:], pc7b[:], start=True, stop=True), r=["Shb", "pc7b"], w=[f"pA{psh}"])
            S.op("dve", lambda e, psh=psh: e.tensor_copy(shf[:], pA[psh][:, 0:2]), r=[f"pA{psh}"], w=["shf"])
            S.op("dve", lambda e, b=b: e.scalar_tensor_tensor(impA[:, b, :, 0], shf[:], 0.5, impA[:, b, :, 0], ALU.mult, ALU.add),
                 r=["shf", f"impA{b}"], w=[f"impA{b}"])
            if os.environ.get("KC3", "9") < "3":
                raise _StopC()
            for g in range(2):
                for r in range(8):
                    S.op("pe", lambda e, g=g, r=r, b=b: e.matmul(pR[g][0:4, 0:64], PcDb[:, b, r, g * 4:(g + 1) * 4], cvcd[:, b, g, r, :],
                                                               start=(r == 0), stop=(r == 7)), r=[f"PcDb{b}", "cvcd"], w=[f"pR{g}"])
                for r in range(8):
                    S.op("pe", lambda e, g=g, r=r, b=b: e.matmul(pR[g][0:4, 64:65], PcDb[:, b, r, g * 4:(g + 1) * 4], onesb[:, 0:1],
                                                               start=(r == 0), stop=(r == 7)), r=[f"PcDb{b}", "onesb"], w=[f"pR{g}"])
                S.op("act", lambda e, g=g, b=b: e.copy(obd[:, 0, b, g, :], pR[g][0:4, 0:65]), r=[f"pR{g}"], w=["obd"])
        if os.environ.get("KC2", "9") < "1":
            raise _StopC()
        IMPA = [f"impA{b}" for b in range(DB)]
        pa = next_pa()
        for u in range(2):
            S.op("pe", lambda e, pa=pa, u=u: e.transpose(pA[pa][0:8, u * 128:(u + 1) * 128], impA[:, :, :, u].rearrange("p b g -> p (b g)"), ident[:]),
                 r=IMPA + ["ident"], w=[f"pA{pa}"])
        S.op("dve", lambda e, pa=pa: e.tensor_copy(scD[:].rearrange("p a b -> p (a b)"), pA[pa][0:8, 0:256]), r=[f"pA{pa}"], w=["scD"])
        S.op("dve", lambda e: e.tensor_tensor(scD[:].rearrange("p a b -> p (a b)"), scD[:].rearrange("p a b -> p (a b)"), A2[:], ALU.mult), r=["scD", "A2"], w=["scD"])
        S.op("dve", lambda e: e.tensor_tensor(scD[:].rearrange("p a b -> p (a b)"), scD[:].rearrange("p a b -> p (a b)"), B2[:], ALU.add), r=["scD", "B2"], w=["scD"])
        if os.environ.get("KC2", "9") < "2":
            raise _StopC()
        S.op("dve", lambda e: e.max(m8d[:, 0:8], scD[:].rearrange("p a b -> p (a b)")), r=["scD"], w=["m8da"])
        S.op("dve", lambda e: e.match_replace(scD2[:].rearrange("p a b -> p (a b)"), m8d[:, 0:8], scD[:].rearrange("p a b -> p (a b)"), -2.0),
             r=["scD", "m8da"], w=["scD2"])
        S.op("dve", lambda e: e.max(m8d[:, 8:16], scD2[:].rearrange("p a b -> p (a b)")), r=["scD2"], w=["m8db"])
        S.op("dve", lambda e: e.tensor_scalar(mkD[:].rearrange("p a b -> p (a b)"), scD[:].rearrange("p a b -> p (a b)"), m8d[:, 14:15], None, ALU.is_ge),
             r=["scD", "m8db"], w=["mkD"])
        if os.environ.get("KC2", "9") < "3":
            raise _StopC()
        pa = next_pa()
        for u in range(2):
            S.op("pe", lambda e, pa=pa, u=u: e.transpose(pA[pa][:, u * 8:(u + 1) * 8], mkD[:, u, :], ident[0:8, 0:8]),
                 r=["mkD", "ident"], w=[f"pA{pa}"])
        S.op("dve", lambda e, pa=pa: e.tensor_copy(MselD[:].rearrange("p a b -> p (a b)"), pA[pa][:, 0:16]), r=[f"pA{pa}"], w=["MselD"])

        if KC < 3:
            raise _StopC()
        knT = S.sb([P, 128], BF16, "knT")
        Pn = S.sb([P, DB, 8], BF16, "Pn")
        pn32 = S.sb([P, DB, 8], F32, "pn32")
        pi = next_pt()
        S.op("pe", lambda e, pi=pi: e.transpose(pT[pi][:, 0:128], knewb[:], identb[:]), r=["knewb", "identb"], w=[f"pT{pi}"])
        S.op("act", lambda e, pi=pi: e.copy(knT[:], pT[pi][:, 0:128]), r=[f"pT{pi}"], w=["knT"])
        pa = next_pa()
        for b in range(DB):
            S.op("pe", lambda e, pa=pa, b=b: e.matmul(pA[pa][:, b * 8:(b + 1) * 8], knT[:, :], bdq[:, b, :], start=True, stop=True),
                 r=["knT", "bdq"], w=[f"pA{pa}"])
        S.op("act", lambda e, pa=pa: e.activation(pn32[:].rearrange("p a b -> p (a b)"), pA[pa][:, 0:32], AF.Exp), r=[f"pA{pa}"], w=["pn32"])
        S.op("dve", lambda e: e.tensor_tensor(Pn[:], pn32[:], EBn[:], ALU.mult), r=["pn32", "EBn"], w=["Pn"])

        if KC < 4:
            raise _StopC()
        es32 = S.sb([P, 512], F32, "es32")
        Pd = S.sb([P, 128, 8], BF16, "Pd")
        psm = S.sb([P, 8], F32, "psm")
        psmb = S.sb([P, 8], BF16, "psmb")
        for b in range(DB):
            build_KT("sk", b)
            KTALL = [f"KT{r}" for r in range(8)]
            for u in range(2):
                for tt in range(64):
                    t = u * 64 + tt
                    S.op("pe", lambda e, u=u, tt=tt, t=t, b=b: e.matmul(pA[u][:, tt * 8:(tt + 1) * 8], KT[:, t, :], bdq[:, b, :], start=True, stop=True),
                         r=KTALL + ["bdq"], w=[f"pA{u}"])
                S.op("act", lambda e, u=u: e.activation(es32[:], pA[u][:], AF.Exp), r=[f"pA{u}"], w=["es32"])
                S.op("dve", lambda e, u=u: e.tensor_tensor(es32[:].rearrange("p (t h) -> p t h", h=8), es32[:].rearrange("p (t h) -> p t h", h=8),
                                                            EBs[:, u * 64:(u + 1) * 64, :], ALU.mult), r=["es32", "EBs"], w=["es32"])
                S.op("dve", lambda e, u=u, b=b: e.tensor_tensor(
                    Pd[:, u * 64:(u + 1) * 64, :].rearrange("p t (g j) -> p t g j", j=4), es32[:].rearrange("p (t g j) -> p t g j", g=2, j=4),
                    MselD[:, u, b * 2:b * 2 + 2].unsqueeze(1).unsqueeze(3).broadcast_to([P, 64, 2, 4]), ALU.mult),
                    r=["es32", "MselD"], w=[f"Pd{u}"])
            S.op("dve", lambda e: e.tensor_reduce(psm[:], Pd[:].rearrange("p t h -> p h t"), AX.X, ALU.add), r=["Pd0", "Pd1"], w=["psm0", "psm1"])
            S.op("dve", lambda e: e.tensor_copy(psmb[:], psm[:]), r=["psm0", "psm1"], w=["psmb"])
            for r in range(8):
                i = gather("sv", b, r)
                for tt in range(16):
                    t = r * 16 + tt
                    for g in range(2):
                        S.op("pe", lambda e, g=g, t=t, tt=tt, i=i: e.matmul(
                            pR[g][0:4, 0:64], Pd[:, t, g * 4:(g + 1) * 4], Xg[i][:, tt * 128 + g * 64: tt * 128 + (g + 1) * 64],
                            start=(t == 0), stop=False), r=["Pd0", "Pd1", f"Xg{i}"], w=[f"pR{g}"])
            for g in range(2):
                S.op("pe", lambda e, g=g, b=b: e.matmul(pR[g][0:4, 0:64], Pn[:, b, g * 4:(g + 1) * 4], Vn[:, g, :], start=False, stop=True),
                     r=["Pn", "Vn"], w=[f"pR{g}"])
                S.op("pe", lambda e, g=g: e.matmul(pR[g][0:4, 64:65], psmb[:, g * 4:(g + 1) * 4], onesb[:, 0:1], start=True, stop=False),
                     r=["psmb", "onesb"], w=[f"pR{g}"])
                S.op("pe", lambda e, g=g, b=b: e.matmul(pR[g][0:4, 64:65], Pn[:, b, g * 4:(g + 1) * 4], onesb[:, 0:1], start=False, stop=True),
                     r=["Pn", "onesb"], w=[f"pR{g}"])
                S.op("act", lambda e, g=g, b=b: e.copy(obd[:, 1, b, g, :], pR[g][0:4, 0:65]), r=[f"pR{g}"], w=["obd"])

        if KC < 5:
            raise _StopC()
        kw32d = S.sb([P, 4, 128], F32, "kw32d")
        vw32d = S.sb([P, 4, 128], F32, "vw32d")
        vwbd = S.sb([P, 4, 128], BF16, "vwbd")
        KwTd = S.sb([P, 4, 128], BF16, "KwTd")
        pw32 = S.sb([P, 4, 8], F32, "pw32")
        Pw = S.sb([P, 4, 8], BF16, "Pw")
        for b in range(DB):
            S.dma("sp", kw32d[:], dout["s_win_k"][b].rearrange("(p u) f -> p u f", u=4), r=[f"swk{b}"], w=["kw32d"], key="kw32d")
            S.dma("sp", vw32d[:], dout["s_win_v"][b].rearrange("(p u) f -> p u f", u=4), r=[f"swv{b}"], w=["vw32d"], key="vw32d")
            S.op("dve", lambda e: e.tensor_copy(vwbd[:], vw32d[:]), r=["vw32d"], w=["vwbd"])
            pa = next_pa()
            for u in range(4):
                S.op("pe", lambda e, pa=pa, u=u: e.transpose(pA[pa][:, u * 128:(u + 1) * 128], kw32d[:, u, :], ident[:]),
                     r=["kw32d", "ident"], w=[f"pA{pa}"])
            S.op("act", lambda e, pa=pa: e.copy(KwTd[:].rearrange("p a b -> p (a b)"), pA[pa][:]), r=[f"pA{pa}"], w=["KwTd"])
            pa = next_pa()
            for u in range(4):
                S.op("pe", lambda e, pa=pa, u=u, b=b: e.matmul(pA[pa][:, u * 8:(u + 1) * 8], KwTd[:, u, :], bdq[:, b, :],
                                                             start=True, stop=True), r=["KwTd", "bdq"], w=[f"pA{pa}"])
            S.op("act", lambda e, pa=pa: e.activation(pw32[:].rearrange("p a b -> p (a b)"), pA[pa][:, 0:32], AF.Exp), r=[f"pA{pa}"], w=["pw32"])
            S.op("dve", lambda e: e.tensor_tensor(Pw[:], pw32[:], EBw[:], ALU.mult), r=["pw32", "EBw"], w=["Pw"])
            for g in range(2):
                for u in range(4):
                    S.op("pe", lambda e, g=g, u=u: e.matmul(pR[g][0:4, 0:64], Pw[:, u, g * 4:(g + 1) * 4], vwbd[:, u, g * 64:(g + 1) * 64],
                                                            start=(u == 0), stop=(u == 3)), r=["Pw", "vwbd"], w=[f"pR{g}"])
                for u in range(4):
                    S.op("pe", lambda e, g=g, u=u: e.matmul(pR[g][0:4, 64:65], Pw[:, u, g * 4:(g + 1) * 4], onesb[:, 0:1],
                                                            start=(u == 0), stop=(u == 3)), r=["Pw", "onesb"], w=[f"pR{g}"])
                S.op("act", lambda e, g=g, b=b: e.copy(obd[:, 2, b, g, :], pR[g][0:4, 0:65]), r=[f"pR{g}"], w=["obd"])

        if KC < 6:
            raise _StopC()
        S.barrier()
        S.release(mC)
        rrd = S.sb([4, 24], F32, "rrd")
        obn = S.sb([4, 24, 64], F32, "obn")
        ond = S.sb([P, 3, 8, 64], F32, "ond")
        prd = S.sb([P, 3, 8, 64], F32, "prd")
        nsas = S.sb([P, 512], F32, "nsas")
        nsasb = S.sb([P, 512], BF16, "nsasb")
        S.op("dve", lambda e: e.tensor_scalar(rrd[:], obd[:].rearrange("p a b c d -> p (a b c) d")[:, :, 64], 1e-30, None, ALU.max), r=["obd"], w=["rrd"])
        S.op("dve", lambda e: e.reciprocal(rrd[:], rrd[:]), r=["rrd"], w=["rrd"])
        S.op("dve", lambda e: e.tensor_tensor(obn[:], obd[:].rearrange("p a b c d -> p (a b c) d")[:, :, 0:64],
                                               rrd[:].unsqueeze(2).broadcast_to([4, 24, 64]), ALU.mult), r=["obd", "rrd"], w=["obn"])
        S.dma("sp", dsc_h.ap().rearrange("j a b g d -> j (a b g) d"), obn[:], r=["obn"], w=["dsc"], key="dsc")
        S.op("pool", lambda e: e.memset(ond[:], 0.0), w=["ond"])
        for br in range(3):
            for g in range(2):
                S.dma("sp", ond[0:DB, br, g * 4:(g + 1) * 4, :], dsc_h.ap()[:, br, :, g, :].rearrange("j b d -> b j d"),
                      r=["dsc"], w=["ond"], key="ond", allow_slow_non_contiguous=True)
        S.op("dve", lambda e: e.tensor_tensor(prd[:], ond[:], gates[:, NT, :].rearrange("p (a h) -> p a h", h=8).unsqueeze(3).broadcast_to([P, 3, 8, 64]),
                                               ALU.mult), r=["ond", f"gates{NT}"], w=["prd"])
        S.op("dve", lambda e: e.tensor_reduce(nsas[:].rearrange("p (h d) -> p h d", d=64), prd[:].rearrange("p a h d -> p h d a"), AX.X, ALU.add),
             r=["prd"], w=["nsas"])
        S.op("act", lambda e: e.copy(nsasb[:], nsas[:]), r=["nsas"], w=["nsasb"])
        pi = next_pt()
        for c4 in range(4):
            S.op("pe", lambda e, pi=pi, c4=c4: e.transpose(pT[pi][:, c4 * 128:(c4 + 1) * 128], nsasb[:, c4 * 128:(c4 + 1) * 128], identb[:]),
                 r=["nsasb", "identb"], w=[f"pT{pi}"])
        S.op("act", lambda e, pi=pi: e.copy(catT[:, 0:4, SEQ:SEQ + P], pT[pi][:, 0:512].rearrange("p (a b) -> p a b", b=128)),
             r=[f"pT{pi}"], w=[f"catT{NT}"])

    except _StopC:
        pass
    S.barrier()
    S.release(mP1)
    wo = S.sb([P, 8, D], BF16, "wo")
    wpg = S.sb([P, 8, D], BF16, "wpg")
    wpp = S.sb([P, 2, D], BF16, "wpp")
    n2c = S.sb([P, 8], F32, "n2c")
    gnc = S.sb([P, 4], F32, "gnc")
    stg = [S.sb([P, D], F32, f"stg{i}") for i in range(2)]
    dst = S.sb([P, 32], F32, "dst")
    S.dma("sp", n2c[:], din["norm2_w"].rearrange("(k p) -> p k", p=P), w=["n2c"], key="c2", allow_slow_non_contiguous=True)
    S.dma("sp", gnc[:], din["ret_gn_w"].rearrange("(k p) -> p k", p=P), w=["gnc"], key="c2", allow_slow_non_contiguous=True)
    si = [0]

    def stage_load(src_ap, shape_view=None):
        b = si[0] % 2
        si[0] += 1
        dst_ap = stg[b][:] if shape_view is None else shape_view(stg[b])
        S.dma("sp", dst_ap, src_ap, w=[f"stg{b}"], key=f"stg{b}")
        return b

    for kc in range(8):
        b = stage_load(din["w_out"][kc * P:(kc + 1) * P, :])
        eng = "dve" if kc % 2 == 0 else "pool"
        if kc < 4:
            S.op(eng, lambda e, b=b, kc=kc: e.tensor_copy(wo[:, kc, :], stg[b][:]), r=[f"stg{b}"], w=["wo"])
        else:
            S.op(eng, lambda e, b=b, kc=kc: e.tensor_scalar(wo[:, kc, :], stg[b][:], gnc[:, kc - 4:kc - 3], None, ALU.mult),
                 r=[f"stg{b}", "gnc"], w=["wo"])
    for kc in range(8):
        b = stage_load(din["w_ple_gate"][kc * P:(kc + 1) * P, :])
        eng = "dve" if kc % 2 == 0 else "pool"
        S.op(eng, lambda e, b=b, kc=kc: e.tensor_copy(wpg[:, kc, :], stg[b][:]), r=[f"stg{b}"], w=["wpg"])
    for kc in range(2):
        b = stage_load(din["w_ple_proj"][kc * P:(kc + 1) * P, :])
        S.op("dve", lambda e, b=b, kc=kc: e.tensor_copy(wpp[:, kc, :], stg[b][:]), r=[f"stg{b}"], w=["wpp"])

    CW = 640
    actT = S.sb([P, NFF, CW], BF16, "actT")
    h32 = S.sb([P, 5, D], F32, "h32")
    wdb = S.sb([P, NFF, 512], BF16, "wdb")
    wds = [S.sb([P, 2, 512], F32, f"wds{i}") for i in range(2)]
    wgs = [S.sb([P, 8, 128], F32, f"wgs{i}") for i in range(2)]
    wgb = [S.sb([P, 8, 128], BF16, f"wgb{i}") for i in range(4)]
    hb = S.sb([P, D], BF16, "hb")
    hpT = S.sb([P, 8, 128], BF16, "hpT")
    ptl = S.sb([P, 256], F32, "ptl")
    pbl = S.sb([P, 256], BF16, "pbl")
    ppT = S.sb([P, 2, 128], BF16, "ppT")
    tmpa = [S.sb([P, 512], F32, f"tmpa{i}") for i in range(2)]
    wg_dram = din["w_gate"].rearrange("(k p) n -> p k n", p=P)
    wu_dram = din["w_up"].rearrange("(k p) n -> p k n", p=P)
    wd_dram = din["w_down"].rearrange("(f p) n -> p f n", p=P)
    n2b = n2c[:, 0:8].unsqueeze(2).broadcast_to([P, 8, 128])
    wgi = [0]
    tai = [0]

    for c in range(4):
        tiles = [4 * c + i for i in range(4)] + ([NT] if c == 3 else [])
        nti = len(tiles)
        col0 = 4 * c * P
        ncols = nti * P
        for ti, t in enumerate(tiles):
            tc_ = slice(t * P, (t + 1) * P)
            if t == NT:
                b = si[0] % 2
                si[0] += 1
                S.op("pool", lambda e, b=b: e.memset(stg[b][:], 0.0), w=[f"stg{b}"])
                S.dma("sp", stg[b][0:DB, :], din["x_sample"], w=[f"stg{b}"], key=f"stg{b}")
            else:
                b = stage_load(din["x_prompt"][tc_, :])
            for g2 in range(2):
                pa = next_pa()
                for kc in range(8):
                    S.op("pe", lambda e, pa=pa, kc=kc, g2=g2, tc_=tc_: e.matmul(
                        pA[pa][:], catT[:, kc, tc_], wo[:, kc, g2 * 512:(g2 + 1) * 512], start=(kc == 0), stop=(kc == 7)),
                        r=[f"catT{t}", "wo"], w=[f"pA{pa}"])
                S.op("dve", lambda e, pa=pa, g2=g2, ti=ti, b=b: e.tensor_tensor(
                    h32[:, ti, g2 * 512:(g2 + 1) * 512], pA[pa][:], stg[b][:, g2 * 512:(g2 + 1) * 512], ALU.add),
                    r=[f"pA{pa}", f"stg{b}"], w=[f"h32_{ti}"])
            S.op("act", lambda e, ti=ti: e.activation(junk[:], h32[:, ti, :], AF.Square, scale=1.0 / 32.0, accum_out=dst[:, 0:1]),
                 r=[f"h32_{ti}"], w=["junk", "dst0"])
            S.op("act", lambda e: e.activation(dst[:, 0:1], dst[:, 0:1], AF.Ln, bias=epsc[:, 0:1]), r=["dst0", "epsc"], w=["dst0"])
            S.op("act", lambda e: e.activation(dst[:, 0:1], dst[:, 0:1], AF.Exp, scale=-0.5), r=["dst0"], w=["dst0"])
            S.op("act", lambda e, ti=ti: e.activation(hb[:], h32[:, ti, :], AF.Copy, scale=dst[:, 0:1]),
                 r=[f"h32_{ti}", "dst0"], w=["hb"])
            for hf in range(2):
                pi = next_pt()
                for k4 in range(4):
                    kc = hf * 4 + k4
                    S.op("pe", lambda e, pi=pi, k4=k4, kc=kc: e.transpose(
                        pT[pi][:, k4 * 128:(k4 + 1) * 128], hb[:, kc * 128:(kc + 1) * 128], identb[:]),
                        r=["hb", "identb"], w=[f"pT{pi}"])
                S.op("dve", lambda e, pi=pi, hf=hf, tc_=tc_: e.tensor_copy(
                    catT[:, hf * 4:(hf + 1) * 4, tc_], pT[pi][:, 0:512].rearrange("p (a b) -> p a b", b=128)),
                    r=[f"pT{pi}"], w=[f"catT{t}"])
        HN = [f"catT{t}" for t in tiles]
        subs = [(0, min(512, ncols))] + ([(512, ncols)] if ncols > 512 else [])
        def load_w(f):
            wl = []
            for wi, wdram in enumerate((wg_dram, wu_dram)):
                sb_ = wgi[0] % 2
                bb_ = wgi[0] % 4
                wgi[0] += 1
                S.dma("sp", wgs[sb_][:], wdram[:, :, f * P:(f + 1) * P], w=[f"wgs{sb_}"], key=f"wgs{sb_}")
                S.op("pool" if wi == 0 else "dve", lambda e, sb_=sb_, bb_=bb_: e.tensor_tensor(wgb[bb_][:], wgs[sb_][:], n2b, ALU.mult),
                     r=[f"wgs{sb_}", "n2c"], w=[f"wgb{bb_}"])
                wl.append(bb_)
            return wl

        wb_next = load_w(0)
        for f in range(NFF):
            wb = wb_next
            if f + 1 < NFF:
                wb_next = load_w(f + 1)
            for (lo, hi) in subs:
                n = hi - lo
                pg = next_pa()
                for kc in range(8):
                    S.op("pe", lambda e, pg=pg, kc=kc, lo=lo, hi=hi, n=n: e.matmul(
                        pA[pg][:, 0:n], wgb[wb[0]][:, kc, :], catT[:, kc, col0 + lo:col0 + hi], start=(kc == 0), stop=(kc == 7)),
                        r=HN + [f"wgb{wb[0]}"], w=[f"pA{pg}"])
                pu = next_pa()
                for kc in range(8):
                    S.op("pe", lambda e, pu=pu, kc=kc, lo=lo, hi=hi, n=n: e.matmul(
                        pA[pu][:, 0:n], wgb[wb[1]][:, kc, :], catT[:, kc, col0 + lo:col0 + hi], start=(kc == 0), stop=(kc == 7)),
                        r=HN + [f"wgb{wb[1]}"], w=[f"pA{pu}"])
                ta = tai[0] % 2
                tai[0] += 1
                S.op("act", lambda e, ta=ta, pg=pg, n=n: e.activation(tmpa[ta][:, 0:n], pA[pg][:, 0:n], AF.Exp, scale=-1.0),
                     r=[f"pA{pg}"], w=[f"tmpa{ta}"])
                S.op("dve", lambda e, ta=ta, n=n: e.tensor_scalar(tmpa[ta][:, 0:n], tmpa[ta][:, 0:n], 1.0, None, ALU.add),
                     r=[f"tmpa{ta}"], w=[f"tmpa{ta}"])
                S.op("dve", lambda e, ta=ta, n=n: e.reciprocal(tmpa[ta][:, 0:n], tmpa[ta][:, 0:n]), r=[f"tmpa{ta}"], w=[f"tmpa{ta}"])
                S.op("dve", lambda e, ta=ta, pg=pg, n=n: e.tensor_tensor(tmpa[ta][:, 0:n], tmpa[ta][:, 0:n], pA[pg][:, 0:n], ALU.mult),
                     r=[f"tmpa{ta}", f"pA{pg}"], w=[f"tmpa{ta}"])
                S.op("dve", lambda e, ta=ta, pu=pu, n=n, lo=lo, hi=hi, f=f: e.tensor_tensor(
                    actT[:, f, lo:hi], tmpa[ta][:, 0:n], pA[pu][:, 0:n], ALU.mult),
                    r=[f"tmpa{ta}", f"pA{pu}"], w=[f"actT{f}"])
        ACTT = [f"actT{f}" for f in range(NFF)]
        for cg in range(2):
            for fp in range(NFF // 2):
                b = fp % 2
                S.dma("sp", wds[b][:], wd_dram[:, 2 * fp:2 * fp + 2, cg * 512:(cg + 1) * 512], w=[f"wds{b}"], key=f"wds{b}")
                eng = "pool" if fp % 2 == 0 else "dve"
                S.op(eng, lambda e, b=b, fp=fp: e.tensor_copy(wdb[:, 2 * fp:2 * fp + 2, :], wds[b][:]),
                     r=[f"wds{b}"], w=["wdb"])
            for ti, t in enumerate(tiles):
                pa = next_pa()
                for f in range(NFF):
                    S.op("pe", lambda e, pa=pa, f=f, ti=ti: e.matmul(
                        pA[pa][:], actT[:, f, ti * P:(ti + 1) * P], wdb[:, f, :], start=(f == 0), stop=(f == NFF - 1)),
                        r=ACTT + ["wdb"], w=[f"pA{pa}"])
                S.op("dve", lambda e, pa=pa, ti=ti, cg=cg: e.tensor_tensor(
                    h32[:, ti, cg * 512:(cg + 1) * 512], h32[:, ti, cg * 512:(cg + 1) * 512], pA[pa][:], ALU.add),
                    r=[f"pA{pa}", f"h32_{ti}"], w=[f"h32_{ti}"])
        for ti, t in enumerate(tiles):
            S.op("act", lambda e, ti=ti: e.copy(hb[:], h32[:, ti, :]), r=[f"h32_{ti}"], w=["hb"])
            for hf in range(2):
                pi = next_pt()
                for k4 in range(4):
                    kc = hf * 4 + k4
                    S.op("pe", lambda e, pi=pi, k4=k4, kc=kc: e.transpose(
                        pT[pi][:, k4 * 128:(k4 + 1) * 128], hb[:, kc * 128:(kc + 1) * 128], identb[:]),
                        r=["hb", "identb"], w=[f"pT{pi}"])
                S.op("act", lambda e, pi=pi, hf=hf: e.copy(
                    hpT[:, hf * 4:(hf + 1) * 4, :].rearrange("p a b -> p (a b)"), pT[pi][:, 0:512]),
                    r=[f"pT{pi}"], w=[f"hpT{hf}"])
            if t == NT:
                S.op("pool", lambda e: e.memset(ptl[:], 0.0), w=["ptl"])
                S.dma("sp", ptl[0:DB, :], din["p_sample"], w=["ptl"], key="ptl")
            else:
                S.dma("sp", ptl[:], din["p_prompt"][t * P:(t + 1) * P, :], w=["ptl"], key="ptl")
            S.op("pool", lambda e: e.tensor_copy(pbl[:], ptl[:]), r=["ptl"], w=["pbl"])
            pi = next_pt()
            for k2 in range(2):
                S.op("pe", lambda e, pi=pi, k2=k2: e.transpose(pT[pi][:, k2 * 128:(k2 + 1) * 128], pbl[:, k2 * 128:(k2 + 1) * 128], identb[:]),
                     r=["pbl", "identb"], w=[f"pT{pi}"])
            S.op("act", lambda e, pi=pi: e.copy(ppT[:].rearrange("p a b -> p (a b)"), pT[pi][:, 0:256]), r=[f"pT{pi}"], w=["ppT"])
            for g2 in range(2):
                cs_ = slice(g2 * 512, (g2 + 1) * 512)
                pgt = next_pa()
                for kc in range(8):
                    S.op("pe", lambda e, pgt=pgt, kc=kc, cs_=cs_: e.matmul(pA[pgt][:], hpT[:, kc, :], wpg[:, kc, cs_], start=(kc == 0), stop=(kc == 7)),
                         r=["hpT0", "hpT1", "wpg"], w=[f"pA{pgt}"])
                ppp = next_pa()
                for kc in range(2):
                    S.op("pe", lambda e, ppp=ppp, kc=kc, cs_=cs_: e.matmul(pA[ppp][:], ppT[:, kc, :], wpp[:, kc, cs_], start=(kc == 0), stop=(kc == 1)),
                         r=["ppT", "wpp"], w=[f"pA{ppp}"])
                ta = tai[0] % 2
                tai[0] += 1
                S.op("act", lambda e, ta=ta, pgt=pgt: e.activation(tmpa[ta][:], pA[pgt][:], AF.Exp, scale=-1.0), r=[f"pA{pgt}"], w=[f"tmpa{ta}"])
                S.op("pool", lambda e, ta=ta: e.tensor_scalar(tmpa[ta][:], tmpa[ta][:], 1.0, None, ALU.add), r=[f"tmpa{ta}"], w=[f"tmpa{ta}"])
                S.op("dve", lambda e, ta=ta: e.reciprocal(tmpa[ta][:], tmpa[ta][:]), r=[f"tmpa{ta}"], w=[f"tmpa{ta}"])
                S.op("dve", lambda e, ta=ta, ppp=ppp: e.tensor_tensor(tmpa[ta][:], tmpa[ta][:], pA[ppp][:], ALU.mult),
                     r=[f"tmpa{ta}", f"pA{ppp}"], w=[f"tmpa{ta}"])
                S.op("dve", lambda e, ta=ta, ti=ti, cs_=cs_: e.tensor_tensor(h32[:, ti, cs_], h32[:, ti, cs_], tmpa[ta][:], ALU.add),
                     r=[f"tmpa{ta}", f"h32_{ti}"], w=[f"h32_{ti}"])
            if t == NT:
                S.dma("pool", dout["y_sample"], h32[0:DB, ti, :], r=[f"h32_{ti}"], key="oy")
            else:
                S.dma("pool", dout["y_prompt"][t * P:(t + 1) * P, :], h32[:, ti, :], r=[f"h32_{ti}"], key="oy")

    print("SBUF peak", S.sb_peak, "of", SB_HI)
    S.emit()
    return nc


_CACHE = {}


def kernel(**inputs):
    if "nc" not in _CACHE:
        _CACHE["nc"] = build_program()
    nc = _CACHE["nc"]
    consts = _consts()
    f = lambda a: np.ascontiguousarray(np.asarray(a))
    in_maps = []
    for c in range(NCORES):
        bs = slice(c * DB, (c + 1) * DB)
        m = {
            "x_prompt": f(inputs["x_prompt"][c]),
            "x_sample": f(inputs["x_sample"][bs, 0]),
            "state_ret": f(inputs["state_ret"][0, bs]),
            "cache_win_k": f(inputs["cache_win_k"][0, bs].reshape(DB, 512, 128)),
            "cache_win_v": f(inputs["cache_win_v"][0, bs].reshape(DB, 512, 128)),
            "norm1_w": f(inputs["norm1_w"][0]),
            "w_in": f(inputs["w_in"][0]),
            "q_norm_w": f(inputs["q_norm_w"][0]),
            "k_norm_w": f(inputs["k_norm_w"][0]),
            "p_prompt": f(inputs["p_prompt"][0, c]), "p_sample": f(inputs["p_sample"][0, bs, 0]),
            "rel_bias": f(inputs["rel_bias"]),
        }
        if ENABLE_DECODE_NSA:
            m["page_table"] = f(inputs["page_table"][bs]).astype(np.int32)
            for k in ("cache_cmp_k", "cache_cmp_v", "cache_slc_k", "cache_slc_v"):
                m[k] = f(inputs[k]).reshape(NPOOL * 8, 2048)
        for k in ("cmp_pe_k", "cmp_w1_k", "cmp_w2_k", "cmp_pe_v", "cmp_w1_v", "cmp_w2_v", "ret_gn_w", "w_out",
                  "norm2_w", "w_gate", "w_up", "w_down", "w_ple_gate", "w_ple_proj"):
            m[k] = f(inputs[k][0])
        for k, v in consts.items():
            m["c_" + k] = v
        in_maps.append(m)
    res = run_bass_kernel_spmd(nc, in_maps, core_ids=list(range(NCORES)))
    R = res.results
    cat = lambda k: np.stack([np.asarray(R[c][k]) for c in range(NCORES)])
    out = {}
    out["y_prompt"] = cat("y_prompt").reshape(8, SEQ, D)
    out["y_sample"] = cat("y_sample").reshape(32, 1, D)
    for k in ("p_cmp_k", "p_cmp_v", "p_slc_k", "p_slc_v"):
        out[k] = cat(k).reshape(1, 8, SEQ, 2, 64)
    for k in ("p_win_k", "p_win_v"):
        out[k] = cat(k).reshape(1, 8, 512, 2, 64)
    out["p_ret"] = cat("p_ret").reshape(1, 8, 4, 128, 128)
    for k in ("s_cmp_k", "s_cmp_v", "s_slc_k", "s_slc_v"):
        out[k] = cat(k).reshape(1, 32, 1, 2, 64)
    for k in ("s_win_k", "s_win_v"):
        out[k] = cat(k).reshape(1, 32, 512, 2, 64)
    out["s_ret"] = cat("s_ret").reshape(1, 32, 4, 128, 128)
    return tuple(np.ascontiguousarray(out[k], dtype=np.float32) for k in OUT_ORDER)
```
